# Optimizing a Trainium2 kernel written in Bass

```python
import jax, jax.numpy as jnp
from jax import lax
import numpy as np

D_MODEL = 1024
BATCH = 2
SEQ = 8192
DEPTH = 1

POOL_WIDTH = D_MODEL // 2
POOL_GROUPS = 4
POOL_GROUP_DIM = POOL_WIDTH // POOL_GROUPS
POOL_WINDOWS = (2, 4, 8, 16)
N_HEADS = 8
HEAD_DIM = 64
ATTN_WIDTH = N_HEADS * HEAD_DIM
ROT_DIM = HEAD_DIM // 4
ROPE_THETA = 500000.0
IDX_HEADS = 8
IDX_DIM = 64
TOPK_MAX = 256
Q_BLOCK = 128
N_BRANCH = 2
D_FF = 2816
CONV_WIDTH = 3
EPS = 1e-6
IN_SPLITS = (POOL_WIDTH, ATTN_WIDTH, ATTN_WIDTH, ATTN_WIDTH,
             IDX_HEADS * IDX_DIM, IDX_DIM, IDX_HEADS, N_BRANCH * D_MODEL)
D_IN = POOL_WIDTH + 3 * ATTN_WIDTH + IDX_HEADS * IDX_DIM + IDX_DIM + IDX_HEADS + N_BRANCH * D_MODEL

kernel_name = "hybrid_pool_dsa_gated_convffn"


def rmsnorm(x, g):
    xf = x.astype(jnp.float32)
    y = xf * lax.rsqrt(jnp.mean(xf * xf, axis=-1, keepdims=True) + EPS)
    return (y * g.astype(jnp.float32)).astype(x.dtype)


def rope_partial(x, pos):
    half = ROT_DIM // 2
    inv_freq = 1.0 / (ROPE_THETA ** (jnp.arange(half, dtype=jnp.float32) * 2.0 / ROT_DIM))
    ang = pos[:, None] * inv_freq[None, :]
    cos = jnp.cos(ang)[None, :, None, :]
    sin = jnp.sin(ang)[None, :, None, :]
    xr = x[..., :ROT_DIM].astype(jnp.float32)
    x1, x2 = xr[..., :half], xr[..., half:]
    rot = jnp.concatenate([x1 * cos - x2 * sin, x2 * cos + x1 * sin], axis=-1)
    return jnp.concatenate([rot.astype(x.dtype), x[..., ROT_DIM:]], axis=-1)


def multiscale_pool(u):
    B, S, _ = u.shape
    ug = u.astype(jnp.float32).reshape(B, S, POOL_GROUPS, POOL_GROUP_DIM)
    c0 = jnp.concatenate([jnp.zeros((B, 1, POOL_GROUPS, POOL_GROUP_DIM), jnp.float32),
                          jnp.cumsum(ug, axis=1)], axis=1)
    t = jnp.arange(S)
    outs = []
    for g, w in enumerate(POOL_WINDOWS):
        lo = jnp.maximum(t + 1 - w, 0)
        cg = c0[:, :, g]
        sums = cg[:, 1:] - cg[:, lo]
        cnt = jnp.minimum(t + 1, w).astype(jnp.float32)[None, :, None]
        outs.append(sums / cnt - ug[:, :, g])
    return jnp.stack(outs, axis=2).astype(u.dtype)


def indexed_sparse_attention(q, k, v, iq, ik, iw):
    B, S, H, Dh = q.shape
    k_top = min(TOPK_MAX, S // 4)
    nblk = S // Q_BLOCK
    kpos = jnp.arange(S)

    def to_blocks(a):
        return a.reshape((B, nblk, Q_BLOCK) + a.shape[2:]).swapaxes(0, 1)

    t_blocks = kpos.reshape(nblk, Q_BLOCK)

    def block_fn(args):
        qb, iqb, iwb, tb = args
        s = jnp.einsum('bqhd,bkd->bhqk', iqb, ik).astype(jnp.float32) * (IDX_DIM ** -0.5)
        w = iwb.astype(jnp.float32) * (IDX_HEADS ** -0.5)
        score = jnp.einsum('bqh,bhqk->bqk', w, jax.nn.relu(s))
        causal = kpos[None, :] <= tb[:, None]
        score = jnp.where(causal[None], score, -jnp.inf)
        _, idx = lax.top_k(score, k_top)
        valid = idx <= tb[None, :, None]
        kg = jax.vmap(lambda kb, ib: kb[ib])(k, idx)
        vg = jax.vmap(lambda vb, ib: vb[ib])(v, idx)
        sc = jnp.einsum('bqhd,bqkhd->bhqk', qb, kg).astype(jnp.float32) * (Dh ** -0.5)
        sc = jnp.where(valid[:, None], sc, -jnp.inf)
        p = jax.nn.softmax(sc, axis=-1)
        return jnp.einsum('bhqk,bqkhd->bqhd', p.astype(vg.dtype), vg)

    out = lax.map(block_fn, (to_blocks(q), to_blocks(iq), to_blocks(iw), t_blocks))
    return out.swapaxes(0, 1).reshape(B, S, H * Dh)


def hybrid_mixer(xn, w_in, pool_w, pool_scale, w_pool_proj, w_attn_proj, w_out):
    B, S, _ = xn.shape
    proj = xn @ w_in
    cuts = [int(c) for c in np.cumsum(IN_SPLITS)[:-1]]
    u_pool, q, k, v, iq, ik, iw, gates = jnp.split(proj, cuts, axis=-1)
    pos = jnp.arange(S, dtype=jnp.float32)

    pooled = multiscale_pool(u_pool)
    mixed = jnp.einsum('bsgc,gcd->bsgd', pooled, pool_w).reshape(B, S, POOL_WIDTH) * pool_scale
    y_pool = mixed @ w_pool_proj

    q = rope_partial(q.reshape(B, S, N_HEADS, HEAD_DIM), pos)
    k = rope_partial(k.reshape(B, S, N_HEADS, HEAD_DIM), pos)
    v = v.reshape(B, S, N_HEADS, HEAD_DIM)
    iq = rope_partial(iq.reshape(B, S, IDX_HEADS, IDX_DIM), pos)
    ik = rope_partial(ik.reshape(B, S, 1, IDX_DIM), pos)[:, :, 0]
    o = indexed_sparse_attention(q, k, v, iq, ik, iw)
    y_attn = o @ w_attn_proj

    g = jax.nn.sigmoid(gates.astype(jnp.float32)).reshape(B, S, N_BRANCH, D_MODEL)
    merged = (g[:, :, 0] * y_pool.astype(jnp.float32) + g[:, :, 1] * y_attn.astype(jnp.float32)).astype(xn.dtype)
    return merged @ w_out


def conv_ffn(xn, w_up, conv_w, conv_b, w_down):
    h = xn @ w_up
    C = h.shape[-1]
    h = lax.conv_general_dilated(h, conv_w[:, None, :].astype(h.dtype), window_strides=(1,),
                                 padding=[(CONV_WIDTH - 1, 0)],
                                 dimension_numbers=('NWC', 'WIO', 'NWC'),
                                 feature_group_count=C) + conv_b
    a, b = jnp.split(h, 2, axis=-1)
    return (jax.nn.silu(a) * b) @ w_down


def setup_inputs(seed: int = 0) -> dict:
    key = jax.random.key(seed)
    ks = jax.random.split(key, 16)
    f32 = jnp.float32
    nrm = lambda k, shape, fan: jax.random.normal(k, shape, f32) * (fan ** -0.5)
    L = DEPTH
    return {
        "x": jax.random.normal(ks[0], (BATCH, SEQ, D_MODEL), f32),
        "norm_mix_g": 1.0 + 0.05 * jax.random.normal(ks[1], (L, D_MODEL), f32),
        "w_in": nrm(ks[2], (L, D_MODEL, D_IN), D_MODEL),
        "pool_w": nrm(ks[3], (L, POOL_GROUPS, POOL_GROUP_DIM, POOL_GROUP_DIM), POOL_GROUP_DIM),
        "pool_scale": 1.0 + 0.1 * jax.random.normal(ks[4], (L, POOL_WIDTH), f32),
        "w_pool_proj": nrm(ks[5], (L, POOL_WIDTH, D_MODEL), POOL_WIDTH),
        "w_attn_proj": nrm(ks[6], (L, ATTN_WIDTH, D_MODEL), ATTN_WIDTH),
        "w_out": nrm(ks[7], (L, D_MODEL, D_MODEL), D_MODEL),
        "norm_ffn_g": 1.0 + 0.05 * jax.random.normal(ks[8], (L, D_MODEL), f32),
        "w_up": nrm(ks[9], (L, D_MODEL, 2 * D_FF), D_MODEL),
        "conv_w": nrm(ks[10], (L, CONV_WIDTH, 2 * D_FF), CONV_WIDTH),
        "conv_b": 0.02 * jax.random.normal(ks[11], (L, 2 * D_FF), f32),
        "w_down": nrm(ks[12], (L, D_FF, D_MODEL), D_FF),
        "norm_final_g": 1.0 + 0.05 * jax.random.normal(ks[13], (D_MODEL,), f32),
    }


def reference(x, norm_mix_g, w_in, pool_w, pool_scale, w_pool_proj, w_attn_proj, w_out,
              norm_ffn_g, w_up, conv_w, conv_b, w_down, norm_final_g):
    h = x
    for l in range(DEPTH):
        h = h + hybrid_mixer(rmsnorm(h, norm_mix_g[l]), w_in[l], pool_w[l], pool_scale[l],
                             w_pool_proj[l], w_attn_proj[l], w_out[l])
        h = h + conv_ffn(rmsnorm(h, norm_ffn_g[l]), w_up[l], conv_w[l], conv_b[l], w_down[l])
    return rmsnorm(h, norm_final_g)
```

```python
import numpy as np
from contextlib import ExitStack
import ml_dtypes
import concourse.bass as bass
import concourse.mybir as mybir
from concourse.bass_utils import run_bass_kernel_spmd

F32 = mybir.dt.float32
BF16 = mybir.dt.bfloat16
U8 = mybir.dt.uint8
ALU = mybir.AluOpType
AF = mybir.ActivationFunctionType
AX = mybir.AxisListType

D = 1024
S = 8192
NT = 64
NR = 4
RT = 640
LR = [4, 8, 12, 16]
NBIS = 16
DFF = 2816
NF = 22
EPS = 1e-6
CS_IDX = (64 ** -0.5) * (8 ** -0.5)
NEG = -1.0e30


class Prog:
    ENGS = ("pe", "act", "dve", "pool", "sp")

    def __init__(self, nc):
        self.nc = nc
        self.ops = []

    def op(self, eng, fn, r=(), w=(), slot=None, barrier=False):
        self.ops.append(dict(eng=eng, fn=fn, r=tuple(r), w=tuple(w), slot=slot, barrier=barrier))

    def dma(self, eng, out, in_, r=(), w=(), slot=None, **kw):
        assert slot is not None
        self.op(eng, lambda e: e.dma_start(out=out, in_=in_, **kw), r, w, slot)

    def emit(self, stack):
        nc = self.nc
        ops = self.ops
        last_w = {}
        readers = {}
        deps = [None] * len(ops)
        needs_inc = [False] * len(ops)
        last_eng = {}
        last_slot = {}
        last_bar = None
        for i, o in enumerate(ops):
            d = set()
            if o["barrier"]:
                d.update(last_eng.values())
                d.update(last_slot.values())
            else:
                for k in o["r"]:
                    if k in last_w:
                        d.add(last_w[k])
                for k in o["w"]:
                    if k in last_w:
                        d.add(last_w[k])
                    for rr in readers.get(k, ()):
                        d.add(rr)
            if last_bar is not None:
                d.add(last_bar)
            d.discard(i)
            d = {p for p in d if not (ops[p]["slot"] is None and ops[p]["eng"] == "pe" and o["eng"] == "pe"
                                      and not ops[p]["barrier"])}
            deps[i] = d
            for p in d:
                needs_inc[p] = True
            for k in o["r"]:
                readers.setdefault(k, []).append(i)
            for k in o["w"]:
                last_w[k] = i
                readers[k] = []
            if o["slot"] is not None:
                last_slot[o["slot"]] = i
            elif o["fn"] is not None:
                last_eng[o["eng"]] = i
            if o["barrier"]:
                last_bar = i
        self._deps = deps
        if CHECK_DEADLOCK:
            done = [False] * len(ops)
            pe_ = {e: [i for i, o in enumerate(ops) if o["eng"] == e] for e in self.ENGS}
            ptr = {e: 0 for e in self.ENGS}
            progress = True
            while progress:
                progress = False
                for e in self.ENGS:
                    while ptr[e] < len(pe_[e]):
                        i = pe_[e][ptr[e]]
                        if all(done[p] for p in deps[i]):
                            done[i] = True
                            ptr[e] += 1
                            progress = True
                        else:
                            break
            stuck = {e: pe_[e][ptr[e]] for e in self.ENGS if ptr[e] < len(pe_[e])}
            print("DEADLOCK CHECK: stuck =", stuck)
            for e, i in stuck.items():
                print(e, i, ops[i]["r"], ops[i]["w"], [(p, ops[p]["eng"], done[p]) for p in deps[i] if not done[p]])
        sems = {}

        def get_sem(name):
            if name not in sems:
                sems[name] = stack.enter_context(nc.semaphore(name))
            return sems[name]

        cnt = {}
        val = [None] * len(ops)
        for i, o in enumerate(ops):
            if o["slot"] is not None:
                key = "d_" + o["slot"]
                cnt[key] = cnt.get(key, 0) + 16
                val[i] = (key, cnt[key])
            elif needs_inc[i]:
                key = "e_" + o["eng"]
                cnt[key] = cnt.get(key, 0) + 1
                val[i] = (key, cnt[key])
        per_eng = {e: [] for e in self.ENGS}
        for i, o in enumerate(ops):
            per_eng[o["eng"]].append(i)
        block = stack.enter_context(nc.Block())

        def make(engname):
            idxs = per_eng[engname]

            def body(eng):
                waited = {}
                for i in idxs:
                    o = ops[i]
                    need = {}
                    for p in deps[i]:
                        k, v = val[p]
                        if v > need.get(k, 0):
                            need[k] = v
                    for k, v in need.items():
                        if waited.get(k, 0) >= v:
                            continue
                        eng.wait_ge(get_sem(k), v)
                        waited[k] = v
                    if o["fn"] is None:
                        continue
                    ins = o["fn"](eng)
                    if val[i] is not None:
                        k, v = val[i]
                        ins.then_inc(get_sem(k), 16 if o["slot"] is not None else 1)
            return body

        if per_eng["sp"]:
            block.sync(make("sp"))
        if per_eng["act"]:
            block.scalar(make("act"))
        if per_eng["dve"]:
            block.vector(make("dve"))
        if per_eng["pool"]:
            block.gpsimd(make("pool"))
        if per_eng["pe"]:
            block.tensor(make("pe"))


class Arena:
    def __init__(self, base, cap):
        self.base = base
        self.cap = cap
        self.off = 0
        self.marks = []

    def alloc(self, shape, dt):
        esz = 2 if dt == BF16 else 4
        n = int(np.prod(shape[1:])) * esz
        n_al = (n + 63) // 64 * 64
        assert self.off + n_al <= self.cap, f"arena overflow {self.off}+{n_al}>{self.cap}"
        ap = self.base[0:shape[0], self.off:self.off + n].bitcast(dt)
        if len(shape) == 3:
            ap = ap.rearrange("p (a b) -> p a b", a=shape[1])
        self.off += n_al
        return ap

    def mark(self):
        self.marks.append(self.off)

    def release(self):
        self.off = self.marks.pop()


DEBUG = False
PIPE_IDX = True
SKEW1 = True
SKEW2 = True
CHECK_DEADLOCK = False
MERGE_ATT = True
SPLIT_PAIR = False


def build_program():
    nc = bass.Bass("TRN2", target_bir_lowering=False)
    skind = "ExternalOutput" if DEBUG else "Internal"

    def din(name, shape, dt=F32):
        return nc.dram_tensor(name, list(shape), dt, kind="ExternalInput").ap()

    xc = din("xc", [S, D])
    xo = din("xo", [NR, RT, D])
    cs_ctx = din("cs_ctx", [S, 16])
    cs_own = din("cs_own", [NR, RT, 16])
    qpos_d = din("qpos", [128, NR * 5])
    hsc_d = din("hscale", [128, NR])
    iota_d = din("iota", [128, 512])
    pow2_d = din("pow2", [128, NBIS])
    ident_d = din("ident", [128, 128])
    g_d = din("gains", [128, 3 * D])
    w_kvi_d = din("w_kvi", [128, 8 * 1152])
    w_qi_d = din("w_qi", [128, 8 * 1032])
    w_u_d = din("w_u", [128, 8 * 512])
    w_g_d = din("w_g", [128, 8 * 2048])
    poolw_d = din("pool_w", [128, 4 * 128])
    pscale_d = din("pool_scale", [128, 4])
    w_pp_d = din("w_pp", [128, 4 * 1024])
    w_ap_d = din("w_ap", [64, 8 * 1024])
    w_out_d = din("w_out", [128, 8 * 1024])
    w_up_d = din("w_up", [128, 8 * 2 * DFF])
    w_dn_d = din("w_down", [128, NF * 1024])
    convw_d = din("conv_w", [128, 44 * 3])
    convb_d = din("conv_b", [128, 44])
    pm_std_d = din("pm_std", [128, 8 * 128])
    pm_first_d = din("pm_first", [128, NR * 12 * 128])
    out_d = nc.dram_tensor("out", [NR, 512, D], F32, kind="ExternalOutput").ap()
    kt_dram = nc.dram_tensor("kt_scr", [NT, 128, 512], BF16, kind=skind).ap()
    v_dram = nc.dram_tensor("v_scr", [NT, 128, 520], BF16, kind=skind).ap()
    ot_dram = nc.dram_tensor("ot_scr", [NR, 64, 8 * RT], BF16, kind=skind).ap()
    h_dram = nc.dram_tensor("h_scr", [NR, RT, D], F32, kind=skind).ap()

    P = Prog(nc)
    st = ExitStack()
    CAP = 207 * 1024 + 768
    arena_t = st.enter_context(nc.sbuf_tensor("arena", [128, CAP], U8))
    A = Arena(arena_t, CAP)
    psbig = st.enter_context(nc.psum_tensor("psbig", [128, 4096], F32))
    psum = [psbig[:, i * 512:(i + 1) * 512] for i in range(8)]
    psb = [p.bitcast(BF16) for p in psum]

    ident_f = A.alloc([128, 128], F32)
    ident_b = A.alloc([128, 128], BF16)
    ident4 = A.alloc([128, 512], BF16)
    gmix = A.alloc([128, D], F32)
    qpos = A.alloc([128, NR * 5], F32)
    hsc = A.alloc([128, NR], F32)
    iota = A.alloc([128, 512], F32)
    pow2 = A.alloc([128, NBIS], F32)
    ones_b = A.alloc([128, 64], BF16)
    hl = A.alloc([128, 1024], BF16)
    cneg = A.alloc([128, 2], F32)
    P.dma("sp", ident_f, ident_d, w=["ident_f", "cchain"], slot="c")
    P.dma("sp", gmix, g_d[:, 0:D], w=["gains", "cchain"], slot="c")
    P.dma("sp", qpos, qpos_d, w=["qpos", "cchain"], slot="c")
    P.dma("sp", hsc, hsc_d, w=["hsc", "cchain"], slot="c")
    P.dma("sp", iota, iota_d, w=["iota", "cchain"], slot="c")
    P.dma("sp", pow2, pow2_d, w=["pow2", "cchain"], slot="c")
    P.op("dve", lambda e: e.tensor_copy(out=ident_b, in_=ident_f), r=["ident_f"], w=["ident_b"])
    P.op("pool", lambda e: e.tensor_copy(out=ident4.rearrange("p (a b) -> p a b", a=4),
                                         in_=ident_f[:, None, :].to_broadcast([128, 4, 128])),
         r=["ident_f"], w=["ident4"])
    P.op("pool", lambda e: e.memset(ones_b, 1.0), w=["ones_b"])
    P.op("pool", lambda e: e.memset(cneg[:, 0:1], -0.5), w=["cneg"])
    A.mark()

    wst_i = [0]

    def load_weight(dram2d, dst2d, ncols, nparts, wst, key):
        c0 = 0
        while c0 < ncols:
            n = min(2048, ncols - c0)
            s = wst_i[0] % 2
            wst_i[0] += 1
            stg = wst[s]
            P.dma("sp", stg[0:nparts, 0:n], dram2d[0:nparts, c0:c0 + n], w=[f"wst{s}"], slot=f"wst{s}")
            eng = "pool" if s == 0 else "dve"
            P.op(eng, (lambda e, a=dst2d[0:nparts, c0:c0 + n], b=stg[0:nparts, 0:n]: e.tensor_copy(out=a, in_=b)),
                 r=[f"wst{s}"], w=[key])
            c0 += n

    def barrier():
        dummy = cneg[:, 1:2]
        P.op("pool", lambda e: e.memset(dummy, 0.0), barrier=True)

    def norm_tile(x_ap, xkey, gain, out_bf, okey, tmp, tkey, junk, jkey):
        P.op("act", lambda e: e.activation(out=junk, in_=x_ap, func=AF.Square, accum_out=tmp[:, 0:1]),
             r=[xkey], w=[jkey, tkey + "0"])
        P.op("dve", lambda e: e.tensor_scalar(out=tmp[:, 1:2], in0=tmp[:, 0:1], scalar1=1.0 / D, scalar2=EPS,
                                              op0=ALU.mult, op1=ALU.add), r=[tkey + "0"], w=[tkey + "1"])
        P.op("pool", lambda e: e.tensor_tensor(out=tmp[:, 2:3], in0=tmp[:, 1:2], in1=cneg[:, 0:1], op=ALU.pow),
             r=[tkey + "1", "cneg"], w=[tkey + "2"])
        P.op("dve", lambda e: e.scalar_tensor_tensor(out=out_bf, in0=x_ap, scalar=tmp[:, 2:3],
                                                     in1=gain, op0=ALU.mult, op1=ALU.mult),
             r=[xkey, tkey + "2", "gains"], w=[okey])

    def transpose_to(src_bf, skey, ncol_blocks, ps_i, dst3, dkey, eng="act"):
        if isinstance(ps_i, tuple):
            pb, pkey = ps_i
        else:
            pb, pkey = psb[ps_i], f"ps{ps_i}"
        for kc in range(ncol_blocks):
            P.op("pe", (lambda e, kc=kc: e.transpose(out=pb[:, kc * 128:(kc + 1) * 128],
                                                    in_=src_bf[:, kc * 128:(kc + 1) * 128], identity=ident_b)),
                 r=[skey, "ident_b"], w=[pkey])
        src3 = pb[:, 0:ncol_blocks * 128].rearrange("p (a b) -> p a b", a=ncol_blocks)
        if eng == "act":
            P.op("act", lambda e: e.activation(out=dst3, in_=src3, func=AF.Copy), r=[pkey], w=[dkey])
        else:
            P.op("dve", lambda e: e.tensor_copy(out=dst3, in_=src3), r=[pkey], w=[dkey])

    def proj_tok(xT3, xkey, tcol0, W3, wkey, c0, n, ps_i):
        if isinstance(ps_i, tuple):
            po, pkey = ps_i
        else:
            po, pkey = psum[ps_i][:, 0:n], f"ps{ps_i}"
        for kc in range(8):
            P.op("pe", (lambda e, kc=kc: e.matmul(out=po, lhsT=xT3[:, kc, tcol0:tcol0 + 128],
                                                 rhs=W3[:, kc, c0:c0 + n], start=(kc == 0), stop=(kc == 7))),
                 r=[xkey, wkey], w=[pkey])

    def rope(src3, skeys, H, cs, cskey, dst3, dkey, rt, rkey):
        cosb = cs[:, None, 0:8].to_broadcast([128, H, 8])
        sinb = cs[:, None, 8:16].to_broadcast([128, H, 8])
        x1 = src3[:, :, 0:8]
        x2 = src3[:, :, 8:16]
        t = [rt[:, i * 64:i * 64 + H * 8].rearrange("p (a b) -> p a b", a=H) for i in range(4)]
        rr = list(skeys) + [cskey]
        P.op("dve", lambda e: e.tensor_tensor(out=t[0], in0=x1, in1=cosb, op=ALU.mult), r=rr, w=[rkey + "0"])
        P.op("dve", lambda e: e.tensor_tensor(out=t[1], in0=x2, in1=sinb, op=ALU.mult), r=rr, w=[rkey + "1"])
        P.op("dve", lambda e: e.tensor_tensor(out=t[2], in0=x2, in1=cosb, op=ALU.mult), r=rr, w=[rkey + "2"])
        P.op("dve", lambda e: e.tensor_tensor(out=t[3], in0=x1, in1=sinb, op=ALU.mult), r=rr, w=[rkey + "3"])
        P.op("dve", lambda e: e.tensor_tensor(out=dst3[:, :, 0:8], in0=t[0], in1=t[1], op=ALU.subtract),
             r=[rkey + "0", rkey + "1"], w=[dkey])
        P.op("dve", lambda e: e.tensor_tensor(out=dst3[:, :, 8:16], in0=t[2], in1=t[3], op=ALU.add),
             r=[rkey + "2", rkey + "3"], w=[dkey])
        P.op("act", lambda e: e.activation(out=dst3[:, :, 16:64], in_=src3[:, :, 16:64], func=AF.Copy),
             r=list(skeys), w=[dkey])

    A.mark()
    WA = A.alloc([128, 8, 1152], BF16)
    ikT = A.alloc([128, S], BF16)
    QT_r = A.alloc([128, 4, RT], BF16)
    iqT_r = A.alloc([128, 4, RT], BF16)
    sgn = A.alloc([128, 5 * 8], F32)
    wabs2 = [A.alloc([128, 8], F32) for _ in range(2)]
    qrel = A.alloc([128, 16], F32)
    bis = A.alloc([128, 8 + NBIS], F32)
    ntmp = [A.alloc([128, 4], F32) for _ in range(3)]
    cst = [A.alloc([128, 16], F32) for _ in range(4)]
    oT_r = A.alloc([64, 8, RT], BF16)
    QTz = A.alloc([128, 8, RT], BF16)
    kchunk = [A.alloc([128, 4, 512], BF16) for _ in range(2)]
    vchunk = [A.alloc([128, 4, 520], BF16) for _ in range(2)]
    diag = [A.alloc([128, 8, 128], BF16) for _ in range(2)]
    biasb = [A.alloc([128, 512], F32) for _ in range(1)]
    pT = [A.alloc([128, 512], BF16) for _ in range(2)]
    osb = [A.alloc([128, 512], F32) for _ in range(4)]
    rbuf = [A.alloc([128, 1024], BF16) for _ in range(2)]
    m0 = A.off
    nmask = [A.alloc([128, S], BF16) for _ in range(2)]
    s0 = A.off
    score2 = [A.alloc([128, S], F32) for _ in range(2)]
    SA = Arena(arena_t, s0)
    SA.off = m0
    xin = [SA.alloc([128, D], F32) for _ in range(3)]
    xnh = [SA.alloc([128, D], BF16) for _ in range(3)]
    ksb = [SA.alloc([128, 8, 64], BF16) for _ in range(4)]
    iksb = [SA.alloc([128, 2, 64], BF16) for _ in range(2)]
    iqs2 = [SA.alloc([128, 8, 64], F32) for _ in range(2)]
    junk1 = SA.alloc([128, D], BF16)
    SB = Arena(arena_t, s0 + 65536)
    SB.off = s0
    ropet = [SB.alloc([128, 256], F32) for _ in range(4)]
    vaug = [SB.alloc([128, 8, 65], BF16) for _ in range(2)]
    kTs = [SB.alloc([128, 4, 128], BF16) for _ in range(2)]
    xnT_r = SB.alloc([128, 8, RT], BF16)
    wst = [SB.alloc([128, 2048], F32) for _ in range(2)]
    xnT1 = [xnT_r[:, :, 0:128], xnT_r[:, :, 128:256], xnT_r[:, :, 256:384]]

    load_weight(w_kvi_d, WA.rearrange("p a b -> p (a b)"), 8 * 1152, 128, wst, "WA")
    for s in range(2):
        P.op("pool", (lambda e, s=s: e.memset(vaug[s].rearrange("p a b -> p (a b)"), 1.0)), w=[f"vaug{s}"])
    P.op("pool", lambda e: e.memset(QTz.rearrange("p a b -> p (a b)"), 0.0), w=["QTz"])

    def p1_stageA1(i):
        s3 = i % 3
        s4 = i % 4
        P.dma("sp", xin[s3], xc[i * 128:(i + 1) * 128, :], w=[f"xin{s3}"], slot=f"xin{s3}")
        P.dma("sp", cst[s4], cs_ctx[i * 128:(i + 1) * 128, :], w=[f"cst{s4}"], slot=f"cst{s4}")
        norm_tile(xin[s3], f"xin{s3}", gmix, xnh[s3], f"xnh{s3}", ntmp[s3], f"nt{s3}", junk1, "junk1")

    def p1_stageA2(i):
        s3 = i % 3
        transpose_to(xnh[s3], f"xnh{s3}", 8, i % 2, xnT1[s3], f"xnT1{s3}")

    def p1_stageB(i):
        s = i % 2
        s3 = i % 3
        ikps = (psum[6][:, s * 128:(s + 1) * 128], f"ps6i{s}")
        proj_tok(xnT_r, f"xnT1{s3}", s3 * 128, WA, "WA", 0, 512, 2 + s)
        proj_tok(xnT_r, f"xnT1{s3}", s3 * 128, WA, "WA", 512, 512, 4 + s)
        proj_tok(xnT_r, f"xnT1{s3}", s3 * 128, WA, "WA", 1024, 128, ikps)
        rope(psum[2 + s].rearrange("p (a b) -> p a b", a=8), [f"ps{2 + s}"], 8, cst[i % 4], f"cst{i % 4}", ksb[s],
             f"ksb{s}", ropet[s], f"rt{s}")
        P.op("act", (lambda e, s=s: e.activation(out=vaug[s][:, :, 0:64],
                                                 in_=psum[4 + s].rearrange("p (a b) -> p a b", a=8), func=AF.Copy)),
             r=[f"ps{4 + s}"], w=[f"vaug{s}"])
        rope(ikps[0].rearrange("p (a b) -> p a b", a=2), [ikps[1]], 2, cst[i % 4], f"cst{i % 4}", iksb[s], f"iksb{s}",
             ropet[2 + s], f"rt{2 + s}")
        transpose_to(ksb[s].rearrange("p a b -> p (a b)"), f"ksb{s}", 4,
                     (psb[7][:, s * 512:(s + 1) * 512], f"ps7{s}"), kTs[s], f"kTs{s}", eng="dve")
        transpose_to(iksb[s].rearrange("p a b -> p (a b)"), f"iksb{s}", 1,
                     (psb[6][:, 512 + s * 128:512 + (s + 1) * 128], f"ps6t{s}"),
                     ikT[:, i * 128:(i + 1) * 128].rearrange("p (a b) -> p a b", a=1), "ikT", eng="dve")
        P.dma("pool", kt_dram[i], kTs[s].rearrange("p a b -> p (a b)"), r=[f"kTs{s}"], w=["kt_dram"], slot=f"kst{s}")
        P.dma("pool", v_dram[i], vaug[s].rearrange("p a b -> p (a b)"), r=[f"vaug{s}"], w=["v_dram"], slot=f"vst{s}")

    p1_stageA1(0)
    p1_stageA1(1)
    p1_stageA1(2)
    p1_stageA2(0)
    p1_stageA2(1)
    for i in range(NT):
        if i + 3 < NT:
            p1_stageA1(i + 3)
        if i + 2 < NT:
            p1_stageA2(i + 2)
        p1_stageB(i)

    barrier()
    load_weight(w_qi_d, WA.rearrange("p a b -> p (a b)")[:, 0:8 * 1032], 8 * 1032, 128, wst, "WA")
    WQ = WA.rearrange("p a b -> p (a b)")[:, 0:8 * 1032].rearrange("p (a b) -> p a b", a=8)

    for r in range(NR):
        L = LR[r]
        KT = 4 * L
        NK = 512 * L
        barrier()
        def p2_stageA1(t, r=r):
            s3 = t % 3
            s4 = t % 4
            P.dma("sp", xin[s3], xo[r, t * 128:(t + 1) * 128, :], w=[f"xin{s3}"], slot=f"xin{s3}")
            P.dma("sp", cst[s4], cs_own[r, t * 128:(t + 1) * 128, :], w=[f"cst{s4}"], slot=f"cst{s4}")
            norm_tile(xin[s3], f"xin{s3}", gmix, xnh[s3], f"xnh{s3}", ntmp[s3], f"nt{s3}", junk1, "junk1")

        def p2_stageA2(t, r=r):
            s3 = t % 3
            transpose_to(xnh[s3], f"xnh{s3}", 8, t % 2, xnT_r[:, :, t * 128:(t + 1) * 128], f"xnT_r{t}")

        def p2_stageB(t, r=r):
            s = t % 2
            s3 = t % 3
            iwps = (psum[6][:, s * 8:(s + 1) * 8], f"ps6i{s}")
            proj_tok(xnT_r, f"xnT_r{t}", t * 128, WQ, "WA", 0, 512, 2 + s)
            proj_tok(xnT_r, f"xnT_r{t}", t * 128, WQ, "WA", 512, 512, 4 + s)
            proj_tok(xnT_r, f"xnT_r{t}", t * 128, WQ, "WA", 1024, 8, iwps)
            wab = wabs2[s]
            iqs = iqs2[s]
            P.op("act", (lambda e, wab=wab, iwps=iwps: e.activation(out=wab, in_=iwps[0], func=AF.Abs, scale=CS_IDX)),
                 r=[iwps[1]], w=[f"wabs{s}"])
            P.op("act", (lambda e, t=t, iwps=iwps: e.activation(out=sgn[:, t * 8:(t + 1) * 8], in_=iwps[0],
                                                                func=AF.Sign)), r=[iwps[1]], w=["sgn"])
            rope(psum[2 + s].rearrange("p (a b) -> p a b", a=8), [f"ps{2 + s}"], 8, cst[t % 4], f"cst{t % 4}", ksb[s],
                 f"ksb{s}", ropet[s], f"rt{s}")
            P.op("dve", (lambda e, s=s, wab=wab, iqs=iqs: e.tensor_tensor(
                out=iqs, in0=psum[4 + s].rearrange("p (a b) -> p a b", a=8),
                in1=wab[:, :, None].to_broadcast([128, 8, 64]), op=ALU.mult)),
                r=[f"ps{4 + s}", f"wabs{s}"], w=[f"iqs{s}"])
            rope(iqs, [f"iqs{s}"], 8, cst[t % 4], f"cst{t % 4}", ksb[2 + s], f"ksb{2 + s}", ropet[2 + s], f"rt{2 + s}")
            for kc in range(4):
                P.op("pe", (lambda e, kc=kc, s=s: e.transpose(
                    out=psb[7][:, kc * 128:(kc + 1) * 128],
                    in_=ksb[s].rearrange("p a b -> p (a b)")[:, kc * 128:(kc + 1) * 128], identity=ident_b)),
                    r=[f"ksb{s}", "ident_b"], w=["ps7q"])
            for half in range(2):
                P.op("dve", (lambda e, half=half, t=t: e.tensor_copy(
                    out=QTz[half * 64:(half + 1) * 64, :, t * 128:(t + 1) * 128].rearrange(
                        "p (a two) b -> p a two b", two=2)[:, :, half, :],
                    in_=psb[7][half * 64:(half + 1) * 64, 0:512].rearrange("p (a b) -> p a b", a=4))),
                    r=["ps7q"], w=["QTz"])
            transpose_to(ksb[2 + s].rearrange("p a b -> p (a b)"), f"ksb{2 + s}", 4,
                         (psb[7][:, 512:1024], "ps7i"), iqT_r[:, :, t * 128:(t + 1) * 128], "iqT_r", eng="dve")

        p2_stageA1(0)
        p2_stageA1(1)
        p2_stageA1(2)
        p2_stageA2(0)
        p2_stageA2(1)
        for t in range(5):
            if t + 3 < 5:
                p2_stageA1(t + 3)
            if t + 2 < 5:
                p2_stageA2(t + 2)
            p2_stageB(t)

        barrier()
        cb0 = max(0, 4 * r - 1)

        def indexer_scores(qb, r=r, L=L, KT=KT, NK=NK, cb0=cb0):
            sbi = qb % 2
            score = score2[sbi]
            dg = diag[sbi]
            for h in range(8):
                P.op("pool", (lambda e, h=h, dg=dg: e.tensor_scalar(
                    out=dg[:, h, :], in0=ident_b, scalar1=sgn[:, qb * 8 + h:qb * 8 + h + 1], scalar2=0.0,
                    op0=ALU.mult, op1=ALU.add)), r=["ident_b", "sgn"], w=[f"diag{sbi}"])
            pairs = [(c, hp) for c in range(L) for hp in range(4)]

            def emit_idx(pi):
                c, hp = pairs[pi]
                pb = pi % 2
                for half in range(2):
                    bank = half
                    P.op("pe", (lambda e, half=half, hp=hp, c=c, bank=bank: e.matmul(
                        out=psum[bank],
                        lhsT=iqT_r[half * 64:(half + 1) * 64, hp, qb * 128:(qb + 1) * 128],
                        rhs=ikT[half * 64:(half + 1) * 64, c * 512:(c + 1) * 512], start=True, stop=True)),
                        r=["iqT_r", "ikT"], w=[f"ps{bank}"])
                P.op("act", (lambda e, pb=pb: e.activation(out=rbuf[pb], in_=psbig[:, 0:1024], func=AF.Relu)),
                     r=["ps0", "ps1"], w=[f"rbuf{pb}"])

            def emit_diag(pi):
                c, hp = pairs[pi]
                pb = pi % 2
                sbank = 2 + c % 2
                for half in range(2):
                    h = hp * 2 + half
                    P.op("pe", (lambda e, h=h, half=half, pb=pb, sbank=sbank, dg=dg: e.matmul(
                        out=psum[sbank], lhsT=dg[:, h, :], rhs=rbuf[pb][:, half * 512:(half + 1) * 512],
                        start=(h == 0), stop=(h == 7))),
                        r=[f"diag{sbi}", f"rbuf{pb}"], w=[f"ps{sbank}"])
                if hp == 3:
                    sc = score[:, c * 512:(c + 1) * 512]
                    P.op("act", (lambda e, sc=sc, sbank=sbank: e.activation(out=sc, in_=psum[sbank], func=AF.Copy)),
                         r=[f"ps{sbank}"], w=[f"score{sbi}_{c}"])

            steps = []
            steps.append(lambda: emit_idx(0))
            for pi in range(len(pairs)):
                def st(pi=pi):
                    if pi + 1 < len(pairs):
                        emit_idx(pi + 1)
                    emit_diag(pi)
                steps.append(st)

            def tail():
                indexer_tail(qb)
            return steps, tail

        def indexer_tail(qb, r=r, L=L, KT=KT, NK=NK, cb0=cb0):
            sbi = qb % 2
            score = score2[sbi]
            allsc = [f"score{sbi}_{c}" for c in range(L)]
            P.op("dve", lambda e: e.tensor_reduce(out=bis[:, 0:1], in_=score[:, 0:NK], axis=AX.X, op=ALU.max,
                                                  apply_absolute_value=True), r=allsc, w=["bis_am"])
            qp = qpos[:, r * 5 + qb:r * 5 + qb + 1]
            for c in range(cb0, L):
                P.op("dve", (lambda e, c=c: e.tensor_scalar(out=qrel[:, c:c + 1], in0=qp, scalar1=float(-512 * c),
                                                            scalar2=None, op0=ALU.add)),
                     r=["qpos"], w=["qrel"])
                bb = biasb[0]
                P.op("dve", (lambda e, c=c, bb=bb: e.tensor_scalar(out=bb, in0=iota, scalar1=qrel[:, c:c + 1],
                                                                  scalar2=NEG, op0=ALU.is_gt, op1=ALU.mult)),
                     r=["iota", "qrel"], w=["biasb0"])
                sc = score[:, c * 512:(c + 1) * 512]
                P.op("dve", (lambda e, sc=sc, bb=bb: e.tensor_tensor(out=sc, in0=sc, in1=bb, op=ALU.add)),
                     r=["biasb0", f"score{sbi}_{c}", "bis_am"], w=[f"score{sbi}_{c}"])

        def bisect_block(qb, r=r, L=L, KT=KT, NK=NK):
            sbi = qb % 2
            score = score2[sbi]
            nm = nmask[sbi]
            nmk = f"nmask{sbi}"
            allsc = [f"score{sbi}_{c}" for c in range(L)]
            P.op("dve", lambda e: e.tensor_scalar(out=bis[:, 2:3], in0=bis[:, 0:1], scalar1=2.000001, scalar2=1e-30,
                                                  op0=ALU.mult, op1=ALU.add), r=["bis_am"], w=["bis_w0"])
            P.op("dve", lambda e: e.tensor_scalar(out=bis[:, 8:8 + NBIS], in0=pow2, scalar1=bis[:, 2:3], scalar2=None,
                                                  op0=ALU.mult), r=["pow2", "bis_w0"], w=["bis_wi"])
            P.op("dve", lambda e: e.tensor_tensor(out=bis[:, 3:4], in0=bis[:, 8:9], in1=bis[:, 0:1], op=ALU.subtract),
                 r=["bis_wi", "bis_am"], w=["bis_mid"])
            for it in range(NBIS):
                wi = bis[:, 8 + it:9 + it]
                P.op("dve", lambda e: e.tensor_scalar(out=nm[:, 0:NK], in0=score[:, 0:NK], scalar1=bis[:, 3:4],
                                                      scalar2=None, op0=ALU.is_ge, op1=ALU.add,
                                                      accum_out=bis[:, 4:5]),
                     r=allsc + ["bis_mid"], w=[nmk, "bis_cnt"])
                P.op("dve", lambda e: e.tensor_scalar(out=bis[:, 5:6], in0=bis[:, 4:5], scalar1=255.5,
                                                      scalar2=-0.5, op0=ALU.is_ge, op1=ALU.add),
                     r=["bis_cnt"], w=["bis_t"])
                P.op("dve", (lambda e, wi=wi: e.scalar_tensor_tensor(out=bis[:, 3:4], in0=bis[:, 5:6], scalar=wi,
                                                                    in1=bis[:, 3:4], op0=ALU.mult, op1=ALU.add)),
                     r=["bis_t", "bis_wi", "bis_mid"], w=["bis_mid"])
            P.op("dve", lambda e: e.scalar_tensor_tensor(out=bis[:, 1:2], in0=bis[:, 8 + NBIS - 1:8 + NBIS],
                                                         scalar=-0.5, in1=bis[:, 3:4], op0=ALU.mult, op1=ALU.add),
                 r=["bis_wi", "bis_mid"], w=["bis_lo"])
            P.op("dve", lambda e: e.tensor_scalar(out=nm[:, 0:NK], in0=score[:, 0:NK], scalar1=bis[:, 1:2],
                                                  scalar2=-30000.0, op0=ALU.is_lt, op1=ALU.mult),
                 r=allsc + ["bis_lo"], w=[nmk])

        def attention_block(qb, L=L, KT=KT):
            nm = nmask[qb % 2]
            nmk = f"nmask{qb % 2}"
            q0 = qb * 128
            groups = []
            for c4 in range(L):
                for j in range(4):
                    for hg in range(2):
                        groups.append((c4, j, hg))

            def load_chunk(c4):
                s3 = c4 % 2
                P.dma("sp", kchunk[s3], kt_dram[4 * c4:4 * c4 + 4].rearrange("t p c -> p t c"), r=["kt_dram"],
                      w=[f"kch{s3}"], slot=f"kch{s3}")
                P.dma("sp", vchunk[s3], v_dram[4 * c4:4 * c4 + 4].rearrange("t p c -> p t c"), r=["v_dram"],
                      w=[f"vch{s3}"], slot=f"vch{s3}")

            def emit_scores(gi):
                c4, j, hg = groups[gi]
                if j == 0 and hg == 0:
                    load_chunk(c4)
                s3 = c4 % 2
                kt = 4 * c4 + j
                bank = 4 + gi % 2
                pbuf = gi % 2
                P.op("pe", (lambda e, kt=kt, bank=bank: e.matmul(
                    out=psum[bank], lhsT=nm[:, kt * 128:(kt + 1) * 128], rhs=ident4, start=True, stop=False)),
                    r=[nmk, "ident4"], w=[f"ps{bank}"])
                for pp in range(2):
                    p = hg * 2 + pp
                    P.op("pe", (lambda e, pp=pp, p=p, j=j, s3=s3, bank=bank: e.matmul(
                        out=psum[bank][:, pp * 256:(pp + 1) * 256],
                        lhsT=kchunk[s3][:, j, p * 128:(p + 1) * 128],
                        rhs=QTz[:, 2 * p:2 * p + 2, q0:q0 + 128], start=False, stop=(pp == 1))),
                        r=[f"kch{s3}", "QTz"], w=[f"ps{bank}"])
                P.op("act", (lambda e, bank=bank, pbuf=pbuf: e.activation(out=pT[pbuf], in_=psum[bank],
                                                                         func=AF.Exp, scale=0.125)),
                     r=[f"ps{bank}"], w=[f"pT{pbuf}"])

            def emit_pv(gi):
                c4, j, hg = groups[gi]
                s3 = c4 % 2
                kt = 4 * c4 + j
                pbuf = gi % 2
                for hh in range(4):
                    h = hg * 4 + hh
                    P.op("pe", (lambda e, hh=hh, h=h, j=j, s3=s3, pbuf=pbuf, hg=hg, kt=kt: e.matmul(
                        out=psum[6 + hg][0:65, hh * 128:(hh + 1) * 128],
                        lhsT=vchunk[s3][:, j, h * 65:(h + 1) * 65],
                        rhs=pT[pbuf][:, hh * 128:(hh + 1) * 128],
                        start=(kt == 0 and hh == 0), stop=(kt == KT - 1))),
                        r=[f"vch{s3}", f"pT{pbuf}"], w=[f"ps{6 + hg}"])

            ng = len(groups)
            steps = []
            steps.append(lambda: emit_scores(0))
            for gi in range(ng):
                def st(gi=gi):
                    if gi + 1 < ng:
                        emit_scores(gi + 1)
                    emit_pv(gi)
                steps.append(st)

            def epilogue():
                attention_epi(qb)
            return steps, epilogue

        def attention_epi(qb):
            for hg in range(2):
                oi = (qb % 2) * 2 + hg
                ob = osb[oi]
                P.op("act", (lambda e, ob=ob, hg=hg: e.activation(out=ob[0:65, :], in_=psum[6 + hg][0:65, :],
                                                                  func=AF.Copy)),
                     r=[f"ps{6 + hg}"], w=[f"osb{oi}"])

        def attention_fin(qb):
            q0 = qb * 128
            for hg in range(2):
                oi = (qb % 2) * 2 + hg
                ob = osb[oi]
                P.op("dve", (lambda e, ob=ob: e.reciprocal(out=ob[64:65, :], in_=ob[64:65, :])),
                     r=[f"osb{oi}"], w=[f"osb{oi}"])
                P.op("dve", (lambda e, ob=ob: e.tensor_copy(out=hl[64:65, 0:512], in_=ob[64:65, :])),
                     r=[f"osb{oi}"], w=["hl_hi"])
                P.op("dve", (lambda e, ob=ob: e.tensor_tensor(out=hl[64:65, 512:1024], in0=ob[64:65, :],
                                                              in1=hl[64:65, 0:512], op=ALU.subtract)),
                     r=[f"osb{oi}", "hl_hi"], w=["hl_lo"])
                P.op("pe", (lambda e, hg=hg: e.matmul(out=psum[4 + hg][0:64, :], lhsT=ones_b[64:65, 0:64],
                                                      rhs=hl[64:65, 0:512], start=True, stop=False)),
                     r=["ones_b", "hl_hi"], w=[f"ps{4 + hg}"])
                P.op("pe", (lambda e, hg=hg: e.matmul(out=psum[4 + hg][0:64, :], lhsT=ones_b[64:65, 0:64],
                                                      rhs=hl[64:65, 512:1024], start=False, stop=True)),
                     r=["ones_b", "hl_lo"], w=[f"ps{4 + hg}"])
                P.op("dve", (lambda e, ob=ob, hg=hg: e.tensor_tensor(
                    out=oT_r[0:64, hg * 4:(hg + 1) * 4, q0:q0 + 128],
                    in0=ob[0:64, :].rearrange("p (a b) -> p a b", a=4),
                    in1=psum[4 + hg][0:64, :].rearrange("p (a b) -> p a b", a=4), op=ALU.mult)),
                    r=[f"osb{oi}", f"ps{4 + hg}"], w=["oT_r"])

        def run_merged(si, sa):
            ni, na = len(si), len(sa)
            i = a = 0
            while i < ni or a < na:
                if a >= na or (i < ni and i * na <= a * ni):
                    si[i]()
                    i += 1
                else:
                    sa[a]()
                    a += 1

        st_i, tail_i = indexer_scores(0)
        run_merged(st_i, [])
        tail_i()
        for qb in range(5):
            if qb > 1:
                attention_fin(qb - 2)
            bisect_block(qb)
            st_i, tail_i, st_a, epi_a = [], None, [], None
            if qb + 1 < 5:
                st_i, tail_i = indexer_scores(qb + 1)
            if qb > 0:
                st_a, epi_a = attention_block(qb - 1)
            run_merged(st_i, st_a)
            if tail_i is not None:
                tail_i()
            if epi_a is not None:
                epi_a()
        attention_fin(3)
        st_a, epi_a = attention_block(4)
        run_merged([], st_a)
        epi_a()
        attention_fin(4)
        P.dma("pool", ot_dram[r], oT_r.rearrange("p a b -> p (a b)"), r=["oT_r"], w=["ot_dram"], slot="otst")

    barrier()
    A.release()
    A.mark()
    Wu = A.alloc([128, 8, 512], BF16)
    Wg = A.alloc([128, 8, 2048], BF16)
    Wpp = A.alloc([128, 4, 1024], BF16)
    Wap = A.alloc([64, 8, 1024], BF16)
    Wo = A.alloc([128, 8, 1024], BF16)
    poolw = A.alloc([128, 4, 128], BF16)
    pscale = A.alloc([128, 4], F32)
    pm_std = A.alloc([128, 8, 128], BF16)
    pm_first = A.alloc([128, NR * 12, 128], BF16)
    xr = [A.alloc([128, D], F32) for _ in range(5)]
    x0 = A.off - 5 * 4096
    SX = Arena(arena_t, A.off)
    SX.off = x0
    wst3 = [SX.alloc([128, 2048], F32) for _ in range(2)]
    xnh3 = [A.alloc([128, D], BF16) for _ in range(2)]
    ntmp3 = [A.alloc([128, 4], F32) for _ in range(2)]
    junk3 = A.alloc([128, D], BF16)
    xnT3b = A.alloc([128, 8, RT], BF16)
    u_sb = A.alloc([128, 5, 512], BF16)
    pooledT = A.alloc([128, 4, RT], BF16)
    mixedT = A.alloc([128, 4, RT], BF16)
    oT3b = A.alloc([64, 8, RT], BF16)
    mergedT = A.alloc([128, 8, RT], BF16)
    sg0 = A.alloc([128, 512], F32)
    sg1 = A.alloc([128, 512], F32)
    tt0 = A.alloc([128, 512], F32)
    tt1 = A.alloc([128, 512], F32)
    hb = [A.alloc([128, D], F32) for _ in range(2)]

    P.dma("sp", pscale, pscale_d, w=["pscale", "cchain"], slot="c")
    load_weight(w_u_d, Wu.rearrange("p a b -> p (a b)"), 8 * 512, 128, wst3, "Wu")
    load_weight(w_g_d, Wg.rearrange("p a b -> p (a b)"), 8 * 2048, 128, wst3, "Wg")
    load_weight(w_pp_d, Wpp.rearrange("p a b -> p (a b)"), 4 * 1024, 128, wst3, "Wpp")
    load_weight(w_ap_d, Wap.rearrange("p a b -> p (a b)"), 8 * 1024, 64, wst3, "Wap")
    load_weight(w_out_d, Wo.rearrange("p a b -> p (a b)"), 8 * 1024, 128, wst3, "Wo")
    load_weight(poolw_d, poolw.rearrange("p a b -> p (a b)"), 4 * 128, 128, wst3, "poolw")
    load_weight(pm_std_d, pm_std.rearrange("p a b -> p (a b)"), 8 * 128, 128, wst3, "pm_std")
    load_weight(pm_first_d, pm_first.rearrange("p a b -> p (a b)"), NR * 12 * 128, 128, wst3, "pm_first")
    barrier()

    slabs = [(0, 512), (512, 128)]
    for r in range(NR):
        P.dma("sp", oT3b.rearrange("p a b -> p (a b)"), ot_dram[r], r=["ot_dram"], w=["oT3"], slot="otl")
        for t in range(5):
            s = t % 2
            P.dma("sp", xr[t], xo[r, t * 128:(t + 1) * 128, :], w=[f"xr{t}"], slot=f"xr{t}")
            norm_tile(xr[t], f"xr{t}", gmix, xnh3[s], f"xnh3{s}", ntmp3[s], f"nt{s}", junk3, "junk3")
            transpose_to(xnh3[s], f"xnh3{s}", 8, s, xnT3b[:, :, t * 128:(t + 1) * 128], "xnT3")
            proj_tok(xnT3b, "xnT3", t * 128, Wu, "Wu", 0, 512, 2)
            P.op("act", (lambda e, t=t: e.activation(out=u_sb[:, t, :], in_=psum[2][:, 0:512], func=AF.Copy)),
                 r=["ps2"], w=[f"u{t}"])
            for g in range(4):
                mats = []
                if t == 1:
                    mats.append((t, pm_first[:, r * 12 + g, :], "pm_first"))
                    mats.append((t, pm_first[:, r * 12 + 4 + g, :], "pm_first"))
                    mats.append((t - 1, pm_first[:, r * 12 + 8 + g, :], "pm_first"))
                else:
                    mats.append((t, pm_std[:, g, :], "pm_std"))
                    if t > 0:
                        mats.append((t - 1, pm_std[:, 4 + g, :], "pm_std"))
                for mi, (tu, M, mk) in enumerate(mats):
                    P.op("pe", (lambda e, tu=tu, M=M, g=g, mi=mi, nm=len(mats): e.matmul(
                        out=psum[3][:, g * 128:(g + 1) * 128], lhsT=u_sb[:, tu, g * 128:(g + 1) * 128], rhs=M,
                        start=(mi == 0), stop=(mi == nm - 1))),
                        r=[f"u{tu}", mk], w=["ps3"])
            P.op("act", (lambda e, t=t: e.activation(out=pooledT[:, :, t * 128:(t + 1) * 128],
                                                     in_=psum[3].rearrange("p (a b) -> p a b", a=4),
                                                     func=AF.Copy)), r=["ps3"], w=["pooledT"])
        for g in range(4):
            for (c0, n) in slabs:
                P.op("pe", (lambda e, g=g, c0=c0, n=n: e.matmul(out=psum[2][:, 0:n], lhsT=poolw[:, g, :],
                                                               rhs=pooledT[:, g, c0:c0 + n], start=True, stop=True)),
                     r=["poolw", "pooledT"], w=["ps2"])
                P.op("dve", (lambda e, g=g, c0=c0, n=n: e.tensor_scalar(out=mixedT[:, g, c0:c0 + n],
                                                                       in0=psum[2][:, 0:n],
                                                                       scalar1=pscale[:, g:g + 1], scalar2=None,
                                                                       op0=ALU.mult)),
                     r=["ps2", "pscale"], w=["mixedT"])
        for c in range(8):
            for (c0, n) in slabs:
                for g in range(4):
                    P.op("pe", (lambda e, g=g, c=c, c0=c0, n=n: e.matmul(
                        out=psum[4][:, 0:n], lhsT=Wpp[:, g, c * 128:(c + 1) * 128], rhs=mixedT[:, g, c0:c0 + n],
                        start=(g == 0), stop=(g == 3))), r=["Wpp", "mixedT"], w=["ps4"])
                for h in range(8):
                    P.op("pe", (lambda e, h=h, c=c, c0=c0, n=n: e.matmul(
                        out=psum[5][:, 0:n], lhsT=Wap[0:64, h, c * 128:(c + 1) * 128], rhs=oT3b[0:64, h, c0:c0 + n],
                        start=(h == 0), stop=(h == 7))), r=["Wap", "oT3"], w=["ps5"])
                for gi in range(2):
                    for kc in range(8):
                        P.op("pe", (lambda e, gi=gi, kc=kc, c=c, c0=c0, n=n: e.matmul(
                            out=psum[6 + gi][:, 0:n],
                            lhsT=Wg[:, kc, gi * 1024 + c * 128:gi * 1024 + (c + 1) * 128],
                            rhs=xnT3b[:, kc, c0:c0 + n], start=(kc == 0), stop=(kc == 7))),
                            r=["Wg", "xnT3"], w=[f"ps{6 + gi}"])
                P.op("act", (lambda e, n=n: e.activation(out=sg0[:, 0:n], in_=psum[6][:, 0:n], func=AF.Sigmoid)),
                     r=["ps6"], w=["sg0"])
                P.op("act", (lambda e, n=n: e.activation(out=sg1[:, 0:n], in_=psum[7][:, 0:n], func=AF.Sigmoid)),
                     r=["ps7"], w=["sg1"])
                P.op("dve", (lambda e, n=n: e.tensor_tensor(out=tt0[:, 0:n], in0=sg0[:, 0:n], in1=psum[4][:, 0:n],
                                                            op=ALU.mult)), r=["sg0", "ps4"], w=["tt0"])
                P.op("dve", (lambda e, n=n: e.tensor_tensor(out=tt1[:, 0:n], in0=sg1[:, 0:n], in1=psum[5][:, 0:n],
                                                            op=ALU.mult)), r=["sg1", "ps5"], w=["tt1"])
                P.op("dve", (lambda e, c=c, c0=c0, n=n: e.tensor_tensor(out=mergedT[:, c, c0:c0 + n], in0=tt0[:, 0:n],
                                                                       in1=tt1[:, 0:n], op=ALU.add)),
                     r=["tt0", "tt1"], w=["mergedT"])
        for t in range(5):
            hs = t % 2
            for half in range(2):
                bank = 2 + half
                for c in range(8):
                    P.op("pe", (lambda e, c=c, t=t, half=half, bank=bank: e.matmul(
                        out=psum[bank][:, 0:512], lhsT=mergedT[:, c, t * 128:(t + 1) * 128],
                        rhs=Wo[:, c, half * 512:(half + 1) * 512], start=(c == 0), stop=(c == 7))),
                        r=["mergedT", "Wo"], w=[f"ps{bank}"])
                P.op("dve", (lambda e, t=t, half=half, bank=bank, hs=hs: e.tensor_tensor(
                    out=hb[hs][:, half * 512:(half + 1) * 512], in0=xr[t][:, half * 512:(half + 1) * 512],
                    in1=psum[bank][:, 0:512], op=ALU.add)), r=[f"xr{t}", f"ps{bank}"], w=[f"hb{hs}"])
            P.dma("pool", h_dram[r, t * 128:(t + 1) * 128, :], hb[hs], r=[f"hb{hs}"], w=["h_dram"], slot=f"hst{hs}")

    barrier()
    A.release()
    A.mark()
    Wup = A.alloc([128, 8, 2 * DFF], BF16)
    Wdn = A.alloc([128, NF, 1024], BF16)
    convw = A.alloc([128, 44, 3], F32)
    convb = A.alloc([128, 44], F32)
    g2 = A.alloc([128, 2 * D], F32)
    ht = [None] + [A.alloc([128, D], F32) for _ in range(4)]
    hnT = A.alloc([128, 8, RT], BF16)
    NFH = NF // 2
    actT = A.alloc([128, NFH, 512], BF16)
    ntmp4 = [A.alloc([128, 4], F32) for _ in range(2)]
    upsb = [A.alloc([128, 516], F32) for _ in range(2)]
    cc2 = A.alloc([128, D], F32)
    cc = [cc2[:, 0:512], cc2[:, 512:1024]]
    ht[0] = cc2
    silu_o = A.alloc([128, 512], F32)
    junk4 = silu_o.bitcast(BF16)
    xnh4 = [A.alloc([128, D], BF16) for _ in range(1)]
    h0 = A.off
    wflat = arena_t
    SW = Arena(arena_t, CAP)
    wst4 = [hnT.rearrange("p a b -> p (a b)")[:, 0:4096].bitcast(F32),
           actT.rearrange("p a b -> p (a b)")[:, 0:4096].bitcast(F32)]

    P.dma("sp", convw.rearrange("p a b -> p (a b)"), convw_d, w=["convw", "cchain"], slot="c")
    P.dma("sp", convb, convb_d, w=["convb", "cchain"], slot="c")
    P.dma("sp", g2, g_d[:, D:3 * D], w=["g2", "cchain"], slot="c")
    load_weight(w_up_d, Wup.rearrange("p a b -> p (a b)"), 8 * 2 * DFF, 128, wst4, "W3b")
    load_weight(w_dn_d, Wdn.rearrange("p a b -> p (a b)"), NF * 1024, 128, wst4, "W3b")
    barrier()

    for r in range(NR):
        for t in range(5):
            hk = [f"ht{t}"] if t > 0 else ["cc0", "cc1"]
            P.dma("sp", ht[t], h_dram[r, t * 128:(t + 1) * 128, :], r=["h_dram"], w=hk, slot=f"ht{t}")
            s = t % 2
            P.op("act", (lambda e, t=t, s=s: e.activation(out=junk4, in_=ht[t], func=AF.Square,
                                                         accum_out=ntmp4[s][:, 0:1])),
                 r=hk, w=["silu_o", f"nt{s}0"])
            P.op("dve", (lambda e, s=s: e.tensor_scalar(out=ntmp4[s][:, 1:2], in0=ntmp4[s][:, 0:1], scalar1=1.0 / D,
                                                        scalar2=EPS, op0=ALU.mult, op1=ALU.add)),
                 r=[f"nt{s}0"], w=[f"nt{s}1"])
            P.op("pool", (lambda e, s=s: e.tensor_tensor(out=ntmp4[s][:, 2:3], in0=ntmp4[s][:, 1:2], in1=cneg[:, 0:1],
                                                         op=ALU.pow)), r=[f"nt{s}1", "cneg"], w=[f"nt{s}2"])
            P.op("dve", (lambda e, t=t, s=s: e.scalar_tensor_tensor(
                out=xnh4[0], in0=ht[t], scalar=ntmp4[s][:, 2:3], in1=g2[:, 0:D], op0=ALU.mult, op1=ALU.mult)),
                r=hk + [f"nt{s}2", "g2"], w=["xnh40"])
            transpose_to(xnh4[0], "xnh40", 8, s, hnT[:, :, t * 128:(t + 1) * 128], "hnT")
        for fh in range(2):
            for fi in range(NFH):
                f = fh * NFH + fi
                for ab in range(2):
                    col = ab * DFF + f * 128
                    ch = ab * NF + f
                    bank = 2 + ab
                    hbank = 4 + ab
                    for kc in range(8):
                        P.op("pe", (lambda e, kc=kc, col=col, bank=bank: e.matmul(
                            out=psum[bank][:, 0:512], lhsT=Wup[:, kc, col:col + 128], rhs=hnT[:, kc, 128:640],
                            start=(kc == 0), stop=(kc == 7))), r=["W3b", "hnT"], w=[f"ps{bank}"])
                    for kc in range(8):
                        P.op("pe", (lambda e, kc=kc, col=col, hbank=hbank: e.matmul(
                            out=psum[hbank][:, 0:2], lhsT=Wup[:, kc, col:col + 128], rhs=hnT[:, kc, 126:128],
                            start=(kc == 0), stop=(kc == 7))), r=["W3b", "hnT"], w=[f"ps{hbank}"])
                    ub = upsb[ab]
                    cb = cc[ab]
                    P.op("act", (lambda e, ub=ub, bank=bank: e.activation(out=ub[:, 2:514], in_=psum[bank][:, 0:512],
                                                                          func=AF.Copy)),
                         r=[f"ps{bank}"], w=[f"upsb{ab}"])
                    P.op("act", (lambda e, ub=ub, hbank=hbank, r=r: e.activation(
                        out=ub[:, 0:2], in_=psum[hbank][:, 0:2], func=AF.Identity, scale=hsc[:, r:r + 1])),
                        r=[f"ps{hbank}", "hsc"], w=[f"upsb{ab}"])
                    P.op("act", (lambda e, cb=cb, bank=bank, ch=ch: e.activation(
                        out=cb, in_=psum[bank][:, 0:512], func=AF.Identity, scale=convw[:, ch, 2:3],
                        bias=convb[:, ch:ch + 1])), r=[f"ps{bank}", "convw", "convb"], w=[f"cc{ab}"])
                    P.op("dve", (lambda e, cb=cb, ub=ub, ch=ch: e.scalar_tensor_tensor(
                        out=cb, in0=ub[:, 1:513], scalar=convw[:, ch, 1:2], in1=cb, op0=ALU.mult, op1=ALU.add)),
                        r=[f"upsb{ab}", f"cc{ab}", "convw"], w=[f"cc{ab}"])
                    P.op("dve", (lambda e, cb=cb, ub=ub, ch=ch: e.scalar_tensor_tensor(
                        out=cb, in0=ub[:, 0:512], scalar=convw[:, ch, 0:1], in1=cb, op0=ALU.mult, op1=ALU.add)),
                        r=[f"upsb{ab}", f"cc{ab}", "convw"], w=[f"cc{ab}"])
                P.op("act", lambda e: e.activation(out=silu_o, in_=cc[0], func=AF.Silu), r=["cc0"], w=["silu_o"])
                P.op("dve", (lambda e, fi=fi: e.tensor_tensor(out=actT[:, fi, :], in0=silu_o, in1=cc[1], op=ALU.mult)),
                     r=["silu_o", "cc1"], w=["actT"])
            for t in range(4):
                for half in range(2):
                    bank = 6 + half
                    for fi in range(NFH):
                        f = fh * NFH + fi
                        P.op("pe", (lambda e, f=f, fi=fi, t=t, half=half, bank=bank: e.matmul(
                            out=psum[bank][:, 0:512], lhsT=actT[:, fi, t * 128:(t + 1) * 128],
                            rhs=Wdn[:, f, half * 512:(half + 1) * 512], start=(fi == 0), stop=(fi == NFH - 1))),
                            r=["actT", "W3b"], w=[f"ps{bank}"])
                    P.op("dve", (lambda e, t=t, half=half, bank=bank: e.tensor_tensor(
                        out=ht[t + 1][:, half * 512:(half + 1) * 512], in0=ht[t + 1][:, half * 512:(half + 1) * 512],
                        in1=psum[bank][:, 0:512], op=ALU.add)), r=[f"ht{t + 1}", f"ps{bank}"], w=[f"ht{t + 1}"])
        for t in range(4):
            s = t % 2
            P.op("act", (lambda e, t=t, s=s: e.activation(out=junk4, in_=ht[t + 1], func=AF.Square,
                                                         accum_out=ntmp4[s][:, 0:1])),
                 r=[f"ht{t + 1}"], w=["silu_o", f"nt{s}0"])
            P.op("dve", (lambda e, s=s: e.tensor_scalar(out=ntmp4[s][:, 1:2], in0=ntmp4[s][:, 0:1], scalar1=1.0 / D,
                                                        scalar2=EPS, op0=ALU.mult, op1=ALU.add)),
                 r=[f"nt{s}0"], w=[f"nt{s}1"])
            P.op("pool", (lambda e, s=s: e.tensor_tensor(out=ntmp4[s][:, 2:3], in0=ntmp4[s][:, 1:2], in1=cneg[:, 0:1],
                                                         op=ALU.pow)), r=[f"nt{s}1", "cneg"], w=[f"nt{s}2"])
            P.op("dve", (lambda e, t=t, s=s: e.scalar_tensor_tensor(
                out=ht[t + 1], in0=ht[t + 1], scalar=ntmp4[s][:, 2:3], in1=g2[:, D:2 * D], op0=ALU.mult,
                op1=ALU.mult)), r=[f"ht{t + 1}", f"nt{s}2", "g2"], w=[f"ht{t + 1}"])
            P.dma("pool", out_d[r, t * 128:(t + 1) * 128, :], ht[t + 1], r=[f"ht{t + 1}"], w=["out"],
                  slot=f"ost{t % 2}")
    P.op("pool", None, r=["out"])
    P.emit(st)
    st.close()
    return nc


def _chunks(j):
    return [j, 7 - j, 8 + j, 15 - j]


def _rope_tables(pos):
    half = 8
    inv = 1.0 / (500000.0 ** (np.arange(half, dtype=np.float32) * 2.0 / 16.0))
    ang = pos.astype(np.float32)[:, None] * inv[None, :].astype(np.float32)
    return np.concatenate([np.cos(ang), np.sin(ang)], axis=1).astype(np.float32)


def _pool_mats(first):
    Am = np.zeros((4, 128, 128), np.float64)
    Bm = np.zeros((4, 128, 128), np.float64)
    for g, w in enumerate((2, 4, 8, 16)):
        for t in range(128):
            cnt = min(t + 1, w) if first else w
            for tp in range(t - w + 1, t + 1):
                if tp >= 0:
                    Am[g, tp, t] += 1.0 / cnt
                elif not first:
                    Bm[g, tp + 128, t] += 1.0 / cnt
            Am[g, t, t] -= 1.0
    return Am, Bm


def _pk(a, kc):
    n = a.shape[1]
    return np.ascontiguousarray(a.reshape(kc, 128, n).transpose(1, 0, 2).reshape(128, kc * n))


_NC_CACHE = {}


def kernel(x, norm_mix_g, w_in, pool_w, pool_scale, w_pool_proj, w_attn_proj, w_out,
           norm_ffn_g, w_up, conv_w, conv_b, w_down, norm_final_g):
    f = lambda a: np.asarray(a, dtype=np.float32)
    x = f(x)
    w_in = f(w_in)[0]
    cuts = np.cumsum([512, 512, 512, 512, 512, 64, 8, 2048])
    Wu_, Wq_, Wk_, Wv_, Wiq_, Wik_, Wiw_, Wg_ = np.split(w_in, cuts[:-1], axis=1)
    shared = {}
    shared["w_kvi"] = _pk(np.concatenate([Wk_, Wv_, Wik_, Wik_], axis=1), 8)
    shared["w_qi"] = _pk(np.concatenate([Wq_, Wiq_, Wiw_], axis=1), 8)
    shared["w_u"] = _pk(Wu_, 8)
    shared["w_g"] = _pk(Wg_, 8)
    shared["pool_w"] = np.ascontiguousarray(f(pool_w)[0].transpose(1, 0, 2).reshape(128, 512))
    shared["pool_scale"] = np.ascontiguousarray(f(pool_scale)[0].reshape(4, 128).T)
    shared["w_pp"] = _pk(f(w_pool_proj)[0], 4)
    shared["w_ap"] = np.ascontiguousarray(f(w_attn_proj)[0].reshape(8, 64, 1024).transpose(1, 0, 2).reshape(64, 8192))
    shared["w_out"] = _pk(f(w_out)[0], 8)
    shared["w_up"] = _pk(f(w_up)[0], 8)
    shared["w_down"] = _pk(f(w_down)[0], NF)
    shared["conv_w"] = np.ascontiguousarray(f(conv_w)[0].reshape(3, 44, 128).transpose(2, 1, 0).reshape(128, 132))
    shared["conv_b"] = np.ascontiguousarray(f(conv_b)[0].reshape(44, 128).T)
    gains = np.concatenate([f(norm_mix_g)[0], f(norm_ffn_g)[0], f(norm_final_g)])
    shared["gains"] = np.ascontiguousarray(np.broadcast_to(gains[None, :], (128, 3 * D)))
    shared["iota"] = np.ascontiguousarray(np.broadcast_to(np.arange(512, dtype=np.float32)[None, :], (128, 512)))
    shared["pow2"] = np.ascontiguousarray(np.broadcast_to(
        (0.5 ** np.arange(1, NBIS + 1)).astype(np.float32)[None, :], (128, NBIS)))
    shared["ident"] = np.eye(128, dtype=np.float32)
    shared["cs_ctx"] = _rope_tables(np.arange(S))
    A_std, B_std = _pool_mats(False)
    A_fst, B_fst = _pool_mats(True)
    shared["pm_std"] = np.ascontiguousarray(
        np.concatenate([A_std, B_std], 0).transpose(1, 0, 2).reshape(128, 8 * 128)).astype(np.float32)

    def hi_lo(a):
        hi = a.astype(np.float32).astype(ml_dtypes.bfloat16).astype(np.float64)
        return hi, a - hi

    in_maps = []
    for c in range(8):
        b, j = c // 4, c % 4
        m = dict(shared)
        m["xc"] = np.ascontiguousarray(x[b])
        xo = np.zeros((NR, RT, D), np.float32)
        qp = np.zeros((NR, RT), np.float32)
        hs = np.zeros((NR,), np.float32)
        pmf = np.zeros((NR, 12, 128, 128), np.float64)
        for r, i in enumerate(_chunks(j)):
            t0 = 512 * i
            xo[r, 128:] = x[b, t0:t0 + 512]
            qp[r, 128:] = np.arange(t0, t0 + 512)
            if i > 0:
                xo[r, :128] = x[b, t0 - 128:t0]
                qp[r, :128] = np.arange(t0 - 128, t0)
                hs[r] = 1.0
                pmf[r, 0:4], pmf[r, 4:8], pmf[r, 8:12] = A_std, 0.0, B_std
            else:
                qp[r, :128] = np.arange(0, 128)
                hi, lo = hi_lo(A_fst)
                pmf[r, 0:4], pmf[r, 4:8], pmf[r, 8:12] = hi, lo, 0.0
        m["xo"] = xo
        m["qpos"] = np.ascontiguousarray(qp.reshape(NR * 5, 128).T)
        m["hscale"] = np.ascontiguousarray(np.broadcast_to(hs[None, :], (128, NR)))
        m["cs_own"] = _rope_tables(qp.reshape(-1)).reshape(NR, RT, 16)
        m["pm_first"] = np.ascontiguousarray(
            pmf.reshape(NR * 12, 128, 128).transpose(1, 0, 2).reshape(128, NR * 12 * 128)).astype(np.float32)
        in_maps.append(m)

    if "nc" not in _NC_CACHE:
        _NC_CACHE["nc"] = build_program()
    nc = _NC_CACHE["nc"]
    res = run_bass_kernel_spmd(nc, in_maps, core_ids=list(range(8)))
    _NC_CACHE['res'] = res if DEBUG else None
    out = np.zeros((2, S, D), np.float32)
    for c in range(8):
        b, j = c // 4, c % 4
        o = res.results[c]["out"]
        for r, i in enumerate(_chunks(j)):
            out[b, 512 * i:512 * i + 512] = o[r]
    return out
```

```python
import numpy as np
from contextlib import ExitStack
import ml_dtypes
import concourse.bass as bass
import concourse.mybir as mybir
from concourse.bass_utils import run_bass_kernel_spmd

F32 = mybir.dt.float32
BF16 = mybir.dt.bfloat16
U8 = mybir.dt.uint8
ALU = mybir.AluOpType
AF = mybir.ActivationFunctionType
AX = mybir.AxisListType

D = 1024
S = 8192
NT = 64
NR = 4
RT = 640
LR = [4, 8, 12, 16]
NBIS = 16
DFF = 2816
NF = 22
EPS = 1e-6
CS_IDX = (64 ** -0.5) * (8 ** -0.5)
NEG = -1.0e30


class Prog:
    ENGS = ("pe", "act", "dve", "pool", "sp")

    def __init__(self, nc):
        self.nc = nc
        self.ops = []

    def op(self, eng, fn, r=(), w=(), slot=None, barrier=False):
        self.ops.append(dict(eng=eng, fn=fn, r=tuple(r), w=tuple(w), slot=slot, barrier=barrier))

    def dma(self, eng, out, in_, r=(), w=(), slot=None, **kw):
        assert slot is not None
        self.op(eng, lambda e: e.dma_start(out=out, in_=in_, **kw), r, w, slot)

    def emit(self, stack):
        nc = self.nc
        ops = self.ops
        last_w = {}
        readers = {}
        deps = [None] * len(ops)
        needs_inc = [False] * len(ops)
        last_eng = {}
        last_slot = {}
        last_bar = None
        for i, o in enumerate(ops):
            d = set()
            if o["barrier"]:
                d.update(last_eng.values())
                d.update(last_slot.values())
            else:
                for k in o["r"]:
                    if k in last_w:
                        d.add(last_w[k])
                for k in o["w"]:
                    if k in last_w:
                        d.add(last_w[k])
                    for rr in readers.get(k, ()):
                        d.add(rr)
            if last_bar is not None:
                d.add(last_bar)
            d.discard(i)
            d = {p for p in d if not (ops[p]["slot"] is None and ops[p]["eng"] == "pe" and o["eng"] == "pe"
                                      and not ops[p]["barrier"])}
            deps[i] = d
            for p in d:
                needs_inc[p] = True
            for k in o["r"]:
                readers.setdefault(k, []).append(i)
            for k in o["w"]:
                last_w[k] = i
                readers[k] = []
            if o["slot"] is not None:
                last_slot[o["slot"]] = i
            elif o["fn"] is not None:
                last_eng[o["eng"]] = i
            if o["barrier"]:
                last_bar = i
        self._deps = deps
        if CHECK_DEADLOCK:
            done = [False] * len(ops)
            pe_ = {e: [i for i, o in enumerate(ops) if o["eng"] == e] for e in self.ENGS}
            ptr = {e: 0 for e in self.ENGS}
            progress = True
            while progress:
                progress = False
                for e in self.ENGS:
                    while ptr[e] < len(pe_[e]):
                        i = pe_[e][ptr[e]]
                        if all(done[p] for p in deps[i]):
                            done[i] = True
                            ptr[e] += 1
                            progress = True
                        else:
                            break
            stuck = {e: pe_[e][ptr[e]] for e in self.ENGS if ptr[e] < len(pe_[e])}
            print("DEADLOCK CHECK: stuck =", stuck)
            for e, i in stuck.items():
                print(e, i, ops[i]["r"], ops[i]["w"], [(p, ops[p]["eng"], done[p]) for p in deps[i] if not done[p]])
        sems = {}

        def get_sem(name):
            if name not in sems:
                sems[name] = stack.enter_context(nc.semaphore(name))
            return sems[name]

        cnt = {}
        val = [None] * len(ops)
        for i, o in enumerate(ops):
            if o["slot"] is not None:
                key = "d_" + o["slot"]
                cnt[key] = cnt.get(key, 0) + 16
                val[i] = (key, cnt[key])
            elif needs_inc[i]:
                key = "e_" + o["eng"]
                cnt[key] = cnt.get(key, 0) + 1
                val[i] = (key, cnt[key])
        per_eng = {e: [] for e in self.ENGS}
        for i, o in enumerate(ops):
            per_eng[o["eng"]].append(i)
        block = stack.enter_context(nc.Block())

        def make(engname):
            idxs = per_eng[engname]

            def body(eng):
                waited = {}
                for i in idxs:
                    o = ops[i]
                    need = {}
                    for p in deps[i]:
                        k, v = val[p]
                        if v > need.get(k, 0):
                            need[k] = v
                    for k, v in need.items():
                        if waited.get(k, 0) >= v:
                            continue
                        eng.wait_ge(get_sem(k), v)
                        waited[k] = v
                    if o["fn"] is None:
                        continue
                    ins = o["fn"](eng)
                    if val[i] is not None:
                        k, v = val[i]
                        ins.then_inc(get_sem(k), 16 if o["slot"] is not None else 1)
            return body

        if per_eng["sp"]:
            block.sync(make("sp"))
        if per_eng["act"]:
            block.scalar(make("act"))
        if per_eng["dve"]:
            block.vector(make("dve"))
        if per_eng["pool"]:
            block.gpsimd(make("pool"))
        if per_eng["pe"]:
            block.tensor(make("pe"))


class Arena:
    def __init__(self, base, cap):
        self.base = base
        self.cap = cap
        self.off = 0
        self.marks = []

    def alloc(self, shape, dt):
        esz = 2 if dt == BF16 else 4
        n = int(np.prod(shape[1:])) * esz
        n_al = (n + 63) // 64 * 64
        assert self.off + n_al <= self.cap, f"arena overflow {self.off}+{n_al}>{self.cap}"
        ap = self.base[0:shape[0], self.off:self.off + n].bitcast(dt)
        if len(shape) == 3:
            ap = ap.rearrange("p (a b) -> p a b", a=shape[1])
        self.off += n_al
        return ap

    def mark(self):
        self.marks.append(self.off)

    def release(self):
        self.off = self.marks.pop()


DEBUG = False
PIPE_IDX = True
SKEW1 = True
SKEW2 = True
CHECK_DEADLOCK = False
MERGE_ATT = True
SPLIT_PAIR = False


def build_program():
    nc = bass.Bass("TRN2", target_bir_lowering=False)
    skind = "ExternalOutput" if DEBUG else "Internal"

    def din(name, shape, dt=F32):
        return nc.dram_tensor(name, list(shape), dt, kind="ExternalInput").ap()

    xc = din("xc", [S, D])
    xo = din("xo", [NR, RT, D])
    cs_ctx = din("cs_ctx", [S, 16])
    cs_own = din("cs_own", [NR, RT, 16])
    qpos_d = din("qpos", [128, NR * 5])
    hsc_d = din("hscale", [128, NR])
    iota_d = din("iota", [128, 512])
    pow2_d = din("pow2", [128, NBIS])
    ident_d = din("ident", [128, 128])
    g_d = din("gains", [128, 3 * D])
    w_kvi_d = din("w_kvi", [128, 8 * 1152])
    w_qi_d = din("w_qi", [128, 8 * 1032])
    w_u_d = din("w_u", [128, 8 * 512])
    w_g_d = din("w_g", [128, 8 * 2048])
    poolw_d = din("pool_w", [128, 4 * 128])
    pscale_d = din("pool_scale", [128, 4])
    w_pp_d = din("w_pp", [128, 4 * 1024])
    w_ap_d = din("w_ap", [64, 8 * 1024])
    w_out_d = din("w_out", [128, 8 * 1024])
    w_up_d = din("w_up", [128, 8 * 2 * DFF])
    w_dn_d = din("w_down", [128, NF * 1024])
    convw_d = din("conv_w", [128, 44 * 3])
    convb_d = din("conv_b", [128, 44])
    pm_std_d = din("pm_std", [128, 8 * 128])
    pm_first_d = din("pm_first", [128, NR * 12 * 128])
    out_d = nc.dram_tensor("out", [NR, 512, D], F32, kind="ExternalOutput").ap()
    kt_dram = nc.dram_tensor("kt_scr", [NT, 128, 512], BF16, kind=skind).ap()
    v_dram = nc.dram_tensor("v_scr", [NT, 128, 520], BF16, kind=skind).ap()
    ot_dram = nc.dram_tensor("ot_scr", [NR, 64, 8 * RT], BF16, kind=skind).ap()
    h_dram = nc.dram_tensor("h_scr", [NR, RT, D], F32, kind=skind).ap()

    P = Prog(nc)
    st = ExitStack()
    CAP = 207 * 1024 + 768
    arena_t = st.enter_context(nc.sbuf_tensor("arena", [128, CAP], U8))
    A = Arena(arena_t, CAP)
    psbig = st.enter_context(nc.psum_tensor("psbig", [128, 4096], F32))
    psum = [psbig[:, i * 512:(i + 1) * 512] for i in range(8)]
    psb = [p.bitcast(BF16) for p in psum]

    ident_f = A.alloc([128, 128], F32)
    ident_b = A.alloc([128, 128], BF16)
    ident4 = A.alloc([128, 512], BF16)
    gmix = A.alloc([128, D], F32)
    qpos = A.alloc([128, NR * 5], F32)
    hsc = A.alloc([128, NR], F32)
    iota = A.alloc([128, 512], F32)
    pow2 = A.alloc([128, NBIS], F32)
    ones_b = A.alloc([128, 64], BF16)
    hl = A.alloc([128, 1024], BF16)
    cneg = A.alloc([128, 2], F32)
    P.dma("sp", ident_f, ident_d, w=["ident_f", "cchain"], slot="c")
    P.dma("sp", gmix, g_d[:, 0:D], w=["gains", "cchain"], slot="c")
    P.dma("sp", qpos, qpos_d, w=["qpos", "cchain"], slot="c")
    P.dma("sp", hsc, hsc_d, w=["hsc", "cchain"], slot="c")
    P.dma("sp", iota, iota_d, w=["iota", "cchain"], slot="c")
    P.dma("sp", pow2, pow2_d, w=["pow2", "cchain"], slot="c")
    P.op("dve", lambda e: e.tensor_copy(out=ident_b, in_=ident_f), r=["ident_f"], w=["ident_b"])
    P.op("pool", lambda e: e.tensor_copy(out=ident4.rearrange("p (a b) -> p a b", a=4),
                                         in_=ident_f[:, None, :].to_broadcast([128, 4, 128])),
         r=["ident_f"], w=["ident4"])
    P.op("pool", lambda e: e.memset(ones_b, 1.0), w=["ones_b"])
    P.op("pool", lambda e: e.memset(cneg[:, 0:1], -0.5), w=["cneg"])
    A.mark()

    wst_i = [0]

    def load_weight(dram2d, dst2d, ncols, nparts, wst, key):
        c0 = 0
        while c0 < ncols:
            n = min(2048, ncols - c0)
            s = wst_i[0] % 2
            wst_i[0] += 1
            stg = wst[s]
            P.dma("sp", stg[0:nparts, 0:n], dram2d[0:nparts, c0:c0 + n], w=[f"wst{s}"], slot=f"wst{s}")
            eng = "pool" if s == 0 else "dve"
            P.op(eng, (lambda e, a=dst2d[0:nparts, c0:c0 + n], b=stg[0:nparts, 0:n]: e.tensor_copy(out=a, in_=b)),
                 r=[f"wst{s}"], w=[key])
            c0 += n

    def barrier():
        dummy = cneg[:, 1:2]
        P.op("pool", lambda e: e.memset(dummy, 0.0), barrier=True)

    def norm_tile(x_ap, xkey, gain, out_bf, okey, tmp, tkey, junk, jkey):
        P.op("act", lambda e: e.activation(out=junk, in_=x_ap, func=AF.Square, accum_out=tmp[:, 0:1]),
             r=[xkey], w=[jkey, tkey + "0"])
        P.op("dve", lambda e: e.tensor_scalar(out=tmp[:, 1:2], in0=tmp[:, 0:1], scalar1=1.0 / D, scalar2=EPS,
                                              op0=ALU.mult, op1=ALU.add), r=[tkey + "0"], w=[tkey + "1"])
        P.op("pool", lambda e: e.tensor_tensor(out=tmp[:, 2:3], in0=tmp[:, 1:2], in1=cneg[:, 0:1], op=ALU.pow),
             r=[tkey + "1", "cneg"], w=[tkey + "2"])
        P.op("dve", lambda e: e.scalar_tensor_tensor(out=out_bf, in0=x_ap, scalar=tmp[:, 2:3],
                                                     in1=gain, op0=ALU.mult, op1=ALU.mult),
             r=[xkey, tkey + "2", "gains"], w=[okey])

    def transpose_to(src_bf, skey, ncol_blocks, ps_i, dst3, dkey, eng="act"):
        if isinstance(ps_i, tuple):
            pb, pkey = ps_i
        else:
            pb, pkey = psb[ps_i], f"ps{ps_i}"
        for kc in range(ncol_blocks):
            P.op("pe", (lambda e, kc=kc: e.transpose(out=pb[:, kc * 128:(kc + 1) * 128],
                                                    in_=src_bf[:, kc * 128:(kc + 1) * 128], identity=ident_b)),
                 r=[skey, "ident_b"], w=[pkey])
        src3 = pb[:, 0:ncol_blocks * 128].rearrange("p (a b) -> p a b", a=ncol_blocks)
        if eng == "act":
            P.op("act", lambda e: e.activation(out=dst3, in_=src3, func=AF.Copy), r=[pkey], w=[dkey])
        else:
            P.op("dve", lambda e: e.tensor_copy(out=dst3, in_=src3), r=[pkey], w=[dkey])

    def proj_tok(xT3, xkey, tcol0, W3, wkey, c0, n, ps_i):
        if isinstance(ps_i, tuple):
            po, pkey = ps_i
        else:
            po, pkey = psum[ps_i][:, 0:n], f"ps{ps_i}"
        for kc in range(8):
            P.op("pe", (lambda e, kc=kc: e.matmul(out=po, lhsT=xT3[:, kc, tcol0:tcol0 + 128],
                                                 rhs=W3[:, kc, c0:c0 + n], start=(kc == 0), stop=(kc == 7))),
                 r=[xkey, wkey], w=[pkey])

    def rope(src3, skeys, H, cs, cskey, dst3, dkey, rt, rkey):
        cosb = cs[:, None, 0:8].to_broadcast([128, H, 8])
        sinb = cs[:, None, 8:16].to_broadcast([128, H, 8])
        x1 = src3[:, :, 0:8]
        x2 = src3[:, :, 8:16]
        t = [rt[:, i * 64:i * 64 + H * 8].rearrange("p (a b) -> p a b", a=H) for i in range(4)]
        rr = list(skeys) + [cskey]
        P.op("dve", lambda e: e.tensor_tensor(out=t[0], in0=x1, in1=cosb, op=ALU.mult), r=rr, w=[rkey + "0"])
        P.op("dve", lambda e: e.tensor_tensor(out=t[1], in0=x2, in1=sinb, op=ALU.mult), r=rr, w=[rkey + "1"])
        P.op("dve", lambda e: e.tensor_tensor(out=t[2], in0=x2, in1=cosb, op=ALU.mult), r=rr, w=[rkey + "2"])
        P.op("dve", lambda e: e.tensor_tensor(out=t[3], in0=x1, in1=sinb, op=ALU.mult), r=rr, w=[rkey + "3"])
        P.op("dve", lambda e: e.tensor_tensor(out=dst3[:, :, 0:8], in0=t[0], in1=t[1], op=ALU.subtract),
             r=[rkey + "0", rkey + "1"], w=[dkey])
        P.op("dve", lambda e: e.tensor_tensor(out=dst3[:, :, 8:16], in0=t[2], in1=t[3], op=ALU.add),
             r=[rkey + "2", rkey + "3"], w=[dkey])
        P.op("act", lambda e: e.activation(out=dst3[:, :, 16:64], in_=src3[:, :, 16:64], func=AF.Copy),
             r=list(skeys), w=[dkey])

    A.mark()
    WA = A.alloc([128, 8, 1152], BF16)
    ikT = A.alloc([128, S], BF16)
    QT_r = A.alloc([128, 4, RT], BF16)
    iqT_r = A.alloc([128, 4, RT], BF16)
    sgn = A.alloc([128, 5 * 8], F32)
    wabs2 = [A.alloc([128, 8], F32) for _ in range(2)]
    qrel = A.alloc([128, 16], F32)
    bis = A.alloc([128, 8 + NBIS], F32)
    ntmp = [A.alloc([128, 4], F32) for _ in range(3)]
    cst = [A.alloc([128, 16], F32) for _ in range(4)]
    oT_r = A.alloc([64, 8, RT], BF16)
    QTz = A.alloc([128, 8, RT], BF16)
    kchunk = [A.alloc([128, 4, 512], BF16) for _ in range(2)]
    vchunk = [A.alloc([128, 4, 520], BF16) for _ in range(2)]
    diag = [A.alloc([128, 8, 128], BF16) for _ in range(2)]
    biasb = [A.alloc([128, 512], F32) for _ in range(1)]
    pT = [A.alloc([128, 512], BF16) for _ in range(2)]
    osb = [A.alloc([128, 512], F32) for _ in range(4)]
    rbuf = [A.alloc([128, 1024], BF16) for _ in range(2)]
    m0 = A.off
    nmask = [A.alloc([128, S], BF16) for _ in range(2)]
    s0 = A.off
    score2 = [A.alloc([128, S], F32) for _ in range(2)]
    SA = Arena(arena_t, s0)
    SA.off = m0
    xin = [SA.alloc([128, D], F32) for _ in range(3)]
    xnh = [SA.alloc([128, D], BF16) for _ in range(3)]
    ksb = [SA.alloc([128, 8, 64], BF16) for _ in range(4)]
    iksb = [SA.alloc([128, 2, 64], BF16) for _ in range(2)]
    iqs2 = [SA.alloc([128, 8, 64], F32) for _ in range(2)]
    junk1 = SA.alloc([128, D], BF16)
    SB = Arena(arena_t, s0 + 65536)
    SB.off = s0
    ropet = [SB.alloc([128, 256], F32) for _ in range(4)]
    vaug = [SB.alloc([128, 8, 65], BF16) for _ in range(2)]
    kTs = [SB.alloc([128, 4, 128], BF16) for _ in range(2)]
    xnT_r = SB.alloc([128, 8, RT], BF16)
    wst = [SB.alloc([128, 2048], F32) for _ in range(2)]
    xnT1 = [xnT_r[:, :, 0:128], xnT_r[:, :, 128:256], xnT_r[:, :, 256:384]]

    load_weight(w_kvi_d, WA.rearrange("p a b -> p (a b)"), 8 * 1152, 128, wst, "WA")
    for s in range(2):
        P.op("pool", (lambda e, s=s: e.memset(vaug[s].rearrange("p a b -> p (a b)"), 1.0)), w=[f"vaug{s}"])
    P.op("pool", lambda e: e.memset(QTz.rearrange("p a b -> p (a b)"), 0.0), w=["QTz"])

    def p1_stageA1(i):
        s3 = i % 3
        s4 = i % 4
        P.dma("sp", xin[s3], xc[i * 128:(i + 1) * 128, :], w=[f"xin{s3}"], slot=f"xin{s3}")
        P.dma("sp", cst[s4], cs_ctx[i * 128:(i + 1) * 128, :], w=[f"cst{s4}"], slot=f"cst{s4}")
        norm_tile(xin[s3], f"xin{s3}", gmix, xnh[s3], f"xnh{s3}", ntmp[s3], f"nt{s3}", junk1, "junk1")

    def p1_stageA2(i):
        s3 = i % 3
        transpose_to(xnh[s3], f"xnh{s3}", 8, i % 2, xnT1[s3], f"xnT1{s3}")

    def p1_stageB(i):
        s = i % 2
        s3 = i % 3
        ikps = (psum[6][:, s * 128:(s + 1) * 128], f"ps6i{s}")
        proj_tok(xnT_r, f"xnT1{s3}", s3 * 128, WA, "WA", 0, 512, 2 + s)
        proj_tok(xnT_r, f"xnT1{s3}", s3 * 128, WA, "WA", 512, 512, 4 + s)
        proj_tok(xnT_r, f"xnT1{s3}", s3 * 128, WA, "WA", 1024, 128, ikps)
        rope(psum[2 + s].rearrange("p (a b) -> p a b", a=8), [f"ps{2 + s}"], 8, cst[i % 4], f"cst{i % 4}", ksb[s],
             f"ksb{s}", ropet[s], f"rt{s}")
        P.op("act", (lambda e, s=s: e.activation(out=vaug[s][:, :, 0:64],
                                                 in_=psum[4 + s].rearrange("p (a b) -> p a b", a=8), func=AF.Copy)),
             r=[f"ps{4 + s}"], w=[f"vaug{s}"])
        rope(ikps[0].rearrange("p (a b) -> p a b", a=2), [ikps[1]], 2, cst[i % 4], f"cst{i % 4}", iksb[s], f"iksb{s}",
             ropet[2 + s], f"rt{2 + s}")
        transpose_to(ksb[s].rearrange("p a b -> p (a b)"), f"ksb{s}", 4,
                     (psb[7][:, s * 512:(s + 1) * 512], f"ps7{s}"), kTs[s], f"kTs{s}", eng="dve")
        transpose_to(iksb[s].rearrange("p a b -> p (a b)"), f"iksb{s}", 1,
                     (psb[6][:, 512 + s * 128:512 + (s + 1) * 128], f"ps6t{s}"),
                     ikT[:, i * 128:(i + 1) * 128].rearrange("p (a b) -> p a b", a=1), "ikT", eng="dve")
        P.dma("pool", kt_dram[i], kTs[s].rearrange("p a b -> p (a b)"), r=[f"kTs{s}"], w=["kt_dram"], slot=f"kst{s}")
        P.dma("pool", v_dram[i], vaug[s].rearrange("p a b -> p (a b)"), r=[f"vaug{s}"], w=["v_dram"], slot=f"vst{s}")

    p1_stageA1(0)
    p1_stageA1(1)
    p1_stageA1(2)
    p1_stageA2(0)
    p1_stageA2(1)
    for i in range(NT):
        if i + 3 < NT:
            p1_stageA1(i + 3)
        if i + 2 < NT:
            p1_stageA2(i + 2)
        p1_stageB(i)

    barrier()
    load_weight(w_qi_d, WA.rearrange("p a b -> p (a b)")[:, 0:8 * 1032], 8 * 1032, 128, wst, "WA")
    WQ = WA.rearrange("p a b -> p (a b)")[:, 0:8 * 1032].rearrange("p (a b) -> p a b", a=8)

    for r in range(NR):
        L = LR[r]
        KT = 4 * L
        NK = 512 * L
        barrier()
        def p2_stageA1(t, r=r):
            s3 = t % 3
            s4 = t % 4
            P.dma("sp", xin[s3], xo[r, t * 128:(t + 1) * 128, :], w=[f"xin{s3}"], slot=f"xin{s3}")
            P.dma("sp", cst[s4], cs_own[r, t * 128:(t + 1) * 128, :], w=[f"cst{s4}"], slot=f"cst{s4}")
            norm_tile(xin[s3], f"xin{s3}", gmix, xnh[s3], f"xnh{s3}", ntmp[s3], f"nt{s3}", junk1, "junk1")

        def p2_stageA2(t, r=r):
            s3 = t % 3
            transpose_to(xnh[s3], f"xnh{s3}", 8, t % 2, xnT_r[:, :, t * 128:(t + 1) * 128], f"xnT_r{t}")

        def p2_stageB(t, r=r):
            s = t % 2
            s3 = t % 3
            iwps = (psum[6][:, s * 8:(s + 1) * 8], f"ps6i{s}")
            proj_tok(xnT_r, f"xnT_r{t}", t * 128, WQ, "WA", 0, 512, 2 + s)
            proj_tok(xnT_r, f"xnT_r{t}", t * 128, WQ, "WA", 512, 512, 4 + s)
            proj_tok(xnT_r, f"xnT_r{t}", t * 128, WQ, "WA", 1024, 8, iwps)
            wab = wabs2[s]
            iqs = iqs2[s]
            P.op("act", (lambda e, wab=wab, iwps=iwps: e.activation(out=wab, in_=iwps[0], func=AF.Abs, scale=CS_IDX)),
                 r=[iwps[1]], w=[f"wabs{s}"])
            P.op("act", (lambda e, t=t, iwps=iwps: e.activation(out=sgn[:, t * 8:(t + 1) * 8], in_=iwps[0],
                                                                func=AF.Sign)), r=[iwps[1]], w=["sgn"])
            rope(psum[2 + s].rearrange("p (a b) -> p a b", a=8), [f"ps{2 + s}"], 8, cst[t % 4], f"cst{t % 4}", ksb[s],
                 f"ksb{s}", ropet[s], f"rt{s}")
            P.op("dve", (lambda e, s=s, wab=wab, iqs=iqs: e.tensor_tensor(
                out=iqs, in0=psum[4 + s].rearrange("p (a b) -> p a b", a=8),
                in1=wab[:, :, None].to_broadcast([128, 8, 64]), op=ALU.mult)),
                r=[f"ps{4 + s}", f"wabs{s}"], w=[f"iqs{s}"])
            rope(iqs, [f"iqs{s}"], 8, cst[t % 4], f"cst{t % 4}", ksb[2 + s], f"ksb{2 + s}", ropet[2 + s], f"rt{2 + s}")
            for kc in range(4):
                P.op("pe", (lambda e, kc=kc, s=s: e.transpose(
                    out=psb[7][:, kc * 128:(kc + 1) * 128],
                    in_=ksb[s].rearrange("p a b -> p (a b)")[:, kc * 128:(kc + 1) * 128], identity=ident_b)),
                    r=[f"ksb{s}", "ident_b"], w=["ps7q"])
            for half in range(2):
                P.op("dve", (lambda e, half=half, t=t: e.tensor_copy(
                    out=QTz[half * 64:(half + 1) * 64, :, t * 128:(t + 1) * 128].rearrange(
                        "p (a two) b -> p a two b", two=2)[:, :, half, :],
                    in_=psb[7][half * 64:(half + 1) * 64, 0:512].rearrange("p (a b) -> p a b", a=4))),
                    r=["ps7q"], w=["QTz"])
            transpose_to(ksb[2 + s].rearrange("p a b -> p (a b)"), f"ksb{2 + s}", 4,
                         (psb[7][:, 512:1024], "ps7i"), iqT_r[:, :, t * 128:(t + 1) * 128], "iqT_r", eng="dve")

        p2_stageA1(0)
        p2_stageA1(1)
        p2_stageA1(2)
        p2_stageA2(0)
        p2_stageA2(1)
        for t in range(5):
            if t + 3 < 5:
                p2_stageA1(t + 3)
            if t + 2 < 5:
                p2_stageA2(t + 2)
            p2_stageB(t)

        barrier()
        cb0 = max(0, 4 * r - 1)

        def indexer_scores(qb, r=r, L=L, KT=KT, NK=NK, cb0=cb0):
            sbi = qb % 2
            score = score2[sbi]
            dg = diag[sbi]
            for h in range(8):
                P.op("pool", (lambda e, h=h, dg=dg: e.tensor_scalar(
                    out=dg[:, h, :], in0=ident_b, scalar1=sgn[:, qb * 8 + h:qb * 8 + h + 1], scalar2=0.0,
                    op0=ALU.mult, op1=ALU.add)), r=["ident_b", "sgn"], w=[f"diag{sbi}"])
            pairs = [(c, hp) for c in range(L) for hp in range(4)]

            def emit_idx(pi):
                c, hp = pairs[pi]
                pb = pi % 2
                for half in range(2):
                    bank = pb * 2 + half
                    P.op("pe", (lambda e, half=half, hp=hp, c=c, bank=bank: e.matmul(
                        out=psum[bank],
                        lhsT=iqT_r[half * 64:(half + 1) * 64, hp, qb * 128:(qb + 1) * 128],
                        rhs=ikT[half * 64:(half + 1) * 64, c * 512:(c + 1) * 512], start=True, stop=True)),
                        r=["iqT_r", "ikT"], w=[f"ps{bank}"])
                P.op("act", (lambda e, pb=pb: e.activation(out=rbuf[pb], in_=psbig[:, pb * 1024:(pb + 1) * 1024],
                                                           func=AF.Relu)),
                     r=[f"ps{pb * 2}", f"ps{pb * 2 + 1}"], w=[f"rbuf{pb}"])

            def emit_diag(pi):
                c, hp = pairs[pi]
                pb = pi % 2
                sbank = 4 + c % 2
                for half in range(2):
                    h = hp * 2 + half
                    P.op("pe", (lambda e, h=h, half=half, pb=pb, sbank=sbank, dg=dg: e.matmul(
                        out=psum[sbank], lhsT=dg[:, h, :], rhs=rbuf[pb][:, half * 512:(half + 1) * 512],
                        start=(h == 0), stop=(h == 7))),
                        r=[f"diag{sbi}", f"rbuf{pb}"], w=[f"ps{sbank}"])
                if hp == 3:
                    sc = score[:, c * 512:(c + 1) * 512]
                    P.op("act", (lambda e, sc=sc, sbank=sbank: e.activation(out=sc, in_=psum[sbank], func=AF.Copy)),
                         r=[f"ps{sbank}"], w=[f"score{sbi}_{c}"])

            if PIPE_IDX:
                emit_idx(0)
                for pi in range(len(pairs)):
                    if pi + 1 < len(pairs):
                        emit_idx(pi + 1)
                    emit_diag(pi)
            else:
                for pi in range(len(pairs)):
                    emit_idx(pi)
                    emit_diag(pi)
            allsc = [f"score{sbi}_{c}" for c in range(L)]
            P.op("dve", lambda e: e.tensor_reduce(out=bis[:, 0:1], in_=score[:, 0:NK], axis=AX.X, op=ALU.max,
                                                  apply_absolute_value=True), r=allsc, w=["bis_am"])
            qp = qpos[:, r * 5 + qb:r * 5 + qb + 1]
            for c in range(cb0, L):
                P.op("dve", (lambda e, c=c: e.tensor_scalar(out=qrel[:, c:c + 1], in0=qp, scalar1=float(-512 * c),
                                                            scalar2=None, op0=ALU.add)),
                     r=["qpos"], w=["qrel"])
                bb = biasb[0]
                P.op("dve", (lambda e, c=c, bb=bb: e.tensor_scalar(out=bb, in0=iota, scalar1=qrel[:, c:c + 1],
                                                                  scalar2=NEG, op0=ALU.is_gt, op1=ALU.mult)),
                     r=["iota", "qrel"], w=["biasb0"])
                sc = score[:, c * 512:(c + 1) * 512]
                P.op("dve", (lambda e, sc=sc, bb=bb: e.tensor_tensor(out=sc, in0=sc, in1=bb, op=ALU.add)),
                     r=["biasb0", f"score{sbi}_{c}", "bis_am"], w=[f"score{sbi}_{c}"])

        def bisect_block(qb, r=r, L=L, KT=KT, NK=NK):
            sbi = qb % 2
            score = score2[sbi]
            nm = nmask[sbi]
            nmk = f"nmask{sbi}"
            allsc = [f"score{sbi}_{c}" for c in range(L)]
            P.op("dve", lambda e: e.tensor_scalar(out=bis[:, 2:3], in0=bis[:, 0:1], scalar1=2.000001, scalar2=1e-30,
                                                  op0=ALU.mult, op1=ALU.add), r=["bis_am"], w=["bis_w0"])
            P.op("dve", lambda e: e.tensor_scalar(out=bis[:, 8:8 + NBIS], in0=pow2, scalar1=bis[:, 2:3], scalar2=None,
                                                  op0=ALU.mult), r=["pow2", "bis_w0"], w=["bis_wi"])
            P.op("dve", lambda e: e.tensor_tensor(out=bis[:, 3:4], in0=bis[:, 8:9], in1=bis[:, 0:1], op=ALU.subtract),
                 r=["bis_wi", "bis_am"], w=["bis_mid"])
            for it in range(NBIS):
                wi = bis[:, 8 + it:9 + it]
                P.op("dve", lambda e: e.tensor_scalar(out=nm[:, 0:NK], in0=score[:, 0:NK], scalar1=bis[:, 3:4],
                                                      scalar2=None, op0=ALU.is_ge, op1=ALU.add,
                                                      accum_out=bis[:, 4:5]),
                     r=allsc + ["bis_mid"], w=[nmk, "bis_cnt"])
                P.op("dve", lambda e: e.tensor_scalar(out=bis[:, 5:6], in0=bis[:, 4:5], scalar1=255.5,
                                                      scalar2=-0.5, op0=ALU.is_ge, op1=ALU.add),
                     r=["bis_cnt"], w=["bis_t"])
                P.op("dve", (lambda e, wi=wi: e.scalar_tensor_tensor(out=bis[:, 3:4], in0=bis[:, 5:6], scalar=wi,
                                                                    in1=bis[:, 3:4], op0=ALU.mult, op1=ALU.add)),
                     r=["bis_t", "bis_wi", "bis_mid"], w=["bis_mid"])
            P.op("dve", lambda e: e.scalar_tensor_tensor(out=bis[:, 1:2], in0=bis[:, 8 + NBIS - 1:8 + NBIS],
                                                         scalar=-0.5, in1=bis[:, 3:4], op0=ALU.mult, op1=ALU.add),
                 r=["bis_wi", "bis_mid"], w=["bis_lo"])
            P.op("dve", lambda e: e.tensor_scalar(out=nm[:, 0:NK], in0=score[:, 0:NK], scalar1=bis[:, 1:2],
                                                  scalar2=-30000.0, op0=ALU.is_lt, op1=ALU.mult),
                 r=allsc + ["bis_lo"], w=[nmk])

        def attention_block(qb, L=L, KT=KT):
            nm = nmask[qb % 2]
            nmk = f"nmask{qb % 2}"
            q0 = qb * 128
            groups = []
            for c4 in range(L):
                for j in range(4):
                    for hg in range(2):
                        groups.append((c4, j, hg))

            def load_chunk(c4):
                s3 = c4 % 2
                P.dma("sp", kchunk[s3], kt_dram[4 * c4:4 * c4 + 4].rearrange("t p c -> p t c"), r=["kt_dram"],
                      w=[f"kch{s3}"], slot=f"kch{s3}")
                P.dma("sp", vchunk[s3], v_dram[4 * c4:4 * c4 + 4].rearrange("t p c -> p t c"), r=["v_dram"],
                      w=[f"vch{s3}"], slot=f"vch{s3}")

            def emit_scores(gi):
                c4, j, hg = groups[gi]
                if j == 0 and hg == 0:
                    load_chunk(c4)
                s3 = c4 % 2
                kt = 4 * c4 + j
                bank = 4 + gi % 2
                pbuf = gi % 2
                P.op("pe", (lambda e, kt=kt, bank=bank: e.matmul(
                    out=psum[bank], lhsT=nm[:, kt * 128:(kt + 1) * 128], rhs=ident4, start=True, stop=False)),
                    r=[nmk, "ident4"], w=[f"ps{bank}"])
                for pp in range(2):
                    p = hg * 2 + pp
                    P.op("pe", (lambda e, pp=pp, p=p, j=j, s3=s3, bank=bank: e.matmul(
                        out=psum[bank][:, pp * 256:(pp + 1) * 256],
                        lhsT=kchunk[s3][:, j, p * 128:(p + 1) * 128],
                        rhs=QTz[:, 2 * p:2 * p + 2, q0:q0 + 128], start=False, stop=(pp == 1))),
                        r=[f"kch{s3}", "QTz"], w=[f"ps{bank}"])
                P.op("act", (lambda e, bank=bank, pbuf=pbuf: e.activation(out=pT[pbuf], in_=psum[bank],
                                                                         func=AF.Exp, scale=0.125)),
                     r=[f"ps{bank}"], w=[f"pT{pbuf}"])

            def emit_pv(gi):
                c4, j, hg = groups[gi]
                s3 = c4 % 2
                kt = 4 * c4 + j
                pbuf = gi % 2
                for hh in range(4):
                    h = hg * 4 + hh
                    P.op("pe", (lambda e, hh=hh, h=h, j=j, s3=s3, pbuf=pbuf, hg=hg, kt=kt: e.matmul(
                        out=psum[6 + hg][0:65, hh * 128:(hh + 1) * 128],
                        lhsT=vchunk[s3][:, j, h * 65:(h + 1) * 65],
                        rhs=pT[pbuf][:, hh * 128:(hh + 1) * 128],
                        start=(kt == 0 and hh == 0), stop=(kt == KT - 1))),
                        r=[f"vch{s3}", f"pT{pbuf}"], w=[f"ps{6 + hg}"])

            ng = len(groups)
            emit_scores(0)
            for gi in range(ng):
                if gi + 1 < ng:
                    emit_scores(gi + 1)
                emit_pv(gi)
            for hg in range(2):
                oi = (qb % 2) * 2 + hg
                ob = osb[oi]
                P.op("act", (lambda e, ob=ob, hg=hg: e.activation(out=ob[0:65, :], in_=psum[6 + hg][0:65, :],
                                                                  func=AF.Copy)),
                     r=[f"ps{6 + hg}"], w=[f"osb{oi}"])

        def attention_fin(qb):
            q0 = qb * 128
            for hg in range(2):
                oi = (qb % 2) * 2 + hg
                ob = osb[oi]
                P.op("dve", (lambda e, ob=ob: e.reciprocal(out=ob[64:65, :], in_=ob[64:65, :])),
                     r=[f"osb{oi}"], w=[f"osb{oi}"])
                P.op("dve", (lambda e, ob=ob: e.tensor_copy(out=hl[64:65, 0:512], in_=ob[64:65, :])),
                     r=[f"osb{oi}"], w=["hl_hi"])
                P.op("dve", (lambda e, ob=ob: e.tensor_tensor(out=hl[64:65, 512:1024], in0=ob[64:65, :],
                                                              in1=hl[64:65, 0:512], op=ALU.subtract)),
                     r=[f"osb{oi}", "hl_hi"], w=["hl_lo"])
                P.op("pe", (lambda e, hg=hg: e.matmul(out=psum[4 + hg][0:64, :], lhsT=ones_b[64:65, 0:64],
                                                      rhs=hl[64:65, 0:512], start=True, stop=False)),
                     r=["ones_b", "hl_hi"], w=[f"ps{4 + hg}"])
                P.op("pe", (lambda e, hg=hg: e.matmul(out=psum[4 + hg][0:64, :], lhsT=ones_b[64:65, 0:64],
                                                      rhs=hl[64:65, 512:1024], start=False, stop=True)),
                     r=["ones_b", "hl_lo"], w=[f"ps{4 + hg}"])
                P.op("dve", (lambda e, ob=ob, hg=hg: e.tensor_tensor(
                    out=oT_r[0:64, hg * 4:(hg + 1) * 4, q0:q0 + 128],
                    in0=ob[0:64, :].rearrange("p (a b) -> p a b", a=4),
                    in1=psum[4 + hg][0:64, :].rearrange("p (a b) -> p a b", a=4), op=ALU.mult)),
                    r=[f"osb{oi}", f"ps{4 + hg}"], w=["oT_r"])

        indexer_scores(0)
        for qb in range(5):
            if qb > 1:
                attention_fin(qb - 2)
            bisect_block(qb)
            if qb + 1 < 5:
                indexer_scores(qb + 1)
            if qb > 0:
                attention_block(qb - 1)
        attention_fin(3)
        attention_block(4)
        attention_fin(4)
        P.dma("pool", ot_dram[r], oT_r.rearrange("p a b -> p (a b)"), r=["oT_r"], w=["ot_dram"], slot="otst")

    barrier()
    A.release()
    A.mark()
    Wu = A.alloc([128, 8, 512], BF16)
    Wg = A.alloc([128, 8, 2048], BF16)
    Wpp = A.alloc([128, 4, 1024], BF16)
    Wap = A.alloc([64, 8, 1024], BF16)
    Wo = A.alloc([128, 8, 1024], BF16)
    poolw = A.alloc([128, 4, 128], BF16)
    pscale = A.alloc([128, 4], F32)
    pm_std = A.alloc([128, 8, 128], BF16)
    pm_first = A.alloc([128, NR * 12, 128], BF16)
    xr = [A.alloc([128, D], F32) for _ in range(5)]
    x0 = A.off - 5 * 4096
    SX = Arena(arena_t, A.off)
    SX.off = x0
    wst3 = [SX.alloc([128, 2048], F32) for _ in range(2)]
    xnh3 = [A.alloc([128, D], BF16) for _ in range(3)]
    ntmp3 = [A.alloc([128, 4], F32) for _ in range(3)]
    junk3 = A.alloc([128, D], BF16)
    xnT3b = A.alloc([128, 8, RT], BF16)
    u_sb = A.alloc([128, 5, 512], BF16)
    pooledT = A.alloc([128, 4, RT], BF16)
    mixedT = A.alloc([128, 4, RT], BF16)
    oT3b = A.alloc([64, 8, RT], BF16)
    mergedT = A.alloc([128, 8, RT], BF16)
    sg0 = A.alloc([128, 512], F32)
    sg1 = A.alloc([128, 512], F32)
    tt0 = A.alloc([128, 512], F32)
    tt1 = A.alloc([128, 512], F32)
    hb = [A.alloc([128, D], F32) for _ in range(2)]

    P.dma("sp", pscale, pscale_d, w=["pscale", "cchain"], slot="c")
    load_weight(w_u_d, Wu.rearrange("p a b -> p (a b)"), 8 * 512, 128, wst3, "Wu")
    load_weight(w_g_d, Wg.rearrange("p a b -> p (a b)"), 8 * 2048, 128, wst3, "Wg")
    load_weight(w_pp_d, Wpp.rearrange("p a b -> p (a b)"), 4 * 1024, 128, wst3, "Wpp")
    load_weight(w_ap_d, Wap.rearrange("p a b -> p (a b)"), 8 * 1024, 64, wst3, "Wap")
    load_weight(w_out_d, Wo.rearrange("p a b -> p (a b)"), 8 * 1024, 128, wst3, "Wo")
    load_weight(poolw_d, poolw.rearrange("p a b -> p (a b)"), 4 * 128, 128, wst3, "poolw")
    load_weight(pm_std_d, pm_std.rearrange("p a b -> p (a b)"), 8 * 128, 128, wst3, "pm_std")
    load_weight(pm_first_d, pm_first.rearrange("p a b -> p (a b)"), NR * 12 * 128, 128, wst3, "pm_first")
    barrier()

    slabs = [(0, 512), (512, 128)]
    for r in range(NR):
        P.dma("sp", oT3b.rearrange("p a b -> p (a b)"), ot_dram[r], r=["ot_dram"], w=["oT3"], slot="otl")
        def p3_A1(t, r=r):
            s3 = t % 3
            P.dma("sp", xr[t], xo[r, t * 128:(t + 1) * 128, :], w=[f"xr{t}"], slot=f"xr{t}")
            norm_tile(xr[t], f"xr{t}", gmix, xnh3[s3], f"xnh3{s3}", ntmp3[s3], f"nt{s3}", junk3, "junk3")

        def p3_A2(t):
            s3 = t % 3
            transpose_to(xnh3[s3], f"xnh3{s3}", 8, t % 2, xnT3b[:, :, t * 128:(t + 1) * 128], f"xnT3_{t}")

        def p3_B(t, r=r):
            proj_tok(xnT3b, f"xnT3_{t}", t * 128, Wu, "Wu", 0, 512, 2)
            P.op("act", (lambda e, t=t: e.activation(out=u_sb[:, t, :], in_=psum[2][:, 0:512], func=AF.Copy)),
                 r=["ps2"], w=[f"u{t}"])
            for g in range(4):
                mats = []
                if t == 1:
                    mats.append((t, pm_first[:, r * 12 + g, :], "pm_first"))
                    mats.append((t, pm_first[:, r * 12 + 4 + g, :], "pm_first"))
                    mats.append((t - 1, pm_first[:, r * 12 + 8 + g, :], "pm_first"))
                else:
                    mats.append((t, pm_std[:, g, :], "pm_std"))
                    if t > 0:
                        mats.append((t - 1, pm_std[:, 4 + g, :], "pm_std"))
                for mi, (tu, M, mk) in enumerate(mats):
                    P.op("pe", (lambda e, tu=tu, M=M, g=g, mi=mi, nm=len(mats): e.matmul(
                        out=psum[3][:, g * 128:(g + 1) * 128], lhsT=u_sb[:, tu, g * 128:(g + 1) * 128], rhs=M,
                        start=(mi == 0), stop=(mi == nm - 1))),
                        r=[f"u{tu}", mk], w=["ps3"])
            P.op("act", (lambda e, t=t: e.activation(out=pooledT[:, :, t * 128:(t + 1) * 128],
                                                     in_=psum[3].rearrange("p (a b) -> p a b", a=4),
                                                     func=AF.Copy)), r=["ps3"], w=["pooledT"])

        p3_A1(0)
        p3_A1(1)
        p3_A1(2)
        p3_A2(0)
        p3_A2(1)
        for t in range(5):
            if t + 3 < 5:
                p3_A1(t + 3)
            if t + 2 < 5:
                p3_A2(t + 2)
            p3_B(t)
        for g in range(4):
            for (c0, n) in slabs:
                P.op("pe", (lambda e, g=g, c0=c0, n=n: e.matmul(out=psum[2][:, 0:n], lhsT=poolw[:, g, :],
                                                               rhs=pooledT[:, g, c0:c0 + n], start=True, stop=True)),
                     r=["poolw", "pooledT"], w=["ps2"])
                P.op("dve", (lambda e, g=g, c0=c0, n=n: e.tensor_scalar(out=mixedT[:, g, c0:c0 + n],
                                                                       in0=psum[2][:, 0:n],
                                                                       scalar1=pscale[:, g:g + 1], scalar2=None,
                                                                       op0=ALU.mult)),
                     r=["ps2", "pscale"], w=["mixedT"])
        for c in range(8):
            for (c0, n) in slabs:
                for g in range(4):
                    P.op("pe", (lambda e, g=g, c=c, c0=c0, n=n: e.matmul(
                        out=psum[4][:, 0:n], lhsT=Wpp[:, g, c * 128:(c + 1) * 128], rhs=mixedT[:, g, c0:c0 + n],
                        start=(g == 0), stop=(g == 3))), r=["Wpp", "mixedT"], w=["ps4"])
                for h in range(8):
                    P.op("pe", (lambda e, h=h, c=c, c0=c0, n=n: e.matmul(
                        out=psum[5][:, 0:n], lhsT=Wap[0:64, h, c * 128:(c + 1) * 128], rhs=oT3b[0:64, h, c0:c0 + n],
                        start=(h == 0), stop=(h == 7))), r=["Wap", "oT3"], w=["ps5"])
                for gi in range(2):
                    for kc in range(8):
                        P.op("pe", (lambda e, gi=gi, kc=kc, c=c, c0=c0, n=n: e.matmul(
                            out=psum[6 + gi][:, 0:n],
                            lhsT=Wg[:, kc, gi * 1024 + c * 128:gi * 1024 + (c + 1) * 128],
                            rhs=xnT3b[:, kc, c0:c0 + n], start=(kc == 0), stop=(kc == 7))),
                            r=["Wg"] + [f"xnT3_{tt}" for tt in range(5)], w=[f"ps{6 + gi}"])
                P.op("act", (lambda e, n=n: e.activation(out=sg0[:, 0:n], in_=psum[6][:, 0:n], func=AF.Sigmoid)),
                     r=["ps6"], w=["sg0"])
                P.op("act", (lambda e, n=n: e.activation(out=sg1[:, 0:n], in_=psum[7][:, 0:n], func=AF.Sigmoid)),
                     r=["ps7"], w=["sg1"])
                P.op("dve", (lambda e, n=n: e.tensor_tensor(out=tt0[:, 0:n], in0=sg0[:, 0:n], in1=psum[4][:, 0:n],
                                                            op=ALU.mult)), r=["sg0", "ps4"], w=["tt0"])
                P.op("dve", (lambda e, n=n: e.tensor_tensor(out=tt1[:, 0:n], in0=sg1[:, 0:n], in1=psum[5][:, 0:n],
                                                            op=ALU.mult)), r=["sg1", "ps5"], w=["tt1"])
                P.op("dve", (lambda e, c=c, c0=c0, n=n: e.tensor_tensor(out=mergedT[:, c, c0:c0 + n], in0=tt0[:, 0:n],
                                                                       in1=tt1[:, 0:n], op=ALU.add)),
                     r=["tt0", "tt1"], w=["mergedT"])
        for t in range(5):
            hs = t % 2
            for half in range(2):
                bank = 2 + half
                for c in range(8):
                    P.op("pe", (lambda e, c=c, t=t, half=half, bank=bank: e.matmul(
                        out=psum[bank][:, 0:512], lhsT=mergedT[:, c, t * 128:(t + 1) * 128],
                        rhs=Wo[:, c, half * 512:(half + 1) * 512], start=(c == 0), stop=(c == 7))),
                        r=["mergedT", "Wo"], w=[f"ps{bank}"])
                P.op("dve", (lambda e, t=t, half=half, bank=bank, hs=hs: e.tensor_tensor(
                    out=hb[hs][:, half * 512:(half + 1) * 512], in0=xr[t][:, half * 512:(half + 1) * 512],
                    in1=psum[bank][:, 0:512], op=ALU.add)), r=[f"xr{t}", f"ps{bank}"], w=[f"hb{hs}"])
            P.dma("pool", h_dram[r, t * 128:(t + 1) * 128, :], hb[hs], r=[f"hb{hs}"], w=["h_dram"], slot=f"hst{hs}")

    barrier()
    A.release()
    A.mark()
    Wup = A.alloc([128, 8, 2 * DFF], BF16)
    Wdn = A.alloc([128, NF, 1024], BF16)
    convw = A.alloc([128, 44, 3], F32)
    convb = A.alloc([128, 44], F32)
    g2 = A.alloc([128, 2 * D], F32)
    ht = [None] + [A.alloc([128, D], F32) for _ in range(4)]
    hnT = A.alloc([128, 8, RT], BF16)
    NFH = NF // 2
    actT = A.alloc([128, NFH, 512], BF16)
    ntmp4 = [A.alloc([128, 4], F32) for _ in range(2)]
    upsb = [A.alloc([128, 516], F32) for _ in range(2)]
    cc2 = A.alloc([128, D], F32)
    cc = [cc2[:, 0:512], cc2[:, 512:1024]]
    ht[0] = cc2
    silu_o = A.alloc([128, 512], F32)
    junk4 = silu_o.bitcast(BF16)
    xnh4 = [A.alloc([128, D], BF16) for _ in range(1)]
    h0 = A.off
    wflat = arena_t
    SW = Arena(arena_t, CAP)
    wst4 = [hnT.rearrange("p a b -> p (a b)")[:, 0:4096].bitcast(F32),
           actT.rearrange("p a b -> p (a b)")[:, 0:4096].bitcast(F32)]

    P.dma("sp", convw.rearrange("p a b -> p (a b)"), convw_d, w=["convw", "cchain"], slot="c")
    P.dma("sp", convb, convb_d, w=["convb", "cchain"], slot="c")
    P.dma("sp", g2, g_d[:, D:3 * D], w=["g2", "cchain"], slot="c")
    load_weight(w_up_d, Wup.rearrange("p a b -> p (a b)"), 8 * 2 * DFF, 128, wst4, "W3b")
    load_weight(w_dn_d, Wdn.rearrange("p a b -> p (a b)"), NF * 1024, 128, wst4, "W3b")
    barrier()

    for r in range(NR):
        for t in range(5):
            hk = [f"ht{t}"] if t > 0 else ["cc0", "cc1"]
            P.dma("sp", ht[t], h_dram[r, t * 128:(t + 1) * 128, :], r=["h_dram"], w=hk, slot=f"ht{t}")
            s = t % 2
            P.op("act", (lambda e, t=t, s=s: e.activation(out=junk4, in_=ht[t], func=AF.Square,
                                                         accum_out=ntmp4[s][:, 0:1])),
                 r=hk, w=["silu_o", f"nt{s}0"])
            P.op("dve", (lambda e, s=s: e.tensor_scalar(out=ntmp4[s][:, 1:2], in0=ntmp4[s][:, 0:1], scalar1=1.0 / D,
                                                        scalar2=EPS, op0=ALU.mult, op1=ALU.add)),
                 r=[f"nt{s}0"], w=[f"nt{s}1"])
            P.op("pool", (lambda e, s=s: e.tensor_tensor(out=ntmp4[s][:, 2:3], in0=ntmp4[s][:, 1:2], in1=cneg[:, 0:1],
                                                         op=ALU.pow)), r=[f"nt{s}1", "cneg"], w=[f"nt{s}2"])
            P.op("dve", (lambda e, t=t, s=s: e.scalar_tensor_tensor(
                out=xnh4[0], in0=ht[t], scalar=ntmp4[s][:, 2:3], in1=g2[:, 0:D], op0=ALU.mult, op1=ALU.mult)),
                r=hk + [f"nt{s}2", "g2"], w=["xnh40"])
            transpose_to(xnh4[0], "xnh40", 8, s, hnT[:, :, t * 128:(t + 1) * 128], "hnT")
        for fh in range(2):
            for fi in range(NFH):
                f = fh * NFH + fi
                for ab in range(2):
                    col = ab * DFF + f * 128
                    ch = ab * NF + f
                    bank = 2 + ab
                    hbank = 4 + ab
                    for kc in range(8):
                        P.op("pe", (lambda e, kc=kc, col=col, bank=bank: e.matmul(
                            out=psum[bank][:, 0:512], lhsT=Wup[:, kc, col:col + 128], rhs=hnT[:, kc, 128:640],
                            start=(kc == 0), stop=(kc == 7))), r=["W3b", "hnT"], w=[f"ps{bank}"])
                    for kc in range(8):
                        P.op("pe", (lambda e, kc=kc, col=col, hbank=hbank: e.matmul(
                            out=psum[hbank][:, 0:2], lhsT=Wup[:, kc, col:col + 128], rhs=hnT[:, kc, 126:128],
                            start=(kc == 0), stop=(kc == 7))), r=["W3b", "hnT"], w=[f"ps{hbank}"])
                    ub = upsb[ab]
                    cb = cc[ab]
                    P.op("act", (lambda e, ub=ub, bank=bank: e.activation(out=ub[:, 2:514], in_=psum[bank][:, 0:512],
                                                                          func=AF.Copy)),
                         r=[f"ps{bank}"], w=[f"upsb{ab}"])
                    P.op("act", (lambda e, ub=ub, hbank=hbank, r=r: e.activation(
                        out=ub[:, 0:2], in_=psum[hbank][:, 0:2], func=AF.Identity, scale=hsc[:, r:r + 1])),
                        r=[f"ps{hbank}", "hsc"], w=[f"upsb{ab}"])
                    P.op("act", (lambda e, cb=cb, bank=bank, ch=ch: e.activation(
                        out=cb, in_=psum[bank][:, 0:512], func=AF.Identity, scale=convw[:, ch, 2:3],
                        bias=convb[:, ch:ch + 1])), r=[f"ps{bank}", "convw", "convb"], w=[f"cc{ab}"])
                    P.op("dve", (lambda e, cb=cb, ub=ub, ch=ch: e.scalar_tensor_tensor(
                        out=cb, in0=ub[:, 1:513], scalar=convw[:, ch, 1:2], in1=cb, op0=ALU.mult, op1=ALU.add)),
                        r=[f"upsb{ab}", f"cc{ab}", "convw"], w=[f"cc{ab}"])
                    P.op("dve", (lambda e, cb=cb, ub=ub, ch=ch: e.scalar_tensor_tensor(
                        out=cb, in0=ub[:, 0:512], scalar=convw[:, ch, 0:1], in1=cb, op0=ALU.mult, op1=ALU.add)),
                        r=[f"upsb{ab}", f"cc{ab}", "convw"], w=[f"cc{ab}"])
                P.op("act", lambda e: e.activation(out=silu_o, in_=cc[0], func=AF.Silu), r=["cc0"], w=["silu_o"])
                P.op("dve", (lambda e, fi=fi: e.tensor_tensor(out=actT[:, fi, :], in0=silu_o, in1=cc[1], op=ALU.mult)),
                     r=["silu_o", "cc1"], w=["actT"])
            for t in range(4):
                for half in range(2):
                    bank = 6 + half
                    for fi in range(NFH):
                        f = fh * NFH + fi
                        P.op("pe", (lambda e, f=f, fi=fi, t=t, half=half, bank=bank: e.matmul(
                            out=psum[bank][:, 0:512], lhsT=actT[:, fi, t * 128:(t + 1) * 128],
                            rhs=Wdn[:, f, half * 512:(half + 1) * 512], start=(fi == 0), stop=(fi == NFH - 1))),
                            r=["actT", "W3b"], w=[f"ps{bank}"])
                    P.op("dve", (lambda e, t=t, half=half, bank=bank: e.tensor_tensor(
                        out=ht[t + 1][:, half * 512:(half + 1) * 512], in0=ht[t + 1][:, half * 512:(half + 1) * 512],
                        in1=psum[bank][:, 0:512], op=ALU.add)), r=[f"ht{t + 1}", f"ps{bank}"], w=[f"ht{t + 1}"])
        for t in range(4):
            s = t % 2
            P.op("act", (lambda e, t=t, s=s: e.activation(out=junk4, in_=ht[t + 1], func=AF.Square,
                                                         accum_out=ntmp4[s][:, 0:1])),
                 r=[f"ht{t + 1}"], w=["silu_o", f"nt{s}0"])
            P.op("dve", (lambda e, s=s: e.tensor_scalar(out=ntmp4[s][:, 1:2], in0=ntmp4[s][:, 0:1], scalar1=1.0 / D,
                                                        scalar2=EPS, op0=ALU.mult, op1=ALU.add)),
                 r=[f"nt{s}0"], w=[f"nt{s}1"])
            P.op("pool", (lambda e, s=s: e.tensor_tensor(out=ntmp4[s][:, 2:3], in0=ntmp4[s][:, 1:2], in1=cneg[:, 0:1],
                                                         op=ALU.pow)), r=[f"nt{s}1", "cneg"], w=[f"nt{s}2"])
            P.op("dve", (lambda e, t=t, s=s: e.scalar_tensor_tensor(
                out=ht[t + 1], in0=ht[t + 1], scalar=ntmp4[s][:, 2:3], in1=g2[:, D:2 * D], op0=ALU.mult,
                op1=ALU.mult)), r=[f"ht{t + 1}", f"nt{s}2", "g2"], w=[f"ht{t + 1}"])
            P.dma("pool", out_d[r, t * 128:(t + 1) * 128, :], ht[t + 1], r=[f"ht{t + 1}"], w=["out"],
                  slot=f"ost{t % 2}")
    P.op("pool", None, r=["out"])
    P.emit(st)
    st.close()
    return nc


def _chunks(j):
    return [j, 7 - j, 8 + j, 15 - j]


def _rope_tables(pos):
    half = 8
    inv = 1.0 / (500000.0 ** (np.arange(half, dtype=np.float32) * 2.0 / 16.0))
    ang = pos.astype(np.float32)[:, None] * inv[None, :].astype(np.float32)
    return np.concatenate([np.cos(ang), np.sin(ang)], axis=1).astype(np.float32)


def _pool_mats(first):
    Am = np.zeros((4, 128, 128), np.float64)
    Bm = np.zeros((4, 128, 128), np.float64)
    for g, w in enumerate((2, 4, 8, 16)):
        for t in range(128):
            cnt = min(t + 1, w) if first else w
            for tp in range(t - w + 1, t + 1):
                if tp >= 0:
                    Am[g, tp, t] += 1.0 / cnt
                elif not first:
                    Bm[g, tp + 128, t] += 1.0 / cnt
            Am[g, t, t] -= 1.0
    return Am, Bm


def _pk(a, kc):
    n = a.shape[1]
    return np.ascontiguousarray(a.reshape(kc, 128, n).transpose(1, 0, 2).reshape(128, kc * n))


_NC_CACHE = {}


def kernel(x, norm_mix_g, w_in, pool_w, pool_scale, w_pool_proj, w_attn_proj, w_out,
           norm_ffn_g, w_up, conv_w, conv_b, w_down, norm_final_g):
    f = lambda a: np.asarray(a, dtype=np.float32)
    x = f(x)
    w_in = f(w_in)[0]
    cuts = np.cumsum([512, 512, 512, 512, 512, 64, 8, 2048])
    Wu_, Wq_, Wk_, Wv_, Wiq_, Wik_, Wiw_, Wg_ = np.split(w_in, cuts[:-1], axis=1)
    shared = {}
    shared["w_kvi"] = _pk(np.concatenate([Wk_, Wv_, Wik_, Wik_], axis=1), 8)
    shared["w_qi"] = _pk(np.concatenate([Wq_, Wiq_, Wiw_], axis=1), 8)
    shared["w_u"] = _pk(Wu_, 8)
    shared["w_g"] = _pk(Wg_, 8)
    shared["pool_w"] = np.ascontiguousarray(f(pool_w)[0].transpose(1, 0, 2).reshape(128, 512))
    shared["pool_scale"] = np.ascontiguousarray(f(pool_scale)[0].reshape(4, 128).T)
    shared["w_pp"] = _pk(f(w_pool_proj)[0], 4)
    shared["w_ap"] = np.ascontiguousarray(f(w_attn_proj)[0].reshape(8, 64, 1024).transpose(1, 0, 2).reshape(64, 8192))
    shared["w_out"] = _pk(f(w_out)[0], 8)
    shared["w_up"] = _pk(f(w_up)[0], 8)
    shared["w_down"] = _pk(f(w_down)[0], NF)
    shared["conv_w"] = np.ascontiguousarray(f(conv_w)[0].reshape(3, 44, 128).transpose(2, 1, 0).reshape(128, 132))
    shared["conv_b"] = np.ascontiguousarray(f(conv_b)[0].reshape(44, 128).T)
    gains = np.concatenate([f(norm_mix_g)[0], f(norm_ffn_g)[0], f(norm_final_g)])
    shared["gains"] = np.ascontiguousarray(np.broadcast_to(gains[None, :], (128, 3 * D)))
    shared["iota"] = np.ascontiguousarray(np.broadcast_to(np.arange(512, dtype=np.float32)[None, :], (128, 512)))
    shared["pow2"] = np.ascontiguousarray(np.broadcast_to(
        (0.5 ** np.arange(1, NBIS + 1)).astype(np.float32)[None, :], (128, NBIS)))
    shared["ident"] = np.eye(128, dtype=np.float32)
    shared["cs_ctx"] = _rope_tables(np.arange(S))
    A_std, B_std = _pool_mats(False)
    A_fst, B_fst = _pool_mats(True)
    shared["pm_std"] = np.ascontiguousarray(
        np.concatenate([A_std, B_std], 0).transpose(1, 0, 2).reshape(128, 8 * 128)).astype(np.float32)

    def hi_lo(a):
        hi = a.astype(np.float32).astype(ml_dtypes.bfloat16).astype(np.float64)
        return hi, a - hi

    in_maps = []
    for c in range(8):
        b, j = c // 4, c % 4
        m = dict(shared)
        m["xc"] = np.ascontiguousarray(x[b])
        xo = np.zeros((NR, RT, D), np.float32)
        qp = np.zeros((NR, RT), np.float32)
        hs = np.zeros((NR,), np.float32)
        pmf = np.zeros((NR, 12, 128, 128), np.float64)
        for r, i in enumerate(_chunks(j)):
            t0 = 512 * i
            xo[r, 128:] = x[b, t0:t0 + 512]
            qp[r, 128:] = np.arange(t0, t0 + 512)
            if i > 0:
                xo[r, :128] = x[b, t0 - 128:t0]
                qp[r, :128] = np.arange(t0 - 128, t0)
                hs[r] = 1.0
                pmf[r, 0:4], pmf[r, 4:8], pmf[r, 8:12] = A_std, 0.0, B_std
            else:
                qp[r, :128] = np.arange(0, 128)
                hi, lo = hi_lo(A_fst)
                pmf[r, 0:4], pmf[r, 4:8], pmf[r, 8:12] = hi, lo, 0.0
        m["xo"] = xo
        m["qpos"] = np.ascontiguousarray(qp.reshape(NR * 5, 128).T)
        m["hscale"] = np.ascontiguousarray(np.broadcast_to(hs[None, :], (128, NR)))
        m["cs_own"] = _rope_tables(qp.reshape(-1)).reshape(NR, RT, 16)
        m["pm_first"] = np.ascontiguousarray(
            pmf.reshape(NR * 12, 128, 128).transpose(1, 0, 2).reshape(128, NR * 12 * 128)).astype(np.float32)
        in_maps.append(m)

    if "nc" not in _NC_CACHE:
        _NC_CACHE["nc"] = build_program()
    nc = _NC_CACHE["nc"]
    res = run_bass_kernel_spmd(nc, in_maps, core_ids=list(range(8)))
    _NC_CACHE['res'] = res if DEBUG else None
    out = np.zeros((2, S, D), np.float32)
    for c in range(8):
        b, j = c // 4, c % 4
        o = res.results[c]["out"]
        for r, i in enumerate(_chunks(j)):
            out[b, 512 * i:512 * i + 512] = o[r]
    return out
```

```python
import numpy as np
from contextlib import ExitStack
import ml_dtypes
import concourse.bass as bass
import concourse.mybir as mybir
from concourse.bass_utils import run_bass_kernel_spmd

F32 = mybir.dt.float32
BF16 = mybir.dt.bfloat16
U8 = mybir.dt.uint8
ALU = mybir.AluOpType
AF = mybir.ActivationFunctionType
AX = mybir.AxisListType

D = 1024
S = 8192
NT = 64
NR = 4
RT = 640
LR = [4, 8, 12, 16]
NBIS = 16
DFF = 2816
NF = 22
EPS = 1e-6
CS_IDX = (64 ** -0.5) * (8 ** -0.5)
NEG = -1.0e30


class Prog:
    ENGS = ("pe", "act", "dve", "pool", "sp")

    def __init__(self, nc):
        self.nc = nc
        self.ops = []

    def op(self, eng, fn, r=(), w=(), slot=None, barrier=False):
        self.ops.append(dict(eng=eng, fn=fn, r=tuple(r), w=tuple(w), slot=slot, barrier=barrier))

    def dma(self, eng, out, in_, r=(), w=(), slot=None, **kw):
        assert slot is not None
        self.op(eng, lambda e: e.dma_start(out=out, in_=in_, **kw), r, w, slot)

    def emit(self, stack):
        nc = self.nc
        ops = self.ops
        last_w = {}
        readers = {}
        deps = [None] * len(ops)
        needs_inc = [False] * len(ops)
        last_eng = {}
        last_slot = {}
        last_bar = None
        for i, o in enumerate(ops):
            d = set()
            if o["barrier"]:
                d.update(last_eng.values())
                d.update(last_slot.values())
            else:
                for k in o["r"]:
                    if k in last_w:
                        d.add(last_w[k])
                for k in o["w"]:
                    if k in last_w:
                        d.add(last_w[k])
                    for rr in readers.get(k, ()):
                        d.add(rr)
            if last_bar is not None:
                d.add(last_bar)
            d.discard(i)
            d = {p for p in d if not (ops[p]["slot"] is None and ops[p]["eng"] == "pe" and o["eng"] == "pe"
                                      and not ops[p]["barrier"])}
            deps[i] = d
            for p in d:
                needs_inc[p] = True
            for k in o["r"]:
                readers.setdefault(k, []).append(i)
            for k in o["w"]:
                last_w[k] = i
                readers[k] = []
            if o["slot"] is not None:
                last_slot[o["slot"]] = i
            elif o["fn"] is not None:
                last_eng[o["eng"]] = i
            if o["barrier"]:
                last_bar = i
        self._deps = deps
        if CHECK_DEADLOCK:
            done = [False] * len(ops)
            pe_ = {e: [i for i, o in enumerate(ops) if o["eng"] == e] for e in self.ENGS}
            ptr = {e: 0 for e in self.ENGS}
            progress = True
            while progress:
                progress = False
                for e in self.ENGS:
                    while ptr[e] < len(pe_[e]):
                        i = pe_[e][ptr[e]]
                        if all(done[p] for p in deps[i]):
                            done[i] = True
                            ptr[e] += 1
                            progress = True
                        else:
                            break
            stuck = {e: pe_[e][ptr[e]] for e in self.ENGS if ptr[e] < len(pe_[e])}
            print("DEADLOCK CHECK: stuck =", stuck)
            for e, i in stuck.items():
                print(e, i, ops[i]["r"], ops[i]["w"], [(p, ops[p]["eng"], done[p]) for p in deps[i] if not done[p]])
        sems = {}

        def get_sem(name):
            if name not in sems:
                sems[name] = stack.enter_context(nc.semaphore(name))
            return sems[name]

        cnt = {}
        val = [None] * len(ops)
        for i, o in enumerate(ops):
            if o["slot"] is not None:
                key = "d_" + o["slot"]
                cnt[key] = cnt.get(key, 0) + 16
                val[i] = (key, cnt[key])
            elif needs_inc[i]:
                key = "e_" + o["eng"]
                cnt[key] = cnt.get(key, 0) + 1
                val[i] = (key, cnt[key])
        per_eng = {e: [] for e in self.ENGS}
        for i, o in enumerate(ops):
            per_eng[o["eng"]].append(i)
        block = stack.enter_context(nc.Block())

        def make(engname):
            idxs = per_eng[engname]

            def body(eng):
                waited = {}
                for i in idxs:
                    o = ops[i]
                    need = {}
                    for p in deps[i]:
                        k, v = val[p]
                        if v > need.get(k, 0):
                            need[k] = v
                    for k, v in need.items():
                        if waited.get(k, 0) >= v:
                            continue
                        eng.wait_ge(get_sem(k), v)
                        waited[k] = v
                    if o["fn"] is None:
                        continue
                    ins = o["fn"](eng)
                    if val[i] is not None:
                        k, v = val[i]
                        ins.then_inc(get_sem(k), 16 if o["slot"] is not None else 1)
            return body

        if per_eng["sp"]:
            block.sync(make("sp"))
        if per_eng["act"]:
            block.scalar(make("act"))
        if per_eng["dve"]:
            block.vector(make("dve"))
        if per_eng["pool"]:
            block.gpsimd(make("pool"))
        if per_eng["pe"]:
            block.tensor(make("pe"))


class Arena:
    def __init__(self, base, cap):
        self.base = base
        self.cap = cap
        self.off = 0
        self.marks = []

    def alloc(self, shape, dt):
        esz = 2 if dt == BF16 else 4
        n = int(np.prod(shape[1:])) * esz
        n_al = (n + 63) // 64 * 64
        assert self.off + n_al <= self.cap, f"arena overflow {self.off}+{n_al}>{self.cap}"
        ap = self.base[0:shape[0], self.off:self.off + n].bitcast(dt)
        if len(shape) == 3:
            ap = ap.rearrange("p (a b) -> p a b", a=shape[1])
        self.off += n_al
        return ap

    def mark(self):
        self.marks.append(self.off)

    def release(self):
        self.off = self.marks.pop()


DEBUG = False
PIPE_IDX = True
SKEW1 = True
SKEW2 = True
CHECK_DEADLOCK = False
MERGE_ATT = True
SPLIT_PAIR = False


def build_program():
    nc = bass.Bass("TRN2", target_bir_lowering=False)
    skind = "ExternalOutput" if DEBUG else "Internal"

    def din(name, shape, dt=F32):
        return nc.dram_tensor(name, list(shape), dt, kind="ExternalInput").ap()

    xc = din("xc", [S, D])
    xo = din("xo", [NR, RT, D])
    cs_ctx = din("cs_ctx", [S, 16])
    cs_own = din("cs_own", [NR, RT, 16])
    qpos_d = din("qpos", [128, NR * 5])
    hsc_d = din("hscale", [128, NR])
    iota_d = din("iota", [128, 512])
    pow2_d = din("pow2", [128, NBIS])
    ident_d = din("ident", [128, 128])
    g_d = din("gains", [128, 3 * D])
    w_kvi_d = din("w_kvi", [128, 8 * 1152])
    w_qi_d = din("w_qi", [128, 8 * 1032])
    w_u_d = din("w_u", [128, 8 * 512])
    w_g_d = din("w_g", [128, 8 * 2048])
    poolw_d = din("pool_w", [128, 4 * 128])
    pscale_d = din("pool_scale", [128, 4])
    w_pp_d = din("w_pp", [128, 4 * 1024])
    w_ap_d = din("w_ap", [64, 8 * 1024])
    w_out_d = din("w_out", [128, 8 * 1024])
    w_up_d = din("w_up", [128, 8 * 2 * DFF])
    w_dn_d = din("w_down", [128, NF * 1024])
    convw_d = din("conv_w", [128, 44 * 3])
    convb_d = din("conv_b", [128, 44])
    pm_std_d = din("pm_std", [128, 8 * 128])
    pm_first_d = din("pm_first", [128, NR * 12 * 128])
    out_d = nc.dram_tensor("out", [NR, 512, D], F32, kind="ExternalOutput").ap()
    kt_dram = nc.dram_tensor("kt_scr", [NT, 128, 512], BF16, kind=skind).ap()
    v_dram = nc.dram_tensor("v_scr", [NT, 128, 520], BF16, kind=skind).ap()
    ot_dram = nc.dram_tensor("ot_scr", [NR, 64, 8 * RT], BF16, kind=skind).ap()
    h_dram = nc.dram_tensor("h_scr", [NR, RT, D], F32, kind=skind).ap()

    P = Prog(nc)
    st = ExitStack()
    CAP = 207 * 1024 + 768
    arena_t = st.enter_context(nc.sbuf_tensor("arena", [128, CAP], U8))
    A = Arena(arena_t, CAP)
    psbig = st.enter_context(nc.psum_tensor("psbig", [128, 4096], F32))
    psum = [psbig[:, i * 512:(i + 1) * 512] for i in range(8)]
    psb = [p.bitcast(BF16) for p in psum]

    ident_f = A.alloc([128, 128], F32)
    ident_b = A.alloc([128, 128], BF16)
    ident4 = A.alloc([128, 512], BF16)
    gmix = A.alloc([128, D], F32)
    qpos = A.alloc([128, NR * 5], F32)
    hsc = A.alloc([128, NR], F32)
    iota = A.alloc([128, 512], F32)
    pow2 = A.alloc([128, NBIS], F32)
    ones_b = A.alloc([128, 64], BF16)
    hl = A.alloc([128, 1024], BF16)
    cneg = A.alloc([128, 2], F32)
    P.dma("sp", ident_f, ident_d, w=["ident_f", "cchain"], slot="c")
    P.dma("sp", gmix, g_d[:, 0:D], w=["gains", "cchain"], slot="c")
    P.dma("sp", qpos, qpos_d, w=["qpos", "cchain"], slot="c")
    P.dma("sp", hsc, hsc_d, w=["hsc", "cchain"], slot="c")
    P.dma("sp", iota, iota_d, w=["iota", "cchain"], slot="c")
    P.dma("sp", pow2, pow2_d, w=["pow2", "cchain"], slot="c")
    P.op("dve", lambda e: e.tensor_copy(out=ident_b, in_=ident_f), r=["ident_f"], w=["ident_b"])
    P.op("pool", lambda e: e.tensor_copy(out=ident4.rearrange("p (a b) -> p a b", a=4),
                                         in_=ident_f[:, None, :].to_broadcast([128, 4, 128])),
         r=["ident_f"], w=["ident4"])
    P.op("pool", lambda e: e.memset(ones_b, 1.0), w=["ones_b"])
    P.op("pool", lambda e: e.memset(cneg[:, 0:1], -0.5), w=["cneg"])
    A.mark()

    wst_i = [0]

    def load_weight(dram2d, dst2d, ncols, nparts, wst, key):
        c0 = 0
        while c0 < ncols:
            n = min(2048, ncols - c0)
            s = wst_i[0] % 2
            wst_i[0] += 1
            stg = wst[s]
            P.dma("sp", stg[0:nparts, 0:n], dram2d[0:nparts, c0:c0 + n], w=[f"wst{s}"], slot=f"wst{s}")
            eng = "pool" if s == 0 else "dve"
            P.op(eng, (lambda e, a=dst2d[0:nparts, c0:c0 + n], b=stg[0:nparts, 0:n]: e.tensor_copy(out=a, in_=b)),
                 r=[f"wst{s}"], w=[key])
            c0 += n

    def barrier():
        dummy = cneg[:, 1:2]
        P.op("pool", lambda e: e.memset(dummy, 0.0), barrier=True)

    def norm_tile(x_ap, xkey, gain, out_bf, okey, tmp, tkey, junk, jkey):
        P.op("act", lambda e: e.activation(out=junk, in_=x_ap, func=AF.Square, accum_out=tmp[:, 0:1]),
             r=[xkey], w=[jkey, tkey + "0"])
        P.op("dve", lambda e: e.tensor_scalar(out=tmp[:, 1:2], in0=tmp[:, 0:1], scalar1=1.0 / D, scalar2=EPS,
                                              op0=ALU.mult, op1=ALU.add), r=[tkey + "0"], w=[tkey + "1"])
        P.op("pool", lambda e: e.tensor_tensor(out=tmp[:, 2:3], in0=tmp[:, 1:2], in1=cneg[:, 0:1], op=ALU.pow),
             r=[tkey + "1", "cneg"], w=[tkey + "2"])
        P.op("dve", lambda e: e.scalar_tensor_tensor(out=out_bf, in0=x_ap, scalar=tmp[:, 2:3],
                                                     in1=gain, op0=ALU.mult, op1=ALU.mult),
             r=[xkey, tkey + "2", "gains"], w=[okey])

    def transpose_to(src_bf, skey, ncol_blocks, ps_i, dst3, dkey, eng="act"):
        if isinstance(ps_i, tuple):
            pb, pkey = ps_i
        else:
            pb, pkey = psb[ps_i], f"ps{ps_i}"
        for kc in range(ncol_blocks):
            P.op("pe", (lambda e, kc=kc: e.transpose(out=pb[:, kc * 128:(kc + 1) * 128],
                                                    in_=src_bf[:, kc * 128:(kc + 1) * 128], identity=ident_b)),
                 r=[skey, "ident_b"], w=[pkey])
        src3 = pb[:, 0:ncol_blocks * 128].rearrange("p (a b) -> p a b", a=ncol_blocks)
        if eng == "act":
            P.op("act", lambda e: e.activation(out=dst3, in_=src3, func=AF.Copy), r=[pkey], w=[dkey])
        else:
            P.op("dve", lambda e: e.tensor_copy(out=dst3, in_=src3), r=[pkey], w=[dkey])

    def proj_tok(xT3, xkey, tcol0, W3, wkey, c0, n, ps_i):
        if isinstance(ps_i, tuple):
            po, pkey = ps_i
        else:
            po, pkey = psum[ps_i][:, 0:n], f"ps{ps_i}"
        for kc in range(8):
            P.op("pe", (lambda e, kc=kc: e.matmul(out=po, lhsT=xT3[:, kc, tcol0:tcol0 + 128],
                                                 rhs=W3[:, kc, c0:c0 + n], start=(kc == 0), stop=(kc == 7))),
                 r=[xkey, wkey], w=[pkey])

    def rope(src3, skeys, H, cs, cskey, dst3, dkey, rt, rkey):
        cosb = cs[:, None, 0:8].to_broadcast([128, H, 8])
        sinb = cs[:, None, 8:16].to_broadcast([128, H, 8])
        x1 = src3[:, :, 0:8]
        x2 = src3[:, :, 8:16]
        t = [rt[:, i * 64:i * 64 + H * 8].rearrange("p (a b) -> p a b", a=H) for i in range(4)]
        rr = list(skeys) + [cskey]
        P.op("dve", lambda e: e.tensor_tensor(out=t[0], in0=x1, in1=cosb, op=ALU.mult), r=rr, w=[rkey + "0"])
        P.op("dve", lambda e: e.tensor_tensor(out=t[1], in0=x2, in1=sinb, op=ALU.mult), r=rr, w=[rkey + "1"])
        P.op("dve", lambda e: e.tensor_tensor(out=t[2], in0=x2, in1=cosb, op=ALU.mult), r=rr, w=[rkey + "2"])
        P.op("dve", lambda e: e.tensor_tensor(out=t[3], in0=x1, in1=sinb, op=ALU.mult), r=rr, w=[rkey + "3"])
        P.op("dve", lambda e: e.tensor_tensor(out=dst3[:, :, 0:8], in0=t[0], in1=t[1], op=ALU.subtract),
             r=[rkey + "0", rkey + "1"], w=[dkey])
        P.op("dve", lambda e: e.tensor_tensor(out=dst3[:, :, 8:16], in0=t[2], in1=t[3], op=ALU.add),
             r=[rkey + "2", rkey + "3"], w=[dkey])
        P.op("act", lambda e: e.activation(out=dst3[:, :, 16:64], in_=src3[:, :, 16:64], func=AF.Copy),
             r=list(skeys), w=[dkey])

    A.mark()
    WA = A.alloc([128, 8, 1152], BF16)
    ikT = A.alloc([128, S], BF16)
    QT_r = A.alloc([128, 4, RT], BF16)
    iqT_r = A.alloc([128, 4, RT], BF16)
    sgn = A.alloc([128, 5 * 8], F32)
    wabs2 = [A.alloc([128, 8], F32) for _ in range(2)]
    qrel = A.alloc([128, 16], F32)
    bis = A.alloc([128, 8 + NBIS], F32)
    ntmp = [A.alloc([128, 4], F32) for _ in range(3)]
    cst = [A.alloc([128, 16], F32) for _ in range(4)]
    oT_r = A.alloc([64, 8, RT], BF16)
    QTz = A.alloc([128, 8, RT], BF16)
    kchunk = [A.alloc([128, 4, 512], BF16) for _ in range(2)]
    vchunk = [A.alloc([128, 4, 520], BF16) for _ in range(2)]
    diag = [A.alloc([128, 8, 128], BF16) for _ in range(2)]
    biasb = [A.alloc([128, 512], F32) for _ in range(1)]
    pT = [A.alloc([128, 512], BF16) for _ in range(2)]
    osb = [A.alloc([128, 512], F32) for _ in range(4)]
    rbuf = [A.alloc([128, 1024], BF16) for _ in range(2)]
    m0 = A.off
    nmask = [A.alloc([128, S], BF16) for _ in range(2)]
    s0 = A.off
    score2 = [A.alloc([128, S], F32) for _ in range(2)]
    SA = Arena(arena_t, s0)
    SA.off = m0
    xin = [SA.alloc([128, D], F32) for _ in range(3)]
    xnh = [SA.alloc([128, D], BF16) for _ in range(3)]
    ksb = [SA.alloc([128, 8, 64], BF16) for _ in range(4)]
    iksb = [SA.alloc([128, 2, 64], BF16) for _ in range(2)]
    iqs2 = [SA.alloc([128, 8, 64], F32) for _ in range(2)]
    junk1 = SA.alloc([128, D], BF16)
    SB = Arena(arena_t, s0 + 65536)
    SB.off = s0
    ropet = [SB.alloc([128, 256], F32) for _ in range(4)]
    vaug = [SB.alloc([128, 8, 65], BF16) for _ in range(2)]
    kTs = [SB.alloc([128, 4, 128], BF16) for _ in range(2)]
    xnT_r = SB.alloc([128, 8, RT], BF16)
    wst = [SB.alloc([128, 2048], F32) for _ in range(2)]
    xnT1 = [xnT_r[:, :, 0:128], xnT_r[:, :, 128:256], xnT_r[:, :, 256:384]]

    load_weight(w_kvi_d, WA.rearrange("p a b -> p (a b)"), 8 * 1152, 128, wst, "WA")
    for s in range(2):
        P.op("pool", (lambda e, s=s: e.memset(vaug[s].rearrange("p a b -> p (a b)"), 1.0)), w=[f"vaug{s}"])
    P.op("pool", lambda e: e.memset(QTz.rearrange("p a b -> p (a b)"), 0.0), w=["QTz"])

    def p1_stageA1(i):
        s3 = i % 3
        s4 = i % 4
        P.dma("sp", xin[s3], xc[i * 128:(i + 1) * 128, :], w=[f"xin{s3}"], slot=f"xin{s3}")
        P.dma("sp", cst[s4], cs_ctx[i * 128:(i + 1) * 128, :], w=[f"cst{s4}"], slot=f"cst{s4}")
        norm_tile(xin[s3], f"xin{s3}", gmix, xnh[s3], f"xnh{s3}", ntmp[s3], f"nt{s3}", junk1, "junk1")

    def p1_stageA2(i):
        s3 = i % 3
        transpose_to(xnh[s3], f"xnh{s3}", 8, i % 2, xnT1[s3], f"xnT1{s3}")

    def p1_stageB(i):
        s = i % 2
        s3 = i % 3
        ikps = (psum[6][:, s * 128:(s + 1) * 128], f"ps6i{s}")
        proj_tok(xnT_r, f"xnT1{s3}", s3 * 128, WA, "WA", 0, 512, 2 + s)
        proj_tok(xnT_r, f"xnT1{s3}", s3 * 128, WA, "WA", 512, 512, 4 + s)
        proj_tok(xnT_r, f"xnT1{s3}", s3 * 128, WA, "WA", 1024, 128, ikps)
        rope(psum[2 + s].rearrange("p (a b) -> p a b", a=8), [f"ps{2 + s}"], 8, cst[i % 4], f"cst{i % 4}", ksb[s],
             f"ksb{s}", ropet[s], f"rt{s}")
        P.op("act", (lambda e, s=s: e.activation(out=vaug[s][:, :, 0:64],
                                                 in_=psum[4 + s].rearrange("p (a b) -> p a b", a=8), func=AF.Copy)),
             r=[f"ps{4 + s}"], w=[f"vaug{s}"])
        rope(ikps[0].rearrange("p (a b) -> p a b", a=2), [ikps[1]], 2, cst[i % 4], f"cst{i % 4}", iksb[s], f"iksb{s}",
             ropet[2 + s], f"rt{2 + s}")
        transpose_to(ksb[s].rearrange("p a b -> p (a b)"), f"ksb{s}", 4,
                     (psb[7][:, s * 512:(s + 1) * 512], f"ps7{s}"), kTs[s], f"kTs{s}", eng="dve")
        transpose_to(iksb[s].rearrange("p a b -> p (a b)"), f"iksb{s}", 1,
                     (psb[6][:, 512 + s * 128:512 + (s + 1) * 128], f"ps6t{s}"),
                     ikT[:, i * 128:(i + 1) * 128].rearrange("p (a b) -> p a b", a=1), "ikT", eng="dve")
        P.dma("pool", kt_dram[i], kTs[s].rearrange("p a b -> p (a b)"), r=[f"kTs{s}"], w=["kt_dram"], slot=f"kst{s}")
        P.dma("pool", v_dram[i], vaug[s].rearrange("p a b -> p (a b)"), r=[f"vaug{s}"], w=["v_dram"], slot=f"vst{s}")

    p1_stageA1(0)
    p1_stageA1(1)
    p1_stageA1(2)
    p1_stageA2(0)
    p1_stageA2(1)
    for i in range(NT):
        if i + 3 < NT:
            p1_stageA1(i + 3)
        if i + 2 < NT:
            p1_stageA2(i + 2)
        p1_stageB(i)

    barrier()
    load_weight(w_qi_d, WA.rearrange("p a b -> p (a b)")[:, 0:8 * 1032], 8 * 1032, 128, wst, "WA")
    WQ = WA.rearrange("p a b -> p (a b)")[:, 0:8 * 1032].rearrange("p (a b) -> p a b", a=8)

    for r in range(NR):
        L = LR[r]
        KT = 4 * L
        NK = 512 * L
        barrier()
        def p2_stageA1(t, r=r):
            s3 = t % 3
            s4 = t % 4
            P.dma("sp", xin[s3], xo[r, t * 128:(t + 1) * 128, :], w=[f"xin{s3}"], slot=f"xin{s3}")
            P.dma("sp", cst[s4], cs_own[r, t * 128:(t + 1) * 128, :], w=[f"cst{s4}"], slot=f"cst{s4}")
            norm_tile(xin[s3], f"xin{s3}", gmix, xnh[s3], f"xnh{s3}", ntmp[s3], f"nt{s3}", junk1, "junk1")

        def p2_stageA2(t, r=r):
            s3 = t % 3
            transpose_to(xnh[s3], f"xnh{s3}", 8, t % 2, xnT_r[:, :, t * 128:(t + 1) * 128], f"xnT_r{t}")

        def p2_stageB(t, r=r):
            s = t % 2
            s3 = t % 3
            iwps = (psum[6][:, s * 8:(s + 1) * 8], f"ps6i{s}")
            proj_tok(xnT_r, f"xnT_r{t}", t * 128, WQ, "WA", 0, 512, 2 + s)
            proj_tok(xnT_r, f"xnT_r{t}", t * 128, WQ, "WA", 512, 512, 4 + s)
            proj_tok(xnT_r, f"xnT_r{t}", t * 128, WQ, "WA", 1024, 8, iwps)
            wab = wabs2[s]
            iqs = iqs2[s]
            P.op("act", (lambda e, wab=wab, iwps=iwps: e.activation(out=wab, in_=iwps[0], func=AF.Abs, scale=CS_IDX)),
                 r=[iwps[1]], w=[f"wabs{s}"])
            P.op("act", (lambda e, t=t, iwps=iwps: e.activation(out=sgn[:, t * 8:(t + 1) * 8], in_=iwps[0],
                                                                func=AF.Sign)), r=[iwps[1]], w=["sgn"])
            rope(psum[2 + s].rearrange("p (a b) -> p a b", a=8), [f"ps{2 + s}"], 8, cst[t % 4], f"cst{t % 4}", ksb[s],
                 f"ksb{s}", ropet[s], f"rt{s}")
            P.op("dve", (lambda e, s=s, wab=wab, iqs=iqs: e.tensor_tensor(
                out=iqs, in0=psum[4 + s].rearrange("p (a b) -> p a b", a=8),
                in1=wab[:, :, None].to_broadcast([128, 8, 64]), op=ALU.mult)),
                r=[f"ps{4 + s}", f"wabs{s}"], w=[f"iqs{s}"])
            rope(iqs, [f"iqs{s}"], 8, cst[t % 4], f"cst{t % 4}", ksb[2 + s], f"ksb{2 + s}", ropet[2 + s], f"rt{2 + s}")
            for kc in range(4):
                P.op("pe", (lambda e, kc=kc, s=s: e.transpose(
                    out=psb[7][:, kc * 128:(kc + 1) * 128],
                    in_=ksb[s].rearrange("p a b -> p (a b)")[:, kc * 128:(kc + 1) * 128], identity=ident_b)),
                    r=[f"ksb{s}", "ident_b"], w=["ps7q"])
            for half in range(2):
                P.op("dve", (lambda e, half=half, t=t: e.tensor_copy(
                    out=QTz[half * 64:(half + 1) * 64, :, t * 128:(t + 1) * 128].rearrange(
                        "p (a two) b -> p a two b", two=2)[:, :, half, :],
                    in_=psb[7][half * 64:(half + 1) * 64, 0:512].rearrange("p (a b) -> p a b", a=4))),
                    r=["ps7q"], w=["QTz"])
            transpose_to(ksb[2 + s].rearrange("p a b -> p (a b)"), f"ksb{2 + s}", 4,
                         (psb[7][:, 512:1024], "ps7i"), iqT_r[:, :, t * 128:(t + 1) * 128], "iqT_r", eng="dve")

        p2_stageA1(0)
        p2_stageA1(1)
        p2_stageA1(2)
        p2_stageA2(0)
        p2_stageA2(1)
        for t in range(5):
            if t + 3 < 5:
                p2_stageA1(t + 3)
            if t + 2 < 5:
                p2_stageA2(t + 2)
            p2_stageB(t)

        barrier()
        cb0 = max(0, 4 * r - 1)

        def indexer_scores(qb, r=r, L=L, KT=KT, NK=NK, cb0=cb0):
            sbi = qb % 2
            score = score2[sbi]
            dg = diag[sbi]
            for h in range(8):
                P.op("pool", (lambda e, h=h, dg=dg: e.tensor_scalar(
                    out=dg[:, h, :], in0=ident_b, scalar1=sgn[:, qb * 8 + h:qb * 8 + h + 1], scalar2=0.0,
                    op0=ALU.mult, op1=ALU.add)), r=["ident_b", "sgn"], w=[f"diag{sbi}"])
            pairs = [(c, hp) for c in range(L) for hp in range(4)]

            def emit_idx(pi):
                c, hp = pairs[pi]
                pb = pi % 2
                for half in range(2):
                    bank = pb * 2 + half
                    P.op("pe", (lambda e, half=half, hp=hp, c=c, bank=bank: e.matmul(
                        out=psum[bank],
                        lhsT=iqT_r[half * 64:(half + 1) * 64, hp, qb * 128:(qb + 1) * 128],
                        rhs=ikT[half * 64:(half + 1) * 64, c * 512:(c + 1) * 512], start=True, stop=True)),
                        r=["iqT_r", "ikT"], w=[f"ps{bank}"])
                P.op("act", (lambda e, pb=pb: e.activation(out=rbuf[pb], in_=psbig[:, pb * 1024:(pb + 1) * 1024],
                                                           func=AF.Relu)),
                     r=[f"ps{pb * 2}", f"ps{pb * 2 + 1}"], w=[f"rbuf{pb}"])

            def emit_diag(pi):
                c, hp = pairs[pi]
                pb = pi % 2
                sbank = 4 + c % 2
                for half in range(2):
                    h = hp * 2 + half
                    P.op("pe", (lambda e, h=h, half=half, pb=pb, sbank=sbank, dg=dg: e.matmul(
                        out=psum[sbank], lhsT=dg[:, h, :], rhs=rbuf[pb][:, half * 512:(half + 1) * 512],
                        start=(h == 0), stop=(h == 7))),
                        r=[f"diag{sbi}", f"rbuf{pb}"], w=[f"ps{sbank}"])
                if hp == 3:
                    sc = score[:, c * 512:(c + 1) * 512]
                    P.op("act", (lambda e, sc=sc, sbank=sbank: e.activation(out=sc, in_=psum[sbank], func=AF.Copy)),
                         r=[f"ps{sbank}"], w=[f"score{sbi}_{c}"])

            if PIPE_IDX:
                emit_idx(0)
                for pi in range(len(pairs)):
                    if pi + 1 < len(pairs):
                        emit_idx(pi + 1)
                    emit_diag(pi)
            else:
                for pi in range(len(pairs)):
                    emit_idx(pi)
                    emit_diag(pi)
            allsc = [f"score{sbi}_{c}" for c in range(L)]
            P.op("dve", lambda e: e.tensor_reduce(out=bis[:, 0:1], in_=score[:, 0:NK], axis=AX.X, op=ALU.max,
                                                  apply_absolute_value=True), r=allsc, w=["bis_am"])
            qp = qpos[:, r * 5 + qb:r * 5 + qb + 1]
            for c in range(cb0, L):
                P.op("dve", (lambda e, c=c: e.tensor_scalar(out=qrel[:, c:c + 1], in0=qp, scalar1=float(-512 * c),
                                                            scalar2=None, op0=ALU.add)),
                     r=["qpos"], w=["qrel"])
                bb = biasb[0]
                P.op("dve", (lambda e, c=c, bb=bb: e.tensor_scalar(out=bb, in0=iota, scalar1=qrel[:, c:c + 1],
                                                                  scalar2=NEG, op0=ALU.is_gt, op1=ALU.mult)),
                     r=["iota", "qrel"], w=["biasb0"])
                sc = score[:, c * 512:(c + 1) * 512]
                P.op("dve", (lambda e, sc=sc, bb=bb: e.tensor_tensor(out=sc, in0=sc, in1=bb, op=ALU.add)),
                     r=["biasb0", f"score{sbi}_{c}", "bis_am"], w=[f"score{sbi}_{c}"])

        def bisect_block(qb, r=r, L=L, KT=KT, NK=NK):
            sbi = qb % 2
            score = score2[sbi]
            nm = nmask[sbi]
            nmk = f"nmask{sbi}"
            allsc = [f"score{sbi}_{c}" for c in range(L)]
            P.op("dve", lambda e: e.tensor_scalar(out=bis[:, 2:3], in0=bis[:, 0:1], scalar1=2.000001, scalar2=1e-30,
                                                  op0=ALU.mult, op1=ALU.add), r=["bis_am"], w=["bis_w0"])
            P.op("dve", lambda e: e.tensor_scalar(out=bis[:, 8:8 + NBIS], in0=pow2, scalar1=bis[:, 2:3], scalar2=None,
                                                  op0=ALU.mult), r=["pow2", "bis_w0"], w=["bis_wi"])
            P.op("dve", lambda e: e.tensor_tensor(out=bis[:, 3:4], in0=bis[:, 8:9], in1=bis[:, 0:1], op=ALU.subtract),
                 r=["bis_wi", "bis_am"], w=["bis_mid"])
            for it in range(NBIS):
                wi = bis[:, 8 + it:9 + it]
                P.op("dve", lambda e: e.tensor_scalar(out=nm[:, 0:NK], in0=score[:, 0:NK], scalar1=bis[:, 3:4],
                                                      scalar2=None, op0=ALU.is_ge, op1=ALU.add,
                                                      accum_out=bis[:, 4:5]),
                     r=allsc + ["bis_mid"], w=[nmk, "bis_cnt"])
                P.op("dve", lambda e: e.tensor_scalar(out=bis[:, 5:6], in0=bis[:, 4:5], scalar1=255.5,
                                                      scalar2=-0.5, op0=ALU.is_ge, op1=ALU.add),
                     r=["bis_cnt"], w=["bis_t"])
                P.op("dve", (lambda e, wi=wi: e.scalar_tensor_tensor(out=bis[:, 3:4], in0=bis[:, 5:6], scalar=wi,
                                                                    in1=bis[:, 3:4], op0=ALU.mult, op1=ALU.add)),
                     r=["bis_t", "bis_wi", "bis_mid"], w=["bis_mid"])
            P.op("dve", lambda e: e.scalar_tensor_tensor(out=bis[:, 1:2], in0=bis[:, 8 + NBIS - 1:8 + NBIS],
                                                         scalar=-0.5, in1=bis[:, 3:4], op0=ALU.mult, op1=ALU.add),
                 r=["bis_wi", "bis_mid"], w=["bis_lo"])
            P.op("dve", lambda e: e.tensor_scalar(out=nm[:, 0:NK], in0=score[:, 0:NK], scalar1=bis[:, 1:2],
                                                  scalar2=-30000.0, op0=ALU.is_lt, op1=ALU.mult),
                 r=allsc + ["bis_lo"], w=[nmk])

        def attention_block(qb, L=L, KT=KT):
            nm = nmask[qb % 2]
            nmk = f"nmask{qb % 2}"
            q0 = qb * 128
            groups = []
            for c4 in range(L):
                for j in range(4):
                    for hg in range(2):
                        groups.append((c4, j, hg))

            def load_chunk(c4):
                s3 = c4 % 2
                P.dma("sp", kchunk[s3], kt_dram[4 * c4:4 * c4 + 4].rearrange("t p c -> p t c"), r=["kt_dram"],
                      w=[f"kch{s3}"], slot=f"kch{s3}")
                P.dma("sp", vchunk[s3], v_dram[4 * c4:4 * c4 + 4].rearrange("t p c -> p t c"), r=["v_dram"],
                      w=[f"vch{s3}"], slot=f"vch{s3}")

            def emit_scores(gi):
                c4, j, hg = groups[gi]
                if j == 0 and hg == 0:
                    load_chunk(c4)
                s3 = c4 % 2
                kt = 4 * c4 + j
                bank = 4 + gi % 2
                pbuf = gi % 2
                P.op("pe", (lambda e, kt=kt, bank=bank: e.matmul(
                    out=psum[bank], lhsT=nm[:, kt * 128:(kt + 1) * 128], rhs=ident4, start=True, stop=False)),
                    r=[nmk, "ident4"], w=[f"ps{bank}"])
                for pp in range(2):
                    p = hg * 2 + pp
                    P.op("pe", (lambda e, pp=pp, p=p, j=j, s3=s3, bank=bank: e.matmul(
                        out=psum[bank][:, pp * 256:(pp + 1) * 256],
                        lhsT=kchunk[s3][:, j, p * 128:(p + 1) * 128],
                        rhs=QTz[:, 2 * p:2 * p + 2, q0:q0 + 128], start=False, stop=(pp == 1))),
                        r=[f"kch{s3}", "QTz"], w=[f"ps{bank}"])
                P.op("act", (lambda e, bank=bank, pbuf=pbuf: e.activation(out=pT[pbuf], in_=psum[bank],
                                                                         func=AF.Exp, scale=0.125)),
                     r=[f"ps{bank}"], w=[f"pT{pbuf}"])

            def emit_pv(gi):
                c4, j, hg = groups[gi]
                s3 = c4 % 2
                kt = 4 * c4 + j
                pbuf = gi % 2
                for hh in range(4):
                    h = hg * 4 + hh
                    P.op("pe", (lambda e, hh=hh, h=h, j=j, s3=s3, pbuf=pbuf, hg=hg, kt=kt: e.matmul(
                        out=psum[6 + hg][0:65, hh * 128:(hh + 1) * 128],
                        lhsT=vchunk[s3][:, j, h * 65:(h + 1) * 65],
                        rhs=pT[pbuf][:, hh * 128:(hh + 1) * 128],
                        start=(kt == 0 and hh == 0), stop=(kt == KT - 1))),
                        r=[f"vch{s3}", f"pT{pbuf}"], w=[f"ps{6 + hg}"])

            ng = len(groups)
            emit_scores(0)
            for gi in range(ng):
                if gi + 1 < ng:
                    emit_scores(gi + 1)
                emit_pv(gi)
            for hg in range(2):
                oi = (qb % 2) * 2 + hg
                ob = osb[oi]
                P.op("act", (lambda e, ob=ob, hg=hg: e.activation(out=ob[0:65, :], in_=psum[6 + hg][0:65, :],
                                                                  func=AF.Copy)),
                     r=[f"ps{6 + hg}"], w=[f"osb{oi}"])

        def attention_fin(qb):
            q0 = qb * 128
            for hg in range(2):
                oi = (qb % 2) * 2 + hg
                ob = osb[oi]
                P.op("dve", (lambda e, ob=ob: e.reciprocal(out=ob[64:65, :], in_=ob[64:65, :])),
                     r=[f"osb{oi}"], w=[f"osb{oi}"])
                P.op("dve", (lambda e, ob=ob: e.tensor_copy(out=hl[64:65, 0:512], in_=ob[64:65, :])),
                     r=[f"osb{oi}"], w=["hl_hi"])
                P.op("dve", (lambda e, ob=ob: e.tensor_tensor(out=hl[64:65, 512:1024], in0=ob[64:65, :],
                                                              in1=hl[64:65, 0:512], op=ALU.subtract)),
                     r=[f"osb{oi}", "hl_hi"], w=["hl_lo"])
                P.op("pe", (lambda e, hg=hg: e.matmul(out=psum[4 + hg][0:64, :], lhsT=ones_b[64:65, 0:64],
                                                      rhs=hl[64:65, 0:512], start=True, stop=False)),
                     r=["ones_b", "hl_hi"], w=[f"ps{4 + hg}"])
                P.op("pe", (lambda e, hg=hg: e.matmul(out=psum[4 + hg][0:64, :], lhsT=ones_b[64:65, 0:64],
                                                      rhs=hl[64:65, 512:1024], start=False, stop=True)),
                     r=["ones_b", "hl_lo"], w=[f"ps{4 + hg}"])
                P.op("dve", (lambda e, ob=ob, hg=hg: e.tensor_tensor(
                    out=oT_r[0:64, hg * 4:(hg + 1) * 4, q0:q0 + 128],
                    in0=ob[0:64, :].rearrange("p (a b) -> p a b", a=4),
                    in1=psum[4 + hg][0:64, :].rearrange("p (a b) -> p a b", a=4), op=ALU.mult)),
                    r=[f"osb{oi}", f"ps{4 + hg}"], w=["oT_r"])

        indexer_scores(0)
        for qb in range(5):
            if qb > 1:
                attention_fin(qb - 2)
            bisect_block(qb)
            if qb + 1 < 5:
                indexer_scores(qb + 1)
            if qb > 0:
                attention_block(qb - 1)
        attention_fin(3)
        attention_block(4)
        attention_fin(4)
        P.dma("pool", ot_dram[r], oT_r.rearrange("p a b -> p (a b)"), r=["oT_r"], w=["ot_dram"], slot="otst")

    barrier()
    A.release()
    A.mark()
    Wu = A.alloc([128, 8, 512], BF16)
    Wg = A.alloc([128, 8, 2048], BF16)
    Wpp = A.alloc([128, 4, 1024], BF16)
    Wap = A.alloc([64, 8, 1024], BF16)
    Wo = A.alloc([128, 8, 1024], BF16)
    poolw = A.alloc([128, 4, 128], BF16)
    pscale = A.alloc([128, 4], F32)
    pm_std = A.alloc([128, 8, 128], BF16)
    pm_first = A.alloc([128, NR * 12, 128], BF16)
    xr = [A.alloc([128, D], F32) for _ in range(5)]
    x0 = A.off - 5 * 4096
    SX = Arena(arena_t, A.off)
    SX.off = x0
    wst3 = [SX.alloc([128, 2048], F32) for _ in range(2)]
    xnh3 = [A.alloc([128, D], BF16) for _ in range(3)]
    ntmp3 = [A.alloc([128, 4], F32) for _ in range(3)]
    junk3 = A.alloc([128, D], BF16)
    xnT3b = A.alloc([128, 8, RT], BF16)
    u_sb = A.alloc([128, 5, 512], BF16)
    pooledT = A.alloc([128, 4, RT], BF16)
    mixedT = A.alloc([128, 4, RT], BF16)
    oT3b = A.alloc([64, 8, RT], BF16)
    mergedT = A.alloc([128, 8, RT], BF16)
    sg0 = A.alloc([128, 512], F32)
    sg1 = A.alloc([128, 512], F32)
    tt0 = A.alloc([128, 512], F32)
    tt1 = A.alloc([128, 512], F32)
    hb = [A.alloc([128, D], F32) for _ in range(2)]

    P.dma("sp", pscale, pscale_d, w=["pscale", "cchain"], slot="c")
    load_weight(w_u_d, Wu.rearrange("p a b -> p (a b)"), 8 * 512, 128, wst3, "Wu")
    load_weight(w_g_d, Wg.rearrange("p a b -> p (a b)"), 8 * 2048, 128, wst3, "Wg")
    load_weight(w_pp_d, Wpp.rearrange("p a b -> p (a b)"), 4 * 1024, 128, wst3, "Wpp")
    load_weight(w_ap_d, Wap.rearrange("p a b -> p (a b)"), 8 * 1024, 64, wst3, "Wap")
    load_weight(w_out_d, Wo.rearrange("p a b -> p (a b)"), 8 * 1024, 128, wst3, "Wo")
    load_weight(poolw_d, poolw.rearrange("p a b -> p (a b)"), 4 * 128, 128, wst3, "poolw")
    load_weight(pm_std_d, pm_std.rearrange("p a b -> p (a b)"), 8 * 128, 128, wst3, "pm_std")
    load_weight(pm_first_d, pm_first.rearrange("p a b -> p (a b)"), NR * 12 * 128, 128, wst3, "pm_first")
    barrier()

    slabs = [(0, 512), (512, 128)]
    for r in range(NR):
        P.dma("sp", oT3b.rearrange("p a b -> p (a b)"), ot_dram[r], r=["ot_dram"], w=["oT3"], slot="otl")
        def p3_A1(t, r=r):
            s3 = t % 3
            P.dma("sp", xr[t], xo[r, t * 128:(t + 1) * 128, :], w=[f"xr{t}"], slot=f"xr{t}")
            norm_tile(xr[t], f"xr{t}", gmix, xnh3[s3], f"xnh3{s3}", ntmp3[s3], f"nt{s3}", junk3, "junk3")

        def p3_A2(t):
            s3 = t % 3
            transpose_to(xnh3[s3], f"xnh3{s3}", 8, t % 2, xnT3b[:, :, t * 128:(t + 1) * 128], f"xnT3_{t}")

        def p3_B(t, r=r):
            proj_tok(xnT3b, f"xnT3_{t}", t * 128, Wu, "Wu", 0, 512, 2)
            P.op("act", (lambda e, t=t: e.activation(out=u_sb[:, t, :], in_=psum[2][:, 0:512], func=AF.Copy)),
                 r=["ps2"], w=[f"u{t}"])
            for g in range(4):
                mats = []
                if t == 1:
                    mats.append((t, pm_first[:, r * 12 + g, :], "pm_first"))
                    mats.append((t, pm_first[:, r * 12 + 4 + g, :], "pm_first"))
                    mats.append((t - 1, pm_first[:, r * 12 + 8 + g, :], "pm_first"))
                else:
                    mats.append((t, pm_std[:, g, :], "pm_std"))
                    if t > 0:
                        mats.append((t - 1, pm_std[:, 4 + g, :], "pm_std"))
                for mi, (tu, M, mk) in enumerate(mats):
                    P.op("pe", (lambda e, tu=tu, M=M, g=g, mi=mi, nm=len(mats): e.matmul(
                        out=psum[3][:, g * 128:(g + 1) * 128], lhsT=u_sb[:, tu, g * 128:(g + 1) * 128], rhs=M,
                        start=(mi == 0), stop=(mi == nm - 1))),
                        r=[f"u{tu}", mk], w=["ps3"])
            P.op("act", (lambda e, t=t: e.activation(out=pooledT[:, :, t * 128:(t + 1) * 128],
                                                     in_=psum[3].rearrange("p (a b) -> p a b", a=4),
                                                     func=AF.Copy)), r=["ps3"], w=["pooledT"])

        p3_A1(0)
        p3_A1(1)
        p3_A1(2)
        p3_A2(0)
        p3_A2(1)
        for t in range(5):
            if t + 3 < 5:
                p3_A1(t + 3)
            if t + 2 < 5:
                p3_A2(t + 2)
            p3_B(t)
        for g in range(4):
            for (c0, n) in slabs:
                P.op("pe", (lambda e, g=g, c0=c0, n=n: e.matmul(out=psum[2][:, 0:n], lhsT=poolw[:, g, :],
                                                               rhs=pooledT[:, g, c0:c0 + n], start=True, stop=True)),
                     r=["poolw", "pooledT"], w=["ps2"])
                P.op("dve", (lambda e, g=g, c0=c0, n=n: e.tensor_scalar(out=mixedT[:, g, c0:c0 + n],
                                                                       in0=psum[2][:, 0:n],
                                                                       scalar1=pscale[:, g:g + 1], scalar2=None,
                                                                       op0=ALU.mult)),
                     r=["ps2", "pscale"], w=["mixedT"])
        for c in range(8):
            for (c0, n) in slabs:
                for g in range(4):
                    P.op("pe", (lambda e, g=g, c=c, c0=c0, n=n: e.matmul(
                        out=psum[4][:, 0:n], lhsT=Wpp[:, g, c * 128:(c + 1) * 128], rhs=mixedT[:, g, c0:c0 + n],
                        start=(g == 0), stop=(g == 3))), r=["Wpp", "mixedT"], w=["ps4"])
                for h in range(8):
                    P.op("pe", (lambda e, h=h, c=c, c0=c0, n=n: e.matmul(
                        out=psum[5][:, 0:n], lhsT=Wap[0:64, h, c * 128:(c + 1) * 128], rhs=oT3b[0:64, h, c0:c0 + n],
                        start=(h == 0), stop=(h == 7))), r=["Wap", "oT3"], w=["ps5"])
                for gi in range(2):
                    for kc in range(8):
                        P.op("pe", (lambda e, gi=gi, kc=kc, c=c, c0=c0, n=n: e.matmul(
                            out=psum[6 + gi][:, 0:n],
                            lhsT=Wg[:, kc, gi * 1024 + c * 128:gi * 1024 + (c + 1) * 128],
                            rhs=xnT3b[:, kc, c0:c0 + n], start=(kc == 0), stop=(kc == 7))),
                            r=["Wg"] + [f"xnT3_{tt}" for tt in range(5)], w=[f"ps{6 + gi}"])
                P.op("act", (lambda e, n=n: e.activation(out=sg0[:, 0:n], in_=psum[6][:, 0:n], func=AF.Sigmoid)),
                     r=["ps6"], w=["sg0"])
                P.op("act", (lambda e, n=n: e.activation(out=sg1[:, 0:n], in_=psum[7][:, 0:n], func=AF.Sigmoid)),
                     r=["ps7"], w=["sg1"])
                P.op("dve", (lambda e, n=n: e.tensor_tensor(out=tt0[:, 0:n], in0=sg0[:, 0:n], in1=psum[4][:, 0:n],
                                                            op=ALU.mult)), r=["sg0", "ps4"], w=["tt0"])
                P.op("dve", (lambda e, n=n: e.tensor_tensor(out=tt1[:, 0:n], in0=sg1[:, 0:n], in1=psum[5][:, 0:n],
                                                            op=ALU.mult)), r=["sg1", "ps5"], w=["tt1"])
                P.op("dve", (lambda e, c=c, c0=c0, n=n: e.tensor_tensor(out=mergedT[:, c, c0:c0 + n], in0=tt0[:, 0:n],
                                                                       in1=tt1[:, 0:n], op=ALU.add)),
                     r=["tt0", "tt1"], w=["mergedT"])
        for t in range(5):
            hs = t % 2
            for half in range(2):
                bank = 2 + half
                for c in range(8):
                    P.op("pe", (lambda e, c=c, t=t, half=half, bank=bank: e.matmul(
                        out=psum[bank][:, 0:512], lhsT=mergedT[:, c, t * 128:(t + 1) * 128],
                        rhs=Wo[:, c, half * 512:(half + 1) * 512], start=(c == 0), stop=(c == 7))),
                        r=["mergedT", "Wo"], w=[f"ps{bank}"])
                P.op("dve", (lambda e, t=t, half=half, bank=bank, hs=hs: e.tensor_tensor(
                    out=hb[hs][:, half * 512:(half + 1) * 512], in0=xr[t][:, half * 512:(half + 1) * 512],
                    in1=psum[bank][:, 0:512], op=ALU.add)), r=[f"xr{t}", f"ps{bank}"], w=[f"hb{hs}"])
            P.dma("pool", h_dram[r, t * 128:(t + 1) * 128, :], hb[hs], r=[f"hb{hs}"], w=["h_dram"], slot=f"hst{hs}")

    barrier()
    A.release()
    A.mark()
    Wup = A.alloc([128, 8, 2 * DFF], BF16)
    Wdn = A.alloc([128, NF, 1024], BF16)
    convw = A.alloc([128, 44, 3], F32)
    convb = A.alloc([128, 44], F32)
    g2 = A.alloc([128, 2 * D], F32)
    ht = [None] + [A.alloc([128, D], F32) for _ in range(4)]
    hnT = A.alloc([128, 8, RT], BF16)
    NFH = NF // 2
    actT = A.alloc([128, NFH, 512], BF16)
    ntmp4 = [A.alloc([128, 4], F32) for _ in range(3)]
    upsb = [A.alloc([128, 516], F32) for _ in range(2)]
    cc2 = A.alloc([128, D], F32)
    cc = [cc2[:, 0:512], cc2[:, 512:1024]]
    ht[0] = cc2
    silu_o = A.alloc([128, 512], F32)
    junk4 = silu_o.bitcast(BF16)
    xnh4 = [A.alloc([128, D], BF16) for _ in range(3)]
    h0 = A.off
    wflat = arena_t
    SW = Arena(arena_t, CAP)
    wst4 = [hnT.rearrange("p a b -> p (a b)")[:, 0:4096].bitcast(F32),
           actT.rearrange("p a b -> p (a b)")[:, 0:4096].bitcast(F32)]

    P.dma("sp", convw.rearrange("p a b -> p (a b)"), convw_d, w=["convw", "cchain"], slot="c")
    P.dma("sp", convb, convb_d, w=["convb", "cchain"], slot="c")
    P.dma("sp", g2, g_d[:, D:3 * D], w=["g2", "cchain"], slot="c")
    load_weight(w_up_d, Wup.rearrange("p a b -> p (a b)"), 8 * 2 * DFF, 128, wst4, "W3b")
    load_weight(w_dn_d, Wdn.rearrange("p a b -> p (a b)"), NF * 1024, 128, wst4, "W3b")
    barrier()

    for r in range(NR):
        def p4_A1(t, r=r):
            hk = [f"ht{t}"] if t > 0 else ["cc0", "cc1"]
            s = t % 3
            P.dma("sp", ht[t], h_dram[r, t * 128:(t + 1) * 128, :], r=["h_dram"], w=hk, slot=f"ht{t}")
            P.op("act", (lambda e, t=t, s=s: e.activation(out=junk4, in_=ht[t], func=AF.Square,
                                                         accum_out=ntmp4[s][:, 0:1])),
                 r=hk, w=["silu_o", f"nt{s}0"])
            P.op("dve", (lambda e, s=s: e.tensor_scalar(out=ntmp4[s][:, 1:2], in0=ntmp4[s][:, 0:1], scalar1=1.0 / D,
                                                        scalar2=EPS, op0=ALU.mult, op1=ALU.add)),
                 r=[f"nt{s}0"], w=[f"nt{s}1"])
            P.op("pool", (lambda e, s=s: e.tensor_tensor(out=ntmp4[s][:, 2:3], in0=ntmp4[s][:, 1:2], in1=cneg[:, 0:1],
                                                         op=ALU.pow)), r=[f"nt{s}1", "cneg"], w=[f"nt{s}2"])
            P.op("dve", (lambda e, t=t, s=s: e.scalar_tensor_tensor(
                out=xnh4[s], in0=ht[t], scalar=ntmp4[s][:, 2:3], in1=g2[:, 0:D], op0=ALU.mult, op1=ALU.mult)),
                r=hk + [f"nt{s}2", "g2"], w=[f"xnh4{s}"])

        def p4_A2(t):
            s = t % 3
            transpose_to(xnh4[s], f"xnh4{s}", 8, t % 2, hnT[:, :, t * 128:(t + 1) * 128], "hnT")

        p4_A1(0)
        p4_A1(1)
        p4_A1(2)
        for t in range(5):
            p4_A2(t)
            if t + 3 < 5:
                p4_A1(t + 3)
        for fh in range(2):
            for fi in range(NFH):
                f = fh * NFH + fi
                for ab in range(2):
                    col = ab * DFF + f * 128
                    ch = ab * NF + f
                    bank = 2 + ab
                    hbank = 4 + ab
                    for kc in range(8):
                        P.op("pe", (lambda e, kc=kc, col=col, bank=bank: e.matmul(
                            out=psum[bank][:, 0:512], lhsT=Wup[:, kc, col:col + 128], rhs=hnT[:, kc, 128:640],
                            start=(kc == 0), stop=(kc == 7))), r=["W3b", "hnT"], w=[f"ps{bank}"])
                    for kc in range(8):
                        P.op("pe", (lambda e, kc=kc, col=col, hbank=hbank: e.matmul(
                            out=psum[hbank][:, 0:2], lhsT=Wup[:, kc, col:col + 128], rhs=hnT[:, kc, 126:128],
                            start=(kc == 0), stop=(kc == 7))), r=["W3b", "hnT"], w=[f"ps{hbank}"])
                    ub = upsb[ab]
                    cb = cc[ab]
                    P.op("act", (lambda e, ub=ub, bank=bank: e.activation(out=ub[:, 2:514], in_=psum[bank][:, 0:512],
                                                                          func=AF.Copy)),
                         r=[f"ps{bank}"], w=[f"upsb{ab}"])
                    P.op("act", (lambda e, ub=ub, hbank=hbank, r=r: e.activation(
                        out=ub[:, 0:2], in_=psum[hbank][:, 0:2], func=AF.Identity, scale=hsc[:, r:r + 1])),
                        r=[f"ps{hbank}", "hsc"], w=[f"upsb{ab}"])
                    P.op("act", (lambda e, cb=cb, bank=bank, ch=ch: e.activation(
                        out=cb, in_=psum[bank][:, 0:512], func=AF.Identity, scale=convw[:, ch, 2:3],
                        bias=convb[:, ch:ch + 1])), r=[f"ps{bank}", "convw", "convb"], w=[f"cc{ab}"])
                    P.op("dve", (lambda e, cb=cb, ub=ub, ch=ch: e.scalar_tensor_tensor(
                        out=cb, in0=ub[:, 1:513], scalar=convw[:, ch, 1:2], in1=cb, op0=ALU.mult, op1=ALU.add)),
                        r=[f"upsb{ab}", f"cc{ab}", "convw"], w=[f"cc{ab}"])
                    P.op("dve", (lambda e, cb=cb, ub=ub, ch=ch: e.scalar_tensor_tensor(
                        out=cb, in0=ub[:, 0:512], scalar=convw[:, ch, 0:1], in1=cb, op0=ALU.mult, op1=ALU.add)),
                        r=[f"upsb{ab}", f"cc{ab}", "convw"], w=[f"cc{ab}"])
                P.op("act", lambda e: e.activation(out=silu_o, in_=cc[0], func=AF.Silu), r=["cc0"], w=["silu_o"])
                P.op("dve", (lambda e, fi=fi: e.tensor_tensor(out=actT[:, fi, :], in0=silu_o, in1=cc[1], op=ALU.mult)),
                     r=["silu_o", "cc1"], w=["actT"])
            for t in range(4):
                for half in range(2):
                    bank = 6 + half
                    for fi in range(NFH):
                        f = fh * NFH + fi
                        P.op("pe", (lambda e, f=f, fi=fi, t=t, half=half, bank=bank: e.matmul(
                            out=psum[bank][:, 0:512], lhsT=actT[:, fi, t * 128:(t + 1) * 128],
                            rhs=Wdn[:, f, half * 512:(half + 1) * 512], start=(fi == 0), stop=(fi == NFH - 1))),
                            r=["actT", "W3b"], w=[f"ps{bank}"])
                    P.op("dve", (lambda e, t=t, half=half, bank=bank: e.tensor_tensor(
                        out=ht[t + 1][:, half * 512:(half + 1) * 512], in0=ht[t + 1][:, half * 512:(half + 1) * 512],
                        in1=psum[bank][:, 0:512], op=ALU.add)), r=[f"ht{t + 1}", f"ps{bank}"], w=[f"ht{t + 1}"])
        for t in range(4):
            s = t % 2
            P.op("act", (lambda e, t=t, s=s: e.activation(out=junk4, in_=ht[t + 1], func=AF.Square,
                                                         accum_out=ntmp4[s][:, 0:1])),
                 r=[f"ht{t + 1}"], w=["silu_o", f"nt{s}0"])
            P.op("dve", (lambda e, s=s: e.tensor_scalar(out=ntmp4[s][:, 1:2], in0=ntmp4[s][:, 0:1], scalar1=1.0 / D,
                                                        scalar2=EPS, op0=ALU.mult, op1=ALU.add)),
                 r=[f"nt{s}0"], w=[f"nt{s}1"])
            P.op("pool", (lambda e, s=s: e.tensor_tensor(out=ntmp4[s][:, 2:3], in0=ntmp4[s][:, 1:2], in1=cneg[:, 0:1],
                                                         op=ALU.pow)), r=[f"nt{s}1", "cneg"], w=[f"nt{s}2"])
            P.op("dve", (lambda e, t=t, s=s: e.scalar_tensor_tensor(
                out=ht[t + 1], in0=ht[t + 1], scalar=ntmp4[s][:, 2:3], in1=g2[:, D:2 * D], op0=ALU.mult,
                op1=ALU.mult)), r=[f"ht{t + 1}", f"nt{s}2", "g2"], w=[f"ht{t + 1}"])
            P.dma("pool", out_d[r, t * 128:(t + 1) * 128, :], ht[t + 1], r=[f"ht{t + 1}"], w=["out"],
                  slot=f"ost{t % 2}")
    P.op("pool", None, r=["out"])
    P.emit(st)
    st.close()
    return nc


def _chunks(j):
    return [j, 7 - j, 8 + j, 15 - j]


def _rope_tables(pos):
    half = 8
    inv = 1.0 / (500000.0 ** (np.arange(half, dtype=np.float32) * 2.0 / 16.0))
    ang = pos.astype(np.float32)[:, None] * inv[None, :].astype(np.float32)
    return np.concatenate([np.cos(ang), np.sin(ang)], axis=1).astype(np.float32)


def _pool_mats(first):
    Am = np.zeros((4, 128, 128), np.float64)
    Bm = np.zeros((4, 128, 128), np.float64)
    for g, w in enumerate((2, 4, 8, 16)):
        for t in range(128):
            cnt = min(t + 1, w) if first else w
            for tp in range(t - w + 1, t + 1):
                if tp >= 0:
                    Am[g, tp, t] += 1.0 / cnt
                elif not first:
                    Bm[g, tp + 128, t] += 1.0 / cnt
            Am[g, t, t] -= 1.0
    return Am, Bm


def _pk(a, kc):
    n = a.shape[1]
    return np.ascontiguousarray(a.reshape(kc, 128, n).transpose(1, 0, 2).reshape(128, kc * n))


_NC_CACHE = {}


def kernel(x, norm_mix_g, w_in, pool_w, pool_scale, w_pool_proj, w_attn_proj, w_out,
           norm_ffn_g, w_up, conv_w, conv_b, w_down, norm_final_g):
    f = lambda a: np.asarray(a, dtype=np.float32)
    x = f(x)
    w_in = f(w_in)[0]
    cuts = np.cumsum([512, 512, 512, 512, 512, 64, 8, 2048])
    Wu_, Wq_, Wk_, Wv_, Wiq_, Wik_, Wiw_, Wg_ = np.split(w_in, cuts[:-1], axis=1)
    shared = {}
    shared["w_kvi"] = _pk(np.concatenate([Wk_, Wv_, Wik_, Wik_], axis=1), 8)
    shared["w_qi"] = _pk(np.concatenate([Wq_, Wiq_, Wiw_], axis=1), 8)
    shared["w_u"] = _pk(Wu_, 8)
    shared["w_g"] = _pk(Wg_, 8)
    shared["pool_w"] = np.ascontiguousarray(f(pool_w)[0].transpose(1, 0, 2).reshape(128, 512))
    shared["pool_scale"] = np.ascontiguousarray(f(pool_scale)[0].reshape(4, 128).T)
    shared["w_pp"] = _pk(f(w_pool_proj)[0], 4)
    shared["w_ap"] = np.ascontiguousarray(f(w_attn_proj)[0].reshape(8, 64, 1024).transpose(1, 0, 2).reshape(64, 8192))
    shared["w_out"] = _pk(f(w_out)[0], 8)
    shared["w_up"] = _pk(f(w_up)[0], 8)
    shared["w_down"] = _pk(f(w_down)[0], NF)
    shared["conv_w"] = np.ascontiguousarray(f(conv_w)[0].reshape(3, 44, 128).transpose(2, 1, 0).reshape(128, 132))
    shared["conv_b"] = np.ascontiguousarray(f(conv_b)[0].reshape(44, 128).T)
    gains = np.concatenate([f(norm_mix_g)[0], f(norm_ffn_g)[0], f(norm_final_g)])
    shared["gains"] = np.ascontiguousarray(np.broadcast_to(gains[None, :], (128, 3 * D)))
    shared["iota"] = np.ascontiguousarray(np.broadcast_to(np.arange(512, dtype=np.float32)[None, :], (128, 512)))
    shared["pow2"] = np.ascontiguousarray(np.broadcast_to(
        (0.5 ** np.arange(1, NBIS + 1)).astype(np.float32)[None, :], (128, NBIS)))
    shared["ident"] = np.eye(128, dtype=np.float32)
    shared["cs_ctx"] = _rope_tables(np.arange(S))
    A_std, B_std = _pool_mats(False)
    A_fst, B_fst = _pool_mats(True)
    shared["pm_std"] = np.ascontiguousarray(
        np.concatenate([A_std, B_std], 0).transpose(1, 0, 2).reshape(128, 8 * 128)).astype(np.float32)

    def hi_lo(a):
        hi = a.astype(np.float32).astype(ml_dtypes.bfloat16).astype(np.float64)
        return hi, a - hi

    in_maps = []
    for c in range(8):
        b, j = c // 4, c % 4
        m = dict(shared)
        m["xc"] = np.ascontiguousarray(x[b])
        xo = np.zeros((NR, RT, D), np.float32)
        qp = np.zeros((NR, RT), np.float32)
        hs = np.zeros((NR,), np.float32)
        pmf = np.zeros((NR, 12, 128, 128), np.float64)
        for r, i in enumerate(_chunks(j)):
            t0 = 512 * i
            xo[r, 128:] = x[b, t0:t0 + 512]
            qp[r, 128:] = np.arange(t0, t0 + 512)
            if i > 0:
                xo[r, :128] = x[b, t0 - 128:t0]
                qp[r, :128] = np.arange(t0 - 128, t0)
                hs[r] = 1.0
                pmf[r, 0:4], pmf[r, 4:8], pmf[r, 8:12] = A_std, 0.0, B_std
            else:
                qp[r, :128] = np.arange(0, 128)
                hi, lo = hi_lo(A_fst)
                pmf[r, 0:4], pmf[r, 4:8], pmf[r, 8:12] = hi, lo, 0.0
        m["xo"] = xo
        m["qpos"] = np.ascontiguousarray(qp.reshape(NR * 5, 128).T)
        m["hscale"] = np.ascontiguousarray(np.broadcast_to(hs[None, :], (128, NR)))
        m["cs_own"] = _rope_tables(qp.reshape(-1)).reshape(NR, RT, 16)
        m["pm_first"] = np.ascontiguousarray(
            pmf.reshape(NR * 12, 128, 128).transpose(1, 0, 2).reshape(128, NR * 12 * 128)).astype(np.float32)
        in_maps.append(m)

    if "nc" not in _NC_CACHE:
        _NC_CACHE["nc"] = build_program()
    nc = _NC_CACHE["nc"]
    res = run_bass_kernel_spmd(nc, in_maps, core_ids=list(range(8)))
    _NC_CACHE['res'] = res if DEBUG else None
    out = np.zeros((2, S, D), np.float32)
    for c in range(8):
        b, j = c // 4, c % 4
        o = res.results[c]["out"]
        for r, i in enumerate(_chunks(j)):
            out[b, 512 * i:512 * i + 512] = o[r]
    return out
```

```python
import numpy as np
from contextlib import ExitStack
import ml_dtypes
import concourse.bass as bass
import concourse.mybir as mybir
from concourse.bass_utils import run_bass_kernel_spmd

F32 = mybir.dt.float32
BF16 = mybir.dt.bfloat16
U8 = mybir.dt.uint8
ALU = mybir.AluOpType
AF = mybir.ActivationFunctionType
AX = mybir.AxisListType

D = 1024
S = 8192
NT = 64
NR = 4
RT = 640
LR = [4, 8, 12, 16]
NBIS = 16
DFF = 2816
NF = 22
EPS = 1e-6
CS_IDX = (64 ** -0.5) * (8 ** -0.5)
NEG = -1.0e30


class Prog:
    ENGS = ("pe", "act", "dve", "pool", "sp")

    def __init__(self, nc):
        self.nc = nc
        self.ops = []

    def op(self, eng, fn, r=(), w=(), slot=None, barrier=False):
        self.ops.append(dict(eng=eng, fn=fn, r=tuple(r), w=tuple(w), slot=slot, barrier=barrier))

    def dma(self, eng, out, in_, r=(), w=(), slot=None, **kw):
        assert slot is not None
        self.op(eng, lambda e: e.dma_start(out=out, in_=in_, **kw), r, w, slot)

    def emit(self, stack):
        nc = self.nc
        ops = self.ops
        last_w = {}
        readers = {}
        deps = [None] * len(ops)
        needs_inc = [False] * len(ops)
        last_eng = {}
        last_slot = {}
        last_bar = None
        for i, o in enumerate(ops):
            d = set()
            if o["barrier"]:
                d.update(last_eng.values())
                d.update(last_slot.values())
            else:
                for k in o["r"]:
                    if k in last_w:
                        d.add(last_w[k])
                for k in o["w"]:
                    if k in last_w:
                        d.add(last_w[k])
                    for rr in readers.get(k, ()):
                        d.add(rr)
            if last_bar is not None:
                d.add(last_bar)
            d.discard(i)
            d = {p for p in d if not (ops[p]["slot"] is None and ops[p]["eng"] == "pe" and o["eng"] == "pe"
                                      and not ops[p]["barrier"])}
            deps[i] = d
            for p in d:
                needs_inc[p] = True
            for k in o["r"]:
                readers.setdefault(k, []).append(i)
            for k in o["w"]:
                last_w[k] = i
                readers[k] = []
            if o["slot"] is not None:
                last_slot[o["slot"]] = i
            elif o["fn"] is not None:
                last_eng[o["eng"]] = i
            if o["barrier"]:
                last_bar = i
        self._deps = deps
        if CHECK_DEADLOCK:
            done = [False] * len(ops)
            pe_ = {e: [i for i, o in enumerate(ops) if o["eng"] == e] for e in self.ENGS}
            ptr = {e: 0 for e in self.ENGS}
            progress = True
            while progress:
                progress = False
                for e in self.ENGS:
                    while ptr[e] < len(pe_[e]):
                        i = pe_[e][ptr[e]]
                        if all(done[p] for p in deps[i]):
                            done[i] = True
                            ptr[e] += 1
                            progress = True
                        else:
                            break
            stuck = {e: pe_[e][ptr[e]] for e in self.ENGS if ptr[e] < len(pe_[e])}
            print("DEADLOCK CHECK: stuck =", stuck)
            for e, i in stuck.items():
                print(e, i, ops[i]["r"], ops[i]["w"], [(p, ops[p]["eng"], done[p]) for p in deps[i] if not done[p]])
        sems = {}

        def get_sem(name):
            if name not in sems:
                sems[name] = stack.enter_context(nc.semaphore(name))
            return sems[name]

        cnt = {}
        val = [None] * len(ops)
        for i, o in enumerate(ops):
            if o["slot"] is not None:
                key = "d_" + o["slot"]
                cnt[key] = cnt.get(key, 0) + 16
                val[i] = (key, cnt[key])
            elif needs_inc[i]:
                key = "e_" + o["eng"]
                cnt[key] = cnt.get(key, 0) + 1
                val[i] = (key, cnt[key])
        per_eng = {e: [] for e in self.ENGS}
        for i, o in enumerate(ops):
            per_eng[o["eng"]].append(i)
        block = stack.enter_context(nc.Block())

        def make(engname):
            idxs = per_eng[engname]

            def body(eng):
                waited = {}
                for i in idxs:
                    o = ops[i]
                    need = {}
                    for p in deps[i]:
                        k, v = val[p]
                        if v > need.get(k, 0):
                            need[k] = v
                    for k, v in need.items():
                        if waited.get(k, 0) >= v:
                            continue
                        eng.wait_ge(get_sem(k), v)
                        waited[k] = v
                    if o["fn"] is None:
                        continue
                    ins = o["fn"](eng)
                    if val[i] is not None:
                        k, v = val[i]
                        ins.then_inc(get_sem(k), 16 if o["slot"] is not None else 1)
            return body

        if per_eng["sp"]:
            block.sync(make("sp"))
        if per_eng["act"]:
            block.scalar(make("act"))
        if per_eng["dve"]:
            block.vector(make("dve"))
        if per_eng["pool"]:
            block.gpsimd(make("pool"))
        if per_eng["pe"]:
            block.tensor(make("pe"))


class Arena:
    def __init__(self, base, cap):
        self.base = base
        self.cap = cap
        self.off = 0
        self.marks = []

    def alloc(self, shape, dt):
        esz = 2 if dt == BF16 else 4
        n = int(np.prod(shape[1:])) * esz
        n_al = (n + 63) // 64 * 64
        assert self.off + n_al <= self.cap, f"arena overflow {self.off}+{n_al}>{self.cap}"
        ap = self.base[0:shape[0], self.off:self.off + n].bitcast(dt)
        if len(shape) == 3:
            ap = ap.rearrange("p (a b) -> p a b", a=shape[1])
        self.off += n_al
        return ap

    def mark(self):
        self.marks.append(self.off)

    def release(self):
        self.off = self.marks.pop()


DEBUG = False
PIPE_IDX = True
SKEW1 = True
SKEW2 = True
CHECK_DEADLOCK = False
MERGE_ATT = True
SPLIT_PAIR = False


def build_program():
    nc = bass.Bass("TRN2", target_bir_lowering=False)
    skind = "ExternalOutput" if DEBUG else "Internal"

    def din(name, shape, dt=F32):
        return nc.dram_tensor(name, list(shape), dt, kind="ExternalInput").ap()

    xc = din("xc", [S, D])
    xo = din("xo", [NR, RT, D])
    cs_ctx = din("cs_ctx", [S, 16])
    cs_own = din("cs_own", [NR, RT, 16])
    qpos_d = din("qpos", [128, NR * 5])
    hsc_d = din("hscale", [128, NR])
    iota_d = din("iota", [128, 512])
    pow2_d = din("pow2", [128, NBIS])
    ident_d = din("ident", [128, 128])
    g_d = din("gains", [128, 3 * D])
    w_kvi_d = din("w_kvi", [128, 8 * 1152])
    w_qi_d = din("w_qi", [128, 8 * 1032])
    w_u_d = din("w_u", [128, 8 * 512])
    w_g_d = din("w_g", [128, 8 * 2048])
    poolw_d = din("pool_w", [128, 4 * 128])
    pscale_d = din("pool_scale", [128, 4])
    w_pp_d = din("w_pp", [128, 4 * 1024])
    w_ap_d = din("w_ap", [64, 8 * 1024])
    w_out_d = din("w_out", [128, 8 * 1024])
    w_up_d = din("w_up", [128, 8 * 2 * DFF])
    w_dn_d = din("w_down", [128, NF * 1024])
    convw_d = din("conv_w", [128, 44 * 3])
    convb_d = din("conv_b", [128, 44])
    pm_std_d = din("pm_std", [128, 8 * 128])
    pm_first_d = din("pm_first", [128, NR * 12 * 128])
    out_d = nc.dram_tensor("out", [NR, 512, D], F32, kind="ExternalOutput").ap()
    kt_dram = nc.dram_tensor("kt_scr", [NT, 128, 512], BF16, kind=skind).ap()
    v_dram = nc.dram_tensor("v_scr", [NT, 128, 520], BF16, kind=skind).ap()
    ot_dram = nc.dram_tensor("ot_scr", [NR, 64, 8 * RT], BF16, kind=skind).ap()
    h_dram = nc.dram_tensor("h_scr", [NR, RT, D], F32, kind=skind).ap()

    P = Prog(nc)
    st = ExitStack()
    CAP = 207 * 1024 + 768
    arena_t = st.enter_context(nc.sbuf_tensor("arena", [128, CAP], U8))
    A = Arena(arena_t, CAP)
    psbig = st.enter_context(nc.psum_tensor("psbig", [128, 4096], F32))
    psum = [psbig[:, i * 512:(i + 1) * 512] for i in range(8)]
    psb = [p.bitcast(BF16) for p in psum]

    ident_f = A.alloc([128, 128], F32)
    ident_b = A.alloc([128, 128], BF16)
    ident4 = A.alloc([128, 512], BF16)
    gmix = A.alloc([128, D], F32)
    qpos = A.alloc([128, NR * 5], F32)
    hsc = A.alloc([128, NR], F32)
    iota = A.alloc([128, 512], F32)
    pow2 = A.alloc([128, NBIS], F32)
    ones_b = A.alloc([128, 64], BF16)
    hl = A.alloc([128, 1024], BF16)
    cneg = A.alloc([128, 2], F32)
    P.dma("sp", ident_f, ident_d, w=["ident_f", "cchain"], slot="c")
    P.dma("sp", gmix, g_d[:, 0:D], w=["gains", "cchain"], slot="c")
    P.dma("sp", qpos, qpos_d, w=["qpos", "cchain"], slot="c")
    P.dma("sp", hsc, hsc_d, w=["hsc", "cchain"], slot="c")
    P.dma("sp", iota, iota_d, w=["iota", "cchain"], slot="c")
    P.dma("sp", pow2, pow2_d, w=["pow2", "cchain"], slot="c")
    P.op("dve", lambda e: e.tensor_copy(out=ident_b, in_=ident_f), r=["ident_f"], w=["ident_b"])
    P.op("pool", lambda e: e.tensor_copy(out=ident4.rearrange("p (a b) -> p a b", a=4),
                                         in_=ident_f[:, None, :].to_broadcast([128, 4, 128])),
         r=["ident_f"], w=["ident4"])
    P.op("pool", lambda e: e.memset(ones_b, 1.0), w=["ones_b"])
    P.op("pool", lambda e: e.memset(cneg[:, 0:1], -0.5), w=["cneg"])
    A.mark()

    wst_i = [0]

    def load_weight(dram2d, dst2d, ncols, nparts, wst, key):
        c0 = 0
        while c0 < ncols:
            n = min(2048, ncols - c0)
            s = wst_i[0] % 2
            wst_i[0] += 1
            stg = wst[s]
            P.dma("sp", stg[0:nparts, 0:n], dram2d[0:nparts, c0:c0 + n], w=[f"wst{s}"], slot=f"wst{s}")
            eng = "dve"
            P.op(eng, (lambda e, a=dst2d[0:nparts, c0:c0 + n], b=stg[0:nparts, 0:n]: e.tensor_copy(out=a, in_=b)),
                 r=[f"wst{s}"], w=[key])
            c0 += n

    def barrier():
        dummy = cneg[:, 1:2]
        P.op("pool", lambda e: e.memset(dummy, 0.0), barrier=True)

    def norm_tile(x_ap, xkey, gain, out_bf, okey, tmp, tkey, junk, jkey):
        P.op("act", lambda e: e.activation(out=junk, in_=x_ap, func=AF.Square, accum_out=tmp[:, 0:1]),
             r=[xkey], w=[jkey, tkey + "0"])
        P.op("dve", lambda e: e.tensor_scalar(out=tmp[:, 1:2], in0=tmp[:, 0:1], scalar1=1.0 / D, scalar2=EPS,
                                              op0=ALU.mult, op1=ALU.add), r=[tkey + "0"], w=[tkey + "1"])
        P.op("pool", lambda e: e.tensor_tensor(out=tmp[:, 2:3], in0=tmp[:, 1:2], in1=cneg[:, 0:1], op=ALU.pow),
             r=[tkey + "1", "cneg"], w=[tkey + "2"])
        P.op("dve", lambda e: e.scalar_tensor_tensor(out=out_bf, in0=x_ap, scalar=tmp[:, 2:3],
                                                     in1=gain, op0=ALU.mult, op1=ALU.mult),
             r=[xkey, tkey + "2", "gains"], w=[okey])

    def transpose_to(src_bf, skey, ncol_blocks, ps_i, dst3, dkey, eng="act"):
        if isinstance(ps_i, tuple):
            pb, pkey = ps_i
        else:
            pb, pkey = psb[ps_i], f"ps{ps_i}"
        for kc in range(ncol_blocks):
            P.op("pe", (lambda e, kc=kc: e.transpose(out=pb[:, kc * 128:(kc + 1) * 128],
                                                    in_=src_bf[:, kc * 128:(kc + 1) * 128], identity=ident_b)),
                 r=[skey, "ident_b"], w=[pkey])
        src3 = pb[:, 0:ncol_blocks * 128].rearrange("p (a b) -> p a b", a=ncol_blocks)
        if eng == "act":
            P.op("act", lambda e: e.activation(out=dst3, in_=src3, func=AF.Copy), r=[pkey], w=[dkey])
        else:
            P.op("dve", lambda e: e.tensor_copy(out=dst3, in_=src3), r=[pkey], w=[dkey])

    def proj_tok(xT3, xkey, tcol0, W3, wkey, c0, n, ps_i):
        if isinstance(ps_i, tuple):
            po, pkey = ps_i
        else:
            po, pkey = psum[ps_i][:, 0:n], f"ps{ps_i}"
        for kc in range(8):
            P.op("pe", (lambda e, kc=kc: e.matmul(out=po, lhsT=xT3[:, kc, tcol0:tcol0 + 128],
                                                 rhs=W3[:, kc, c0:c0 + n], start=(kc == 0), stop=(kc == 7))),
                 r=[xkey, wkey], w=[pkey])

    def rope(src3, skeys, H, cs, cskey, dst3, dkey, rt, rkey):
        cosb = cs[:, None, 0:8].to_broadcast([128, H, 8])
        sinb = cs[:, None, 8:16].to_broadcast([128, H, 8])
        x1 = src3[:, :, 0:8]
        x2 = src3[:, :, 8:16]
        t = [rt[:, i * 64:i * 64 + H * 8].rearrange("p (a b) -> p a b", a=H) for i in range(4)]
        rr = list(skeys) + [cskey]
        P.op("dve", lambda e: e.tensor_tensor(out=t[0], in0=x1, in1=cosb, op=ALU.mult), r=rr, w=[rkey + "0"])
        P.op("dve", lambda e: e.tensor_tensor(out=t[1], in0=x2, in1=sinb, op=ALU.mult), r=rr, w=[rkey + "1"])
        P.op("dve", lambda e: e.tensor_tensor(out=t[2], in0=x2, in1=cosb, op=ALU.mult), r=rr, w=[rkey + "2"])
        P.op("dve", lambda e: e.tensor_tensor(out=t[3], in0=x1, in1=sinb, op=ALU.mult), r=rr, w=[rkey + "3"])
        P.op("dve", lambda e: e.tensor_tensor(out=dst3[:, :, 0:8], in0=t[0], in1=t[1], op=ALU.subtract),
             r=[rkey + "0", rkey + "1"], w=[dkey])
        P.op("dve", lambda e: e.tensor_tensor(out=dst3[:, :, 8:16], in0=t[2], in1=t[3], op=ALU.add),
             r=[rkey + "2", rkey + "3"], w=[dkey])
        P.op("act", lambda e: e.activation(out=dst3[:, :, 16:64], in_=src3[:, :, 16:64], func=AF.Copy),
             r=list(skeys), w=[dkey])

    A.mark()
    WA = A.alloc([128, 8, 1152], BF16)
    ikT = A.alloc([128, S], BF16)
    QT_r = A.alloc([128, 4, RT], BF16)
    iqT_r = A.alloc([128, 4, RT], BF16)
    sgn = A.alloc([128, 5 * 8], F32)
    wabs2 = [A.alloc([128, 8], F32) for _ in range(2)]
    qrel = A.alloc([128, 16], F32)
    bis = A.alloc([128, 8 + NBIS], F32)
    ntmp = [A.alloc([128, 4], F32) for _ in range(3)]
    cst = [A.alloc([128, 16], F32) for _ in range(4)]
    oT_r = A.alloc([64, 8, RT], BF16)
    QTz = A.alloc([128, 8, RT], BF16)
    kchunk = [A.alloc([128, 4, 512], BF16) for _ in range(2)]
    vchunk = [A.alloc([128, 4, 520], BF16) for _ in range(2)]
    diag = [A.alloc([128, 8, 128], BF16) for _ in range(2)]
    biasb = [A.alloc([128, 512], F32) for _ in range(1)]
    pT = [A.alloc([128, 512], BF16) for _ in range(2)]
    osb = [A.alloc([128, 512], F32) for _ in range(4)]
    rbuf = [A.alloc([128, 1024], BF16) for _ in range(2)]
    m0 = A.off
    nmask = [A.alloc([128, S], BF16) for _ in range(2)]
    s0 = A.off
    score2 = [A.alloc([128, S], F32) for _ in range(2)]
    SA = Arena(arena_t, s0)
    SA.off = m0
    xin = [SA.alloc([128, D], F32) for _ in range(3)]
    xnh = [SA.alloc([128, D], BF16) for _ in range(3)]
    ksb = [SA.alloc([128, 8, 64], BF16) for _ in range(4)]
    iksb = [SA.alloc([128, 2, 64], BF16) for _ in range(2)]
    iqs2 = [SA.alloc([128, 8, 64], F32) for _ in range(2)]
    junk1 = SA.alloc([128, D], BF16)
    SB = Arena(arena_t, s0 + 65536)
    SB.off = s0
    ropet = [SB.alloc([128, 256], F32) for _ in range(4)]
    vaug = [SB.alloc([128, 8, 65], BF16) for _ in range(2)]
    kTs = [SB.alloc([128, 4, 128], BF16) for _ in range(2)]
    xnT_r = SB.alloc([128, 8, RT], BF16)
    wst = [SB.alloc([128, 2048], F32) for _ in range(2)]
    xnT1 = [xnT_r[:, :, 0:128], xnT_r[:, :, 128:256], xnT_r[:, :, 256:384]]

    load_weight(w_kvi_d, WA.rearrange("p a b -> p (a b)"), 8 * 1152, 128, wst, "WA")
    for s in range(2):
        P.op("pool", (lambda e, s=s: e.memset(vaug[s].rearrange("p a b -> p (a b)"), 1.0)), w=[f"vaug{s}"])
    P.op("pool", lambda e: e.memset(QTz.rearrange("p a b -> p (a b)"), 0.0), w=["QTz"])

    def p1_stageA1(i):
        s3 = i % 3
        s4 = i % 4
        P.dma("sp", xin[s3], xc[i * 128:(i + 1) * 128, :], w=[f"xin{s3}"], slot=f"xin{s3}")
        P.dma("sp", cst[s4], cs_ctx[i * 128:(i + 1) * 128, :], w=[f"cst{s4}"], slot=f"cst{s4}")
        norm_tile(xin[s3], f"xin{s3}", gmix, xnh[s3], f"xnh{s3}", ntmp[s3], f"nt{s3}", junk1, "junk1")

    def p1_stageA2(i):
        s3 = i % 3
        transpose_to(xnh[s3], f"xnh{s3}", 8, i % 2, xnT1[s3], f"xnT1{s3}")

    def p1_stageB(i):
        s = i % 2
        s3 = i % 3
        ikps = (psum[6][:, s * 128:(s + 1) * 128], f"ps6i{s}")
        proj_tok(xnT_r, f"xnT1{s3}", s3 * 128, WA, "WA", 0, 512, 2 + s)
        proj_tok(xnT_r, f"xnT1{s3}", s3 * 128, WA, "WA", 512, 512, 4 + s)
        proj_tok(xnT_r, f"xnT1{s3}", s3 * 128, WA, "WA", 1024, 128, ikps)
        rope(psum[2 + s].rearrange("p (a b) -> p a b", a=8), [f"ps{2 + s}"], 8, cst[i % 4], f"cst{i % 4}", ksb[s],
             f"ksb{s}", ropet[s], f"rt{s}")
        P.op("act", (lambda e, s=s: e.activation(out=vaug[s][:, :, 0:64],
                                                 in_=psum[4 + s].rearrange("p (a b) -> p a b", a=8), func=AF.Copy)),
             r=[f"ps{4 + s}"], w=[f"vaug{s}"])
        rope(ikps[0].rearrange("p (a b) -> p a b", a=2), [ikps[1]], 2, cst[i % 4], f"cst{i % 4}", iksb[s], f"iksb{s}",
             ropet[2 + s], f"rt{2 + s}")
        transpose_to(ksb[s].rearrange("p a b -> p (a b)"), f"ksb{s}", 4,
                     (psb[7][:, s * 512:(s + 1) * 512], f"ps7{s}"), kTs[s], f"kTs{s}", eng="dve")
        transpose_to(iksb[s].rearrange("p a b -> p (a b)"), f"iksb{s}", 1,
                     (psb[6][:, 512 + s * 128:512 + (s + 1) * 128], f"ps6t{s}"),
                     ikT[:, i * 128:(i + 1) * 128].rearrange("p (a b) -> p a b", a=1), "ikT", eng="dve")
        P.dma("pool", kt_dram[i], kTs[s].rearrange("p a b -> p (a b)"), r=[f"kTs{s}"], w=["kt_dram"], slot=f"kst{s}")
        P.dma("pool", v_dram[i], vaug[s].rearrange("p a b -> p (a b)"), r=[f"vaug{s}"], w=["v_dram"], slot=f"vst{s}")

    p1_stageA1(0)
    p1_stageA1(1)
    p1_stageA1(2)
    p1_stageA2(0)
    p1_stageA2(1)
    for i in range(NT):
        if i + 3 < NT:
            p1_stageA1(i + 3)
        if i + 2 < NT:
            p1_stageA2(i + 2)
        p1_stageB(i)

    barrier()
    load_weight(w_qi_d, WA.rearrange("p a b -> p (a b)")[:, 0:8 * 1032], 8 * 1032, 128, wst, "WA")
    WQ = WA.rearrange("p a b -> p (a b)")[:, 0:8 * 1032].rearrange("p (a b) -> p a b", a=8)

    for r in range(NR):
        L = LR[r]
        KT = 4 * L
        NK = 512 * L
        barrier()
        def p2_stageA1(t, r=r):
            s3 = t % 3
            s4 = t % 4
            P.dma("sp", xin[s3], xo[r, t * 128:(t + 1) * 128, :], w=[f"xin{s3}"], slot=f"xin{s3}")
            P.dma("sp", cst[s4], cs_own[r, t * 128:(t + 1) * 128, :], w=[f"cst{s4}"], slot=f"cst{s4}")
            norm_tile(xin[s3], f"xin{s3}", gmix, xnh[s3], f"xnh{s3}", ntmp[s3], f"nt{s3}", junk1, "junk1")

        def p2_stageA2(t, r=r):
            s3 = t % 3
            transpose_to(xnh[s3], f"xnh{s3}", 8, t % 2, xnT_r[:, :, t * 128:(t + 1) * 128], f"xnT_r{t}")

        def p2_stageB(t, r=r):
            s = t % 2
            s3 = t % 3
            iwps = (psum[6][:, s * 8:(s + 1) * 8], f"ps6i{s}")
            proj_tok(xnT_r, f"xnT_r{t}", t * 128, WQ, "WA", 0, 512, 2 + s)
            proj_tok(xnT_r, f"xnT_r{t}", t * 128, WQ, "WA", 512, 512, 4 + s)
            proj_tok(xnT_r, f"xnT_r{t}", t * 128, WQ, "WA", 1024, 8, iwps)
            wab = wabs2[s]
            iqs = iqs2[s]
            P.op("act", (lambda e, wab=wab, iwps=iwps: e.activation(out=wab, in_=iwps[0], func=AF.Abs, scale=CS_IDX)),
                 r=[iwps[1]], w=[f"wabs{s}"])
            P.op("act", (lambda e, t=t, iwps=iwps: e.activation(out=sgn[:, t * 8:(t + 1) * 8], in_=iwps[0],
                                                                func=AF.Sign)), r=[iwps[1]], w=["sgn"])
            rope(psum[2 + s].rearrange("p (a b) -> p a b", a=8), [f"ps{2 + s}"], 8, cst[t % 4], f"cst{t % 4}", ksb[s],
                 f"ksb{s}", ropet[s], f"rt{s}")
            P.op("dve", (lambda e, s=s, wab=wab, iqs=iqs: e.tensor_tensor(
                out=iqs, in0=psum[4 + s].rearrange("p (a b) -> p a b", a=8),
                in1=wab[:, :, None].to_broadcast([128, 8, 64]), op=ALU.mult)),
                r=[f"ps{4 + s}", f"wabs{s}"], w=[f"iqs{s}"])
            rope(iqs, [f"iqs{s}"], 8, cst[t % 4], f"cst{t % 4}", ksb[2 + s], f"ksb{2 + s}", ropet[2 + s], f"rt{2 + s}")
            for kc in range(4):
                P.op("pe", (lambda e, kc=kc, s=s: e.transpose(
                    out=psb[7][:, kc * 128:(kc + 1) * 128],
                    in_=ksb[s].rearrange("p a b -> p (a b)")[:, kc * 128:(kc + 1) * 128], identity=ident_b)),
                    r=[f"ksb{s}", "ident_b"], w=["ps7q"])
            for half in range(2):
                P.op("dve", (lambda e, half=half, t=t: e.tensor_copy(
                    out=QTz[half * 64:(half + 1) * 64, :, t * 128:(t + 1) * 128].rearrange(
                        "p (a two) b -> p a two b", two=2)[:, :, half, :],
                    in_=psb[7][half * 64:(half + 1) * 64, 0:512].rearrange("p (a b) -> p a b", a=4))),
                    r=["ps7q"], w=["QTz"])
            transpose_to(ksb[2 + s].rearrange("p a b -> p (a b)"), f"ksb{2 + s}", 4,
                         (psb[7][:, 512:1024], "ps7i"), iqT_r[:, :, t * 128:(t + 1) * 128], "iqT_r", eng="dve")

        p2_stageA1(0)
        p2_stageA1(1)
        p2_stageA1(2)
        p2_stageA2(0)
        p2_stageA2(1)
        for t in range(5):
            if t + 3 < 5:
                p2_stageA1(t + 3)
            if t + 2 < 5:
                p2_stageA2(t + 2)
            p2_stageB(t)

        barrier()
        cb0 = max(0, 4 * r - 1)

        def indexer_scores(qb, r=r, L=L, KT=KT, NK=NK, cb0=cb0):
            sbi = qb % 2
            score = score2[sbi]
            dg = diag[sbi]
            for h in range(8):
                P.op("pool", (lambda e, h=h, dg=dg: e.tensor_scalar(
                    out=dg[:, h, :], in0=ident_b, scalar1=sgn[:, qb * 8 + h:qb * 8 + h + 1], scalar2=0.0,
                    op0=ALU.mult, op1=ALU.add)), r=["ident_b", "sgn"], w=[f"diag{sbi}"])
            pairs = [(c, hp) for c in range(L) for hp in range(4)]

            def emit_idx(pi):
                c, hp = pairs[pi]
                pb = pi % 2
                for half in range(2):
                    bank = pb * 2 + half
                    P.op("pe", (lambda e, half=half, hp=hp, c=c, bank=bank: e.matmul(
                        out=psum[bank],
                        lhsT=iqT_r[half * 64:(half + 1) * 64, hp, qb * 128:(qb + 1) * 128],
                        rhs=ikT[half * 64:(half + 1) * 64, c * 512:(c + 1) * 512], start=True, stop=True)),
                        r=["iqT_r", "ikT"], w=[f"ps{bank}"])
                P.op("act", (lambda e, pb=pb: e.activation(out=rbuf[pb], in_=psbig[:, pb * 1024:(pb + 1) * 1024],
                                                           func=AF.Relu)),
                     r=[f"ps{pb * 2}", f"ps{pb * 2 + 1}"], w=[f"rbuf{pb}"])

            def emit_diag(pi):
                c, hp = pairs[pi]
                pb = pi % 2
                sbank = 4 + c % 2
                for half in range(2):
                    h = hp * 2 + half
                    P.op("pe", (lambda e, h=h, half=half, pb=pb, sbank=sbank, dg=dg: e.matmul(
                        out=psum[sbank], lhsT=dg[:, h, :], rhs=rbuf[pb][:, half * 512:(half + 1) * 512],
                        start=(h == 0), stop=(h == 7))),
                        r=[f"diag{sbi}", f"rbuf{pb}"], w=[f"ps{sbank}"])
                if hp == 3:
                    sc = score[:, c * 512:(c + 1) * 512]
                    P.op("act", (lambda e, sc=sc, sbank=sbank: e.activation(out=sc, in_=psum[sbank], func=AF.Copy)),
                         r=[f"ps{sbank}"], w=[f"score{sbi}_{c}"])

            if PIPE_IDX:
                emit_idx(0)
                for pi in range(len(pairs)):
                    if pi + 1 < len(pairs):
                        emit_idx(pi + 1)
                    emit_diag(pi)
            else:
                for pi in range(len(pairs)):
                    emit_idx(pi)
                    emit_diag(pi)
            allsc = [f"score{sbi}_{c}" for c in range(L)]
            P.op("dve", lambda e: e.tensor_reduce(out=bis[:, 0:1], in_=score[:, 0:NK], axis=AX.X, op=ALU.max,
                                                  apply_absolute_value=True), r=allsc, w=["bis_am"])
            qp = qpos[:, r * 5 + qb:r * 5 + qb + 1]
            for c in range(cb0, L):
                P.op("dve", (lambda e, c=c: e.tensor_scalar(out=qrel[:, c:c + 1], in0=qp, scalar1=float(-512 * c),
                                                            scalar2=None, op0=ALU.add)),
                     r=["qpos"], w=["qrel"])
                bb = biasb[0]
                P.op("dve", (lambda e, c=c, bb=bb: e.tensor_scalar(out=bb, in0=iota, scalar1=qrel[:, c:c + 1],
                                                                  scalar2=NEG, op0=ALU.is_gt, op1=ALU.mult)),
                     r=["iota", "qrel"], w=["biasb0"])
                sc = score[:, c * 512:(c + 1) * 512]
                P.op("dve", (lambda e, sc=sc, bb=bb: e.tensor_tensor(out=sc, in0=sc, in1=bb, op=ALU.add)),
                     r=["biasb0", f"score{sbi}_{c}", "bis_am"], w=[f"score{sbi}_{c}"])

        def bisect_block(qb, r=r, L=L, KT=KT, NK=NK):
            sbi = qb % 2
            score = score2[sbi]
            nm = nmask[sbi]
            nmk = f"nmask{sbi}"
            allsc = [f"score{sbi}_{c}" for c in range(L)]
            P.op("dve", lambda e: e.tensor_scalar(out=bis[:, 2:3], in0=bis[:, 0:1], scalar1=2.000001, scalar2=1e-30,
                                                  op0=ALU.mult, op1=ALU.add), r=["bis_am"], w=["bis_w0"])
            P.op("dve", lambda e: e.tensor_scalar(out=bis[:, 8:8 + NBIS], in0=pow2, scalar1=bis[:, 2:3], scalar2=None,
                                                  op0=ALU.mult), r=["pow2", "bis_w0"], w=["bis_wi"])
            P.op("dve", lambda e: e.tensor_tensor(out=bis[:, 3:4], in0=bis[:, 8:9], in1=bis[:, 0:1], op=ALU.subtract),
                 r=["bis_wi", "bis_am"], w=["bis_mid"])
            for it in range(NBIS):
                wi = bis[:, 8 + it:9 + it]
                P.op("dve", lambda e: e.tensor_scalar(out=nm[:, 0:NK], in0=score[:, 0:NK], scalar1=bis[:, 3:4],
                                                      scalar2=None, op0=ALU.is_ge, op1=ALU.add,
                                                      accum_out=bis[:, 4:5]),
                     r=allsc + ["bis_mid"], w=[nmk, "bis_cnt"])
                P.op("dve", lambda e: e.tensor_scalar(out=bis[:, 5:6], in0=bis[:, 4:5], scalar1=255.5,
                                                      scalar2=-0.5, op0=ALU.is_ge, op1=ALU.add),
                     r=["bis_cnt"], w=["bis_t"])
                P.op("dve", (lambda e, wi=wi: e.scalar_tensor_tensor(out=bis[:, 3:4], in0=bis[:, 5:6], scalar=wi,
                                                                    in1=bis[:, 3:4], op0=ALU.mult, op1=ALU.add)),
                     r=["bis_t", "bis_wi", "bis_mid"], w=["bis_mid"])
            P.op("dve", lambda e: e.scalar_tensor_tensor(out=bis[:, 1:2], in0=bis[:, 8 + NBIS - 1:8 + NBIS],
                                                         scalar=-0.5, in1=bis[:, 3:4], op0=ALU.mult, op1=ALU.add),
                 r=["bis_wi", "bis_mid"], w=["bis_lo"])
            P.op("dve", lambda e: e.tensor_scalar(out=nm[:, 0:NK], in0=score[:, 0:NK], scalar1=bis[:, 1:2],
                                                  scalar2=-30000.0, op0=ALU.is_lt, op1=ALU.mult),
                 r=allsc + ["bis_lo"], w=[nmk])

        def attention_block(qb, L=L, KT=KT):
            nm = nmask[qb % 2]
            nmk = f"nmask{qb % 2}"
            q0 = qb * 128
            groups = []
            for c4 in range(L):
                for j in range(4):
                    for hg in range(2):
                        groups.append((c4, j, hg))

            def load_chunk(c4):
                s3 = c4 % 2
                P.dma("sp", kchunk[s3], kt_dram[4 * c4:4 * c4 + 4].rearrange("t p c -> p t c"), r=["kt_dram"],
                      w=[f"kch{s3}"], slot=f"kch{s3}")
                P.dma("sp", vchunk[s3], v_dram[4 * c4:4 * c4 + 4].rearrange("t p c -> p t c"), r=["v_dram"],
                      w=[f"vch{s3}"], slot=f"vch{s3}")

            def emit_scores(gi):
                c4, j, hg = groups[gi]
                if j == 0 and hg == 0:
                    load_chunk(c4)
                s3 = c4 % 2
                kt = 4 * c4 + j
                bank = 4 + gi % 2
                pbuf = gi % 2
                P.op("pe", (lambda e, kt=kt, bank=bank: e.matmul(
                    out=psum[bank], lhsT=nm[:, kt * 128:(kt + 1) * 128], rhs=ident4, start=True, stop=False)),
                    r=[nmk, "ident4"], w=[f"ps{bank}"])
                for pp in range(2):
                    p = hg * 2 + pp
                    P.op("pe", (lambda e, pp=pp, p=p, j=j, s3=s3, bank=bank: e.matmul(
                        out=psum[bank][:, pp * 256:(pp + 1) * 256],
                        lhsT=kchunk[s3][:, j, p * 128:(p + 1) * 128],
                        rhs=QTz[:, 2 * p:2 * p + 2, q0:q0 + 128], start=False, stop=(pp == 1))),
                        r=[f"kch{s3}", "QTz"], w=[f"ps{bank}"])
                P.op("act", (lambda e, bank=bank, pbuf=pbuf: e.activation(out=pT[pbuf], in_=psum[bank],
                                                                         func=AF.Exp, scale=0.125)),
                     r=[f"ps{bank}"], w=[f"pT{pbuf}"])

            def emit_pv(gi):
                c4, j, hg = groups[gi]
                s3 = c4 % 2
                kt = 4 * c4 + j
                pbuf = gi % 2
                for hh in range(4):
                    h = hg * 4 + hh
                    P.op("pe", (lambda e, hh=hh, h=h, j=j, s3=s3, pbuf=pbuf, hg=hg, kt=kt: e.matmul(
                        out=psum[6 + hg][0:65, hh * 128:(hh + 1) * 128],
                        lhsT=vchunk[s3][:, j, h * 65:(h + 1) * 65],
                        rhs=pT[pbuf][:, hh * 128:(hh + 1) * 128],
                        start=(kt == 0 and hh == 0), stop=(kt == KT - 1))),
                        r=[f"vch{s3}", f"pT{pbuf}"], w=[f"ps{6 + hg}"])

            ng = len(groups)
            emit_scores(0)
            for gi in range(ng):
                if gi + 1 < ng:
                    emit_scores(gi + 1)
                emit_pv(gi)
            for hg in range(2):
                oi = (qb % 2) * 2 + hg
                ob = osb[oi]
                P.op("act", (lambda e, ob=ob, hg=hg: e.activation(out=ob[0:65, :], in_=psum[6 + hg][0:65, :],
                                                                  func=AF.Copy)),
                     r=[f"ps{6 + hg}"], w=[f"osb{oi}"])

        def attention_fin(qb):
            q0 = qb * 128
            for hg in range(2):
                oi = (qb % 2) * 2 + hg
                ob = osb[oi]
                P.op("dve", (lambda e, ob=ob: e.reciprocal(out=ob[64:65, :], in_=ob[64:65, :])),
                     r=[f"osb{oi}"], w=[f"osb{oi}"])
                P.op("dve", (lambda e, ob=ob: e.tensor_copy(out=hl[64:65, 0:512], in_=ob[64:65, :])),
                     r=[f"osb{oi}"], w=["hl_hi"])
                P.op("dve", (lambda e, ob=ob: e.tensor_tensor(out=hl[64:65, 512:1024], in0=ob[64:65, :],
                                                              in1=hl[64:65, 0:512], op=ALU.subtract)),
                     r=[f"osb{oi}", "hl_hi"], w=["hl_lo"])
                P.op("pe", (lambda e, hg=hg: e.matmul(out=psum[4 + hg][0:64, :], lhsT=ones_b[64:65, 0:64],
                                                      rhs=hl[64:65, 0:512], start=True, stop=False)),
                     r=["ones_b", "hl_hi"], w=[f"ps{4 + hg}"])
                P.op("pe", (lambda e, hg=hg: e.matmul(out=psum[4 + hg][0:64, :], lhsT=ones_b[64:65, 0:64],
                                                      rhs=hl[64:65, 512:1024], start=False, stop=True)),
                     r=["ones_b", "hl_lo"], w=[f"ps{4 + hg}"])
                P.op("dve", (lambda e, ob=ob, hg=hg: e.tensor_tensor(
                    out=oT_r[0:64, hg * 4:(hg + 1) * 4, q0:q0 + 128],
                    in0=ob[0:64, :].rearrange("p (a b) -> p a b", a=4),
                    in1=psum[4 + hg][0:64, :].rearrange("p (a b) -> p a b", a=4), op=ALU.mult)),
                    r=[f"osb{oi}", f"ps{4 + hg}"], w=["oT_r"])

        indexer_scores(0)
        for qb in range(5):
            if qb > 1:
                attention_fin(qb - 2)
            bisect_block(qb)
            if qb + 1 < 5:
                indexer_scores(qb + 1)
            if qb > 0:
                attention_block(qb - 1)
        attention_fin(3)
        attention_block(4)
        attention_fin(4)
        P.dma("pool", ot_dram[r], oT_r.rearrange("p a b -> p (a b)"), r=["oT_r"], w=["ot_dram"], slot="otst")

    barrier()
    A.release()
    A.mark()
    Wu = A.alloc([128, 8, 512], BF16)
    Wg = A.alloc([128, 8, 2048], BF16)
    Wpp = A.alloc([128, 4, 1024], BF16)
    Wap = A.alloc([64, 8, 1024], BF16)
    Wo = A.alloc([128, 8, 1024], BF16)
    poolw = A.alloc([128, 4, 128], BF16)
    pscale = A.alloc([128, 4], F32)
    pm_std = A.alloc([128, 8, 128], BF16)
    pm_first = A.alloc([128, NR * 12, 128], BF16)
    xr = [A.alloc([128, D], F32) for _ in range(5)]
    x0 = A.off - 5 * 4096
    SX = Arena(arena_t, A.off)
    SX.off = x0
    wst3 = [SX.alloc([128, 2048], F32) for _ in range(2)]
    xnh3 = [A.alloc([128, D], BF16) for _ in range(3)]
    ntmp3 = [A.alloc([128, 4], F32) for _ in range(3)]
    junk3 = A.alloc([128, D], BF16)
    xnT3b = A.alloc([128, 8, RT], BF16)
    u_sb = A.alloc([128, 5, 512], BF16)
    pooledT = A.alloc([128, 4, RT], BF16)
    mixedT = A.alloc([128, 4, RT], BF16)
    oT3b = A.alloc([64, 8, RT], BF16)
    mergedT = A.alloc([128, 8, RT], BF16)
    sg0 = A.alloc([128, 512], F32)
    sg1 = A.alloc([128, 512], F32)
    tt0 = A.alloc([128, 512], F32)
    tt1 = A.alloc([128, 512], F32)
    hb = [A.alloc([128, D], F32) for _ in range(2)]

    P.dma("sp", pscale, pscale_d, w=["pscale", "cchain"], slot="c")
    load_weight(w_u_d, Wu.rearrange("p a b -> p (a b)"), 8 * 512, 128, wst3, "Wu")
    load_weight(w_g_d, Wg.rearrange("p a b -> p (a b)"), 8 * 2048, 128, wst3, "Wg")
    load_weight(w_pp_d, Wpp.rearrange("p a b -> p (a b)"), 4 * 1024, 128, wst3, "Wpp")
    load_weight(w_ap_d, Wap.rearrange("p a b -> p (a b)"), 8 * 1024, 64, wst3, "Wap")
    load_weight(w_out_d, Wo.rearrange("p a b -> p (a b)"), 8 * 1024, 128, wst3, "Wo")
    load_weight(poolw_d, poolw.rearrange("p a b -> p (a b)"), 4 * 128, 128, wst3, "poolw")
    load_weight(pm_std_d, pm_std.rearrange("p a b -> p (a b)"), 8 * 128, 128, wst3, "pm_std")
    load_weight(pm_first_d, pm_first.rearrange("p a b -> p (a b)"), NR * 12 * 128, 128, wst3, "pm_first")
    barrier()

    slabs = [(0, 512), (512, 128)]
    for r in range(NR):
        P.dma("sp", oT3b.rearrange("p a b -> p (a b)"), ot_dram[r], r=["ot_dram"], w=["oT3"], slot="otl")
        def p3_A1(t, r=r):
            s3 = t % 3
            P.dma("sp", xr[t], xo[r, t * 128:(t + 1) * 128, :], w=[f"xr{t}"], slot=f"xr{t}")
            norm_tile(xr[t], f"xr{t}", gmix, xnh3[s3], f"xnh3{s3}", ntmp3[s3], f"nt{s3}", junk3, "junk3")

        def p3_A2(t):
            s3 = t % 3
            transpose_to(xnh3[s3], f"xnh3{s3}", 8, t % 2, xnT3b[:, :, t * 128:(t + 1) * 128], f"xnT3_{t}")

        def p3_B(t, r=r):
            proj_tok(xnT3b, f"xnT3_{t}", t * 128, Wu, "Wu", 0, 512, 2)
            P.op("act", (lambda e, t=t: e.activation(out=u_sb[:, t, :], in_=psum[2][:, 0:512], func=AF.Copy)),
                 r=["ps2"], w=[f"u{t}"])
            for g in range(4):
                mats = []
                if t == 1:
                    mats.append((t, pm_first[:, r * 12 + g, :], "pm_first"))
                    mats.append((t, pm_first[:, r * 12 + 4 + g, :], "pm_first"))
                    mats.append((t - 1, pm_first[:, r * 12 + 8 + g, :], "pm_first"))
                else:
                    mats.append((t, pm_std[:, g, :], "pm_std"))
                    if t > 0:
                        mats.append((t - 1, pm_std[:, 4 + g, :], "pm_std"))
                for mi, (tu, M, mk) in enumerate(mats):
                    P.op("pe", (lambda e, tu=tu, M=M, g=g, mi=mi, nm=len(mats): e.matmul(
                        out=psum[3][:, g * 128:(g + 1) * 128], lhsT=u_sb[:, tu, g * 128:(g + 1) * 128], rhs=M,
                        start=(mi == 0), stop=(mi == nm - 1))),
                        r=[f"u{tu}", mk], w=["ps3"])
            P.op("act", (lambda e, t=t: e.activation(out=pooledT[:, :, t * 128:(t + 1) * 128],
                                                     in_=psum[3].rearrange("p (a b) -> p a b", a=4),
                                                     func=AF.Copy)), r=["ps3"], w=["pooledT"])

        p3_A1(0)
        p3_A1(1)
        p3_A1(2)
        p3_A2(0)
        p3_A2(1)
        for t in range(5):
            if t + 3 < 5:
                p3_A1(t + 3)
            if t + 2 < 5:
                p3_A2(t + 2)
            p3_B(t)
        for g in range(4):
            for (c0, n) in slabs:
                P.op("pe", (lambda e, g=g, c0=c0, n=n: e.matmul(out=psum[2][:, 0:n], lhsT=poolw[:, g, :],
                                                               rhs=pooledT[:, g, c0:c0 + n], start=True, stop=True)),
                     r=["poolw", "pooledT"], w=["ps2"])
                P.op("dve", (lambda e, g=g, c0=c0, n=n: e.tensor_scalar(out=mixedT[:, g, c0:c0 + n],
                                                                       in0=psum[2][:, 0:n],
                                                                       scalar1=pscale[:, g:g + 1], scalar2=None,
                                                                       op0=ALU.mult)),
                     r=["ps2", "pscale"], w=["mixedT"])
        for c in range(8):
            for (c0, n) in slabs:
                for g in range(4):
                    P.op("pe", (lambda e, g=g, c=c, c0=c0, n=n: e.matmul(
                        out=psum[4][:, 0:n], lhsT=Wpp[:, g, c * 128:(c + 1) * 128], rhs=mixedT[:, g, c0:c0 + n],
                        start=(g == 0), stop=(g == 3))), r=["Wpp", "mixedT"], w=["ps4"])
                for h in range(8):
                    P.op("pe", (lambda e, h=h, c=c, c0=c0, n=n: e.matmul(
                        out=psum[5][:, 0:n], lhsT=Wap[0:64, h, c * 128:(c + 1) * 128], rhs=oT3b[0:64, h, c0:c0 + n],
                        start=(h == 0), stop=(h == 7))), r=["Wap", "oT3"], w=["ps5"])
                for gi in range(2):
                    for kc in range(8):
                        P.op("pe", (lambda e, gi=gi, kc=kc, c=c, c0=c0, n=n: e.matmul(
                            out=psum[6 + gi][:, 0:n],
                            lhsT=Wg[:, kc, gi * 1024 + c * 128:gi * 1024 + (c + 1) * 128],
                            rhs=xnT3b[:, kc, c0:c0 + n], start=(kc == 0), stop=(kc == 7))),
                            r=["Wg"] + [f"xnT3_{tt}" for tt in range(5)], w=[f"ps{6 + gi}"])
                P.op("act", (lambda e, n=n: e.activation(out=sg0[:, 0:n], in_=psum[6][:, 0:n], func=AF.Sigmoid)),
                     r=["ps6"], w=["sg0"])
                P.op("act", (lambda e, n=n: e.activation(out=sg1[:, 0:n], in_=psum[7][:, 0:n], func=AF.Sigmoid)),
                     r=["ps7"], w=["sg1"])
                P.op("dve", (lambda e, n=n: e.tensor_tensor(out=tt0[:, 0:n], in0=sg0[:, 0:n], in1=psum[4][:, 0:n],
                                                            op=ALU.mult)), r=["sg0", "ps4"], w=["tt0"])
                P.op("dve", (lambda e, n=n: e.tensor_tensor(out=tt1[:, 0:n], in0=sg1[:, 0:n], in1=psum[5][:, 0:n],
                                                            op=ALU.mult)), r=["sg1", "ps5"], w=["tt1"])
                P.op("dve", (lambda e, c=c, c0=c0, n=n: e.tensor_tensor(out=mergedT[:, c, c0:c0 + n], in0=tt0[:, 0:n],
                                                                       in1=tt1[:, 0:n], op=ALU.add)),
                     r=["tt0", "tt1"], w=["mergedT"])
        for t in range(5):
            hs = t % 2
            for half in range(2):
                bank = 2 + half
                for c in range(8):
                    P.op("pe", (lambda e, c=c, t=t, half=half, bank=bank: e.matmul(
                        out=psum[bank][:, 0:512], lhsT=mergedT[:, c, t * 128:(t + 1) * 128],
                        rhs=Wo[:, c, half * 512:(half + 1) * 512], start=(c == 0), stop=(c == 7))),
                        r=["mergedT", "Wo"], w=[f"ps{bank}"])
                P.op("dve", (lambda e, t=t, half=half, bank=bank, hs=hs: e.tensor_tensor(
                    out=hb[hs][:, half * 512:(half + 1) * 512], in0=xr[t][:, half * 512:(half + 1) * 512],
                    in1=psum[bank][:, 0:512], op=ALU.add)), r=[f"xr{t}", f"ps{bank}"], w=[f"hb{hs}"])
            P.dma("pool", h_dram[r, t * 128:(t + 1) * 128, :], hb[hs], r=[f"hb{hs}"], w=["h_dram"], slot=f"hst{hs}")

    barrier()
    A.release()
    A.mark()
    Wup = A.alloc([128, 8, 2 * DFF], BF16)
    Wdn = A.alloc([128, NF, 1024], BF16)
    convw = A.alloc([128, 44, 3], F32)
    convb = A.alloc([128, 44], F32)
    g2 = A.alloc([128, 2 * D], F32)
    ht = [None] + [A.alloc([128, D], F32) for _ in range(4)]
    hnT = A.alloc([128, 8, RT], BF16)
    NFH = NF // 2
    actT = A.alloc([128, NFH, 512], BF16)
    ntmp4 = [A.alloc([128, 4], F32) for _ in range(3)]
    upsb = [A.alloc([128, 516], F32) for _ in range(2)]
    cc2 = A.alloc([128, D], F32)
    cc = [cc2[:, 0:512], cc2[:, 512:1024]]
    ht[0] = cc2
    silu_o = A.alloc([128, 512], F32)
    junk4 = silu_o.bitcast(BF16)
    xnh4 = [A.alloc([128, D], BF16) for _ in range(3)]
    h0 = A.off
    wflat = arena_t
    SW = Arena(arena_t, CAP)
    wst4 = [hnT.rearrange("p a b -> p (a b)")[:, 0:4096].bitcast(F32),
           actT.rearrange("p a b -> p (a b)")[:, 0:4096].bitcast(F32)]

    P.dma("sp", convw.rearrange("p a b -> p (a b)"), convw_d, w=["convw", "cchain"], slot="c")
    P.dma("sp", convb, convb_d, w=["convb", "cchain"], slot="c")
    P.dma("sp", g2, g_d[:, D:3 * D], w=["g2", "cchain"], slot="c")
    load_weight(w_up_d, Wup.rearrange("p a b -> p (a b)"), 8 * 2 * DFF, 128, wst4, "W3b")
    load_weight(w_dn_d, Wdn.rearrange("p a b -> p (a b)"), NF * 1024, 128, wst4, "W3b")
    barrier()

    for r in range(NR):
        def p4_A1(t, r=r):
            hk = [f"ht{t}"] if t > 0 else ["cc0", "cc1"]
            s = t % 3
            P.dma("sp", ht[t], h_dram[r, t * 128:(t + 1) * 128, :], r=["h_dram"], w=hk, slot=f"ht{t}")
            P.op("act", (lambda e, t=t, s=s: e.activation(out=junk4, in_=ht[t], func=AF.Square,
                                                         accum_out=ntmp4[s][:, 0:1])),
                 r=hk, w=["silu_o", f"nt{s}0"])
            P.op("dve", (lambda e, s=s: e.tensor_scalar(out=ntmp4[s][:, 1:2], in0=ntmp4[s][:, 0:1], scalar1=1.0 / D,
                                                        scalar2=EPS, op0=ALU.mult, op1=ALU.add)),
                 r=[f"nt{s}0"], w=[f"nt{s}1"])
            P.op("pool", (lambda e, s=s: e.tensor_tensor(out=ntmp4[s][:, 2:3], in0=ntmp4[s][:, 1:2], in1=cneg[:, 0:1],
                                                         op=ALU.pow)), r=[f"nt{s}1", "cneg"], w=[f"nt{s}2"])
            P.op("dve", (lambda e, t=t, s=s: e.scalar_tensor_tensor(
                out=xnh4[s], in0=ht[t], scalar=ntmp4[s][:, 2:3], in1=g2[:, 0:D], op0=ALU.mult, op1=ALU.mult)),
                r=hk + [f"nt{s}2", "g2"], w=[f"xnh4{s}"])

        def p4_A2(t):
            s = t % 3
            transpose_to(xnh4[s], f"xnh4{s}", 8, t % 2, hnT[:, :, t * 128:(t + 1) * 128], "hnT")

        p4_A1(0)
        p4_A1(1)
        p4_A1(2)
        for t in range(5):
            p4_A2(t)
            if t + 3 < 5:
                p4_A1(t + 3)
        for fh in range(2):
            for fi in range(NFH):
                f = fh * NFH + fi
                for ab in range(2):
                    col = ab * DFF + f * 128
                    ch = ab * NF + f
                    bank = 2 + ab
                    hbank = 4 + ab
                    for kc in range(8):
                        P.op("pe", (lambda e, kc=kc, col=col, bank=bank: e.matmul(
                            out=psum[bank][:, 0:512], lhsT=Wup[:, kc, col:col + 128], rhs=hnT[:, kc, 128:640],
                            start=(kc == 0), stop=(kc == 7))), r=["W3b", "hnT"], w=[f"ps{bank}"])
                    for kc in range(8):
                        P.op("pe", (lambda e, kc=kc, col=col, hbank=hbank: e.matmul(
                            out=psum[hbank][:, 0:2], lhsT=Wup[:, kc, col:col + 128], rhs=hnT[:, kc, 126:128],
                            start=(kc == 0), stop=(kc == 7))), r=["W3b", "hnT"], w=[f"ps{hbank}"])
                    ub = upsb[ab]
                    cb = cc[ab]
                    P.op("act", (lambda e, ub=ub, bank=bank: e.activation(out=ub[:, 2:514], in_=psum[bank][:, 0:512],
                                                                          func=AF.Copy)),
                         r=[f"ps{bank}"], w=[f"upsb{ab}"])
                    P.op("act", (lambda e, ub=ub, hbank=hbank, r=r: e.activation(
                        out=ub[:, 0:2], in_=psum[hbank][:, 0:2], func=AF.Identity, scale=hsc[:, r:r + 1])),
                        r=[f"ps{hbank}", "hsc"], w=[f"upsb{ab}"])
                    P.op("act", (lambda e, cb=cb, bank=bank, ch=ch: e.activation(
                        out=cb, in_=psum[bank][:, 0:512], func=AF.Identity, scale=convw[:, ch, 2:3],
                        bias=convb[:, ch:ch + 1])), r=[f"ps{bank}", "convw", "convb"], w=[f"cc{ab}"])
                    P.op("dve", (lambda e, cb=cb, ub=ub, ch=ch: e.scalar_tensor_tensor(
                        out=cb, in0=ub[:, 1:513], scalar=convw[:, ch, 1:2], in1=cb, op0=ALU.mult, op1=ALU.add)),
                        r=[f"upsb{ab}", f"cc{ab}", "convw"], w=[f"cc{ab}"])
                    P.op("dve", (lambda e, cb=cb, ub=ub, ch=ch: e.scalar_tensor_tensor(
                        out=cb, in0=ub[:, 0:512], scalar=convw[:, ch, 0:1], in1=cb, op0=ALU.mult, op1=ALU.add)),
                        r=[f"upsb{ab}", f"cc{ab}", "convw"], w=[f"cc{ab}"])
                P.op("act", lambda e: e.activation(out=silu_o, in_=cc[0], func=AF.Silu), r=["cc0"], w=["silu_o"])
                P.op("dve", (lambda e, fi=fi: e.tensor_tensor(out=actT[:, fi, :], in0=silu_o, in1=cc[1], op=ALU.mult)),
                     r=["silu_o", "cc1"], w=["actT"])
            for t in range(4):
                for half in range(2):
                    bank = 6 + half
                    for fi in range(NFH):
                        f = fh * NFH + fi
                        P.op("pe", (lambda e, f=f, fi=fi, t=t, half=half, bank=bank: e.matmul(
                            out=psum[bank][:, 0:512], lhsT=actT[:, fi, t * 128:(t + 1) * 128],
                            rhs=Wdn[:, f, half * 512:(half + 1) * 512], start=(fi == 0), stop=(fi == NFH - 1))),
                            r=["actT", "W3b"], w=[f"ps{bank}"])
                    P.op("dve", (lambda e, t=t, half=half, bank=bank: e.tensor_tensor(
                        out=ht[t + 1][:, half * 512:(half + 1) * 512], in0=ht[t + 1][:, half * 512:(half + 1) * 512],
                        in1=psum[bank][:, 0:512], op=ALU.add)), r=[f"ht{t + 1}", f"ps{bank}"], w=[f"ht{t + 1}"])
        for t in range(4):
            s = t % 2
            P.op("act", (lambda e, t=t, s=s: e.activation(out=junk4, in_=ht[t + 1], func=AF.Square,
                                                         accum_out=ntmp4[s][:, 0:1])),
                 r=[f"ht{t + 1}"], w=["silu_o", f"nt{s}0"])
            P.op("dve", (lambda e, s=s: e.tensor_scalar(out=ntmp4[s][:, 1:2], in0=ntmp4[s][:, 0:1], scalar1=1.0 / D,
                                                        scalar2=EPS, op0=ALU.mult, op1=ALU.add)),
                 r=[f"nt{s}0"], w=[f"nt{s}1"])
            P.op("pool", (lambda e, s=s: e.tensor_tensor(out=ntmp4[s][:, 2:3], in0=ntmp4[s][:, 1:2], in1=cneg[:, 0:1],
                                                         op=ALU.pow)), r=[f"nt{s}1", "cneg"], w=[f"nt{s}2"])
            P.op("dve", (lambda e, t=t, s=s: e.scalar_tensor_tensor(
                out=ht[t + 1], in0=ht[t + 1], scalar=ntmp4[s][:, 2:3], in1=g2[:, D:2 * D], op0=ALU.mult,
                op1=ALU.mult)), r=[f"ht{t + 1}", f"nt{s}2", "g2"], w=[f"ht{t + 1}"])
            P.dma("pool", out_d[r, t * 128:(t + 1) * 128, :], ht[t + 1], r=[f"ht{t + 1}"], w=["out"],
                  slot=f"ost{t % 2}")
    P.op("pool", None, r=["out"])
    P.emit(st)
    st.close()
    return nc


def _chunks(j):
    return [j, 7 - j, 8 + j, 15 - j]


def _rope_tables(pos):
    half = 8
    inv = 1.0 / (500000.0 ** (np.arange(half, dtype=np.float32) * 2.0 / 16.0))
    ang = pos.astype(np.float32)[:, None] * inv[None, :].astype(np.float32)
    return np.concatenate([np.cos(ang), np.sin(ang)], axis=1).astype(np.float32)


def _pool_mats(first):
    Am = np.zeros((4, 128, 128), np.float64)
    Bm = np.zeros((4, 128, 128), np.float64)
    for g, w in enumerate((2, 4, 8, 16)):
        for t in range(128):
            cnt = min(t + 1, w) if first else w
            for tp in range(t - w + 1, t + 1):
                if tp >= 0:
                    Am[g, tp, t] += 1.0 / cnt
                elif not first:
                    Bm[g, tp + 128, t] += 1.0 / cnt
            Am[g, t, t] -= 1.0
    return Am, Bm


def _pk(a, kc):
    n = a.shape[1]
    return np.ascontiguousarray(a.reshape(kc, 128, n).transpose(1, 0, 2).reshape(128, kc * n))


_NC_CACHE = {}


def kernel(x, norm_mix_g, w_in, pool_w, pool_scale, w_pool_proj, w_attn_proj, w_out,
           norm_ffn_g, w_up, conv_w, conv_b, w_down, norm_final_g):
    f = lambda a: np.asarray(a, dtype=np.float32)
    x = f(x)
    w_in = f(w_in)[0]
    cuts = np.cumsum([512, 512, 512, 512, 512, 64, 8, 2048])
    Wu_, Wq_, Wk_, Wv_, Wiq_, Wik_, Wiw_, Wg_ = np.split(w_in, cuts[:-1], axis=1)
    shared = {}
    shared["w_kvi"] = _pk(np.concatenate([Wk_, Wv_, Wik_, Wik_], axis=1), 8)
    shared["w_qi"] = _pk(np.concatenate([Wq_, Wiq_, Wiw_], axis=1), 8)
    shared["w_u"] = _pk(Wu_, 8)
    shared["w_g"] = _pk(Wg_, 8)
    shared["pool_w"] = np.ascontiguousarray(f(pool_w)[0].transpose(1, 0, 2).reshape(128, 512))
    shared["pool_scale"] = np.ascontiguousarray(f(pool_scale)[0].reshape(4, 128).T)
    shared["w_pp"] = _pk(f(w_pool_proj)[0], 4)
    shared["w_ap"] = np.ascontiguousarray(f(w_attn_proj)[0].reshape(8, 64, 1024).transpose(1, 0, 2).reshape(64, 8192))
    shared["w_out"] = _pk(f(w_out)[0], 8)
    shared["w_up"] = _pk(f(w_up)[0], 8)
    shared["w_down"] = _pk(f(w_down)[0], NF)
    shared["conv_w"] = np.ascontiguousarray(f(conv_w)[0].reshape(3, 44, 128).transpose(2, 1, 0).reshape(128, 132))
    shared["conv_b"] = np.ascontiguousarray(f(conv_b)[0].reshape(44, 128).T)
    gains = np.concatenate([f(norm_mix_g)[0], f(norm_ffn_g)[0], f(norm_final_g)])
    shared["gains"] = np.ascontiguousarray(np.broadcast_to(gains[None, :], (128, 3 * D)))
    shared["iota"] = np.ascontiguousarray(np.broadcast_to(np.arange(512, dtype=np.float32)[None, :], (128, 512)))
    shared["pow2"] = np.ascontiguousarray(np.broadcast_to(
        (0.5 ** np.arange(1, NBIS + 1)).astype(np.float32)[None, :], (128, NBIS)))
    shared["ident"] = np.eye(128, dtype=np.float32)
    shared["cs_ctx"] = _rope_tables(np.arange(S))
    A_std, B_std = _pool_mats(False)
    A_fst, B_fst = _pool_mats(True)
    shared["pm_std"] = np.ascontiguousarray(
        np.concatenate([A_std, B_std], 0).transpose(1, 0, 2).reshape(128, 8 * 128)).astype(np.float32)

    def hi_lo(a):
        hi = a.astype(np.float32).astype(ml_dtypes.bfloat16).astype(np.float64)
        return hi, a - hi

    in_maps = []
    for c in range(8):
        b, j = c // 4, c % 4
        m = dict(shared)
        m["xc"] = np.ascontiguousarray(x[b])
        xo = np.zeros((NR, RT, D), np.float32)
        qp = np.zeros((NR, RT), np.float32)
        hs = np.zeros((NR,), np.float32)
        pmf = np.zeros((NR, 12, 128, 128), np.float64)
        for r, i in enumerate(_chunks(j)):
            t0 = 512 * i
            xo[r, 128:] = x[b, t0:t0 + 512]
            qp[r, 128:] = np.arange(t0, t0 + 512)
            if i > 0:
                xo[r, :128] = x[b, t0 - 128:t0]
                qp[r, :128] = np.arange(t0 - 128, t0)
                hs[r] = 1.0
                pmf[r, 0:4], pmf[r, 4:8], pmf[r, 8:12] = A_std, 0.0, B_std
            else:
                qp[r, :128] = np.arange(0, 128)
                hi, lo = hi_lo(A_fst)
                pmf[r, 0:4], pmf[r, 4:8], pmf[r, 8:12] = hi, lo, 0.0
        m["xo"] = xo
        m["qpos"] = np.ascontiguousarray(qp.reshape(NR * 5, 128).T)
        m["hscale"] = np.ascontiguousarray(np.broadcast_to(hs[None, :], (128, NR)))
        m["cs_own"] = _rope_tables(qp.reshape(-1)).reshape(NR, RT, 16)
        m["pm_first"] = np.ascontiguousarray(
            pmf.reshape(NR * 12, 128, 128).transpose(1, 0, 2).reshape(128, NR * 12 * 128)).astype(np.float32)
        in_maps.append(m)

    if "nc" not in _NC_CACHE:
        _NC_CACHE["nc"] = build_program()
    nc = _NC_CACHE["nc"]
    res = run_bass_kernel_spmd(nc, in_maps, core_ids=list(range(8)))
    _NC_CACHE['res'] = res if DEBUG else None
    out = np.zeros((2, S, D), np.float32)
    for c in range(8):
        b, j = c // 4, c % 4
        o = res.results[c]["out"]
        for r, i in enumerate(_chunks(j)):
            out[b, 512 * i:512 * i + 512] = o[r]
    return out
```
